# Optimizing a Trainium2 kernel written in Bass

```python
import math
import jax, jax.numpy as jnp
from jax import lax
import numpy as np

D_MODEL = 1024
BATCH = 2
SEQ = 8192
DEPTH = 2

CHUNK = 64
Q_BLOCK = 128
N_MIXERS = 2
MEM_LEN = 256
MEM_WIDTH = D_MODEL // 4
MEM_HEADS = 4
MEM_HEAD_DIM = MEM_WIDTH // MEM_HEADS
TOKEN_WIDTH = D_MODEL - MEM_WIDTH
MIX_WIDTH = TOKEN_WIDTH + MEM_WIDTH
DIFF_HEAD_DIM = 64
DIFF_HEADS = TOKEN_WIDTH // (2 * DIFF_HEAD_DIM)
DIFF_PROJ = 3 * TOKEN_WIDTH + MEM_WIDTH
GLA_HEADS = 4
GLA_V_DIM = TOKEN_WIDTH // GLA_HEADS
GLA_K_DIM = GLA_V_DIM // 2
GLA_GATE_RANK = 16
GLA_GATE_TAU = 16.0
GLA_PROJ = 2 * GLA_HEADS * GLA_K_DIM + 2 * TOKEN_WIDTH + GLA_GATE_RANK + MEM_WIDTH
REL_BUCKETS = 32
REL_MAX_DIST = 128
D_FF = ((8 * D_MODEL // 3 + 127) // 128) * 128
CONV_WIDTH = 3
EPS = 1e-6
N_DIFF_LAYERS = (DEPTH + N_MIXERS - 1) // N_MIXERS
N_GLA_LAYERS = DEPTH // N_MIXERS

kernel_name = 'hybrid_diffattn_gla_memxattn_convffn'


def rms_norm(x, g):
    xf = x.astype(jnp.float32)
    y = xf * lax.rsqrt(jnp.mean(xf * xf, axis=-1, keepdims=True) + EPS)
    return (y * g.astype(jnp.float32)).astype(x.dtype)


def t5_bucket(rel):
    half = REL_BUCKETS // 2
    max_exact = half // 2
    ret = jnp.where(rel > 0, half, 0)
    n = jnp.abs(rel)
    nf = jnp.maximum(n, 1).astype(jnp.float32)
    large = max_exact + (jnp.log(nf / max_exact) / math.log(REL_MAX_DIST / max_exact)
                         * (half - max_exact)).astype(jnp.int32)
    large = jnp.minimum(large, half - 1)
    return ret + jnp.where(n < max_exact, n, large)


def diff_attention(q, k, v, qk_norm_g, lam_vecs, out_norm_g, rel_bias, lam_init):
    B, S = q.shape[0], q.shape[1]
    H, d = DIFF_HEADS, DIFF_HEAD_DIM
    nb = S // Q_BLOCK
    q = rms_norm(q, qk_norm_g[0]) * (d ** -0.5)
    k = rms_norm(k, qk_norm_g[1])
    q = q.transpose(0, 2, 3, 1, 4)
    k = k.transpose(0, 2, 3, 1, 4)
    v = v.transpose(0, 2, 1, 3)
    lv = lam_vecs.astype(jnp.float32)
    lam = jnp.exp(jnp.sum(lv[0] * lv[1])) - jnp.exp(jnp.sum(lv[2] * lv[3])) + lam_init
    table = rel_bias.astype(jnp.float32)
    key_pos = jnp.arange(S)
    key_chunk = key_pos // CHUNK
    q_blocks = jnp.moveaxis(q.reshape(B, H, 2, nb, Q_BLOCK, d), 3, 0)

    def attend_block(args):
        qb, b_idx = args
        q_pos = b_idx * Q_BLOCK + jnp.arange(Q_BLOCK)
        logits = jnp.einsum('bhmqd,bhmkd->bhmqk', qb, k).astype(jnp.float32)
        bias = table[t5_bucket(key_pos[None, :] - q_pos[:, None])]
        logits = logits + jnp.transpose(bias, (2, 0, 1))[None, :, None]
        visible = key_chunk[None, :] <= (q_pos // CHUNK)[:, None]
        logits = jnp.where(visible, logits, -jnp.inf)
        p = jax.nn.softmax(logits, axis=-1)
        w = p[:, :, 0] - lam * p[:, :, 1]
        return jnp.einsum('bhqk,bhkv->bhqv', w.astype(v.dtype), v)

    o = lax.map(attend_block, (q_blocks, jnp.arange(nb)))
    o = o.transpose(1, 0, 3, 2, 4).reshape(B, S, H, 2 * d)
    o = rms_norm(o, out_norm_g) * (1.0 - lam_init)
    return o.reshape(B, S, H * 2 * d)


def gla_attention(q, k, v, r, gate_low, gate_w, gate_b, out_norm_g):
    B, S = q.shape[0], q.shape[1]
    H, dk, dv = GLA_HEADS, GLA_K_DIM, GLA_V_DIM
    nc = S // CHUNK
    f32 = jnp.float32
    log_a = jax.nn.log_sigmoid((gate_low @ gate_w + gate_b).astype(f32)) / GLA_GATE_TAU

    def to_chunks(t, dim):
        return t.astype(f32).reshape(B, nc, CHUNK, H, dim).transpose(0, 3, 1, 2, 4)

    qc = to_chunks(q, dk) * (dk ** -0.5)
    kc = to_chunks(k, dk)
    vc = to_chunks(v, dv)
    g = jnp.cumsum(to_chunks(log_a, dk), axis=3)
    g_end = g[:, :, :, -1:, :]
    eg, ieg = jnp.exp(g), jnp.exp(-g)
    a_past = jnp.einsum('bhnqd,bhnkd->bhnqk', qc * eg, kc * ieg)
    a_future = jnp.einsum('bhnqd,bhnkd->bhnqk', qc * ieg, kc * eg)
    t_idx = jnp.arange(CHUNK)
    scores = jnp.where(t_idx[None, :] <= t_idx[:, None], a_past, a_future)
    intra = jnp.einsum('bhnqk,bhnkv->bhnqv', scores, vc)
    kv = jnp.einsum('bhnkd,bhnkv->bhndv', kc * jnp.exp(g_end - g), vc)
    decay = jnp.exp(g_end[:, :, :, 0, :])

    def step(state, inp):
        kv_n, dec_n = inp
        return dec_n[..., None] * state + kv_n, state

    _, s_prev = lax.scan(step, jnp.zeros((B, H, dk, dv), f32),
                         (jnp.moveaxis(kv, 2, 0), jnp.moveaxis(decay, 2, 0)))
    inter = jnp.einsum('bhnqd,nbhdv->bhnqv', qc * eg, s_prev)
    o = (intra + inter).transpose(0, 2, 3, 1, 4).reshape(B, S, H, dv)
    o = rms_norm(o, out_norm_g).reshape(B, S, H * dv)
    return (o * jax.nn.silu(r.astype(f32))).astype(r.dtype)


def memory_cross_attention(q, mem, mem_norm_g, w_kv, qk_norm_g):
    B, S = q.shape[0], q.shape[1]
    M = mem.shape[1]
    q = q.reshape(B, S, MEM_HEADS, MEM_HEAD_DIM)
    kv = rms_norm(mem, mem_norm_g) @ w_kv
    k, v = jnp.split(kv, 2, axis=-1)
    k = k.reshape(B, M, MEM_HEADS, MEM_HEAD_DIM)
    v = v.reshape(B, M, MEM_HEADS, MEM_HEAD_DIM)
    q = rms_norm(q, qk_norm_g[0]) * (MEM_HEAD_DIM ** -0.5)
    k = rms_norm(k, qk_norm_g[1])
    logits = jnp.einsum('bqhd,bkhd->bhqk', q, k).astype(jnp.float32)
    p = jax.nn.softmax(logits, axis=-1)
    o = jnp.einsum('bhqk,bkhd->bqhd', p.astype(v.dtype), v)
    return o.reshape(B, S, MEM_WIDTH)


def conv_ffn(h, w_up, conv_w, conv_b, w_down):
    u = h @ w_up
    c = lax.conv_general_dilated(u, conv_w[:, None, :].astype(u.dtype), window_strides=(1,),
                                 padding=[(CONV_WIDTH - 1, 0)],
                                 dimension_numbers=('NWC', 'WIO', 'NWC'),
                                 feature_group_count=u.shape[-1]) + conv_b
    a, g = jnp.split(c, 2, axis=-1)
    return (jax.nn.silu(g) * a) @ w_down


def setup_inputs(seed: int = 0) -> dict:
    key = jax.random.key(seed)
    ks = jax.random.split(key, 21)
    f32 = jnp.float32

    def nrm(k, shape, scale):
        return jax.random.normal(k, shape, f32) * scale

    def gain(k, shape):
        return 1.0 + 0.05 * jax.random.normal(k, shape, f32)

    return {
        'x': nrm(ks[0], (BATCH, SEQ, D_MODEL), 1.0),
        'mem': nrm(ks[1], (BATCH, MEM_LEN, D_MODEL), 1.0),
        'rel_bias': nrm(ks[2], (REL_BUCKETS, DIFF_HEADS), 0.5),
        'attn_norm': gain(ks[3], (DEPTH, D_MODEL)),
        'ffn_norm': gain(ks[4], (DEPTH, D_MODEL)),
        'mem_norm': gain(ks[5], (DEPTH, D_MODEL)),
        'w_in_diff': nrm(ks[6], (N_DIFF_LAYERS, D_MODEL, DIFF_PROJ), D_MODEL ** -0.5),
        'diff_qk_norm': gain(ks[7], (N_DIFF_LAYERS, 2, DIFF_HEAD_DIM)),
        'diff_lambda': nrm(ks[8], (N_DIFF_LAYERS, 4, DIFF_HEAD_DIM), 0.1),
        'diff_out_norm': gain(ks[9], (N_DIFF_LAYERS, 2 * DIFF_HEAD_DIM)),
        'w_in_gla': nrm(ks[10], (N_GLA_LAYERS, D_MODEL, GLA_PROJ), D_MODEL ** -0.5),
        'gla_gate_w': nrm(ks[11], (N_GLA_LAYERS, GLA_GATE_RANK, GLA_HEADS * GLA_K_DIM), GLA_GATE_RANK ** -0.5),
        'gla_gate_b': nrm(ks[12], (N_GLA_LAYERS, GLA_HEADS * GLA_K_DIM), 0.1),
        'gla_out_norm': gain(ks[13], (N_GLA_LAYERS, GLA_V_DIM)),
        'w_mem_kv': nrm(ks[14], (DEPTH, D_MODEL, 2 * MEM_WIDTH), D_MODEL ** -0.5),
        'mem_qk_norm': gain(ks[15], (DEPTH, 2, MEM_HEAD_DIM)),
        'w_out': nrm(ks[16], (DEPTH, MIX_WIDTH, D_MODEL), MIX_WIDTH ** -0.5),
        'w_up': nrm(ks[17], (DEPTH, D_MODEL, 2 * D_FF), D_MODEL ** -0.5),
        'conv_w': nrm(ks[18], (DEPTH, CONV_WIDTH, 2 * D_FF), CONV_WIDTH ** -0.5),
        'conv_b': nrm(ks[19], (DEPTH, 2 * D_FF), 0.02),
        'w_down': nrm(ks[20], (DEPTH, D_FF, D_MODEL), D_FF ** -0.5),
    }


def reference(x, mem, rel_bias, attn_norm, ffn_norm, mem_norm, w_in_diff, diff_qk_norm,
              diff_lambda, diff_out_norm, w_in_gla, gla_gate_w, gla_gate_b, gla_out_norm,
              w_mem_kv, mem_qk_norm, w_out, w_up, conv_w, conv_b, w_down):
    B, S = x.shape[0], x.shape[1]
    tw = TOKEN_WIDTH
    for i in range(DEPTH):
        h = rms_norm(x, attn_norm[i])
        j = i // N_MIXERS
        if i % N_MIXERS == 0:
            proj = h @ w_in_diff[j]
            q = proj[..., :tw].reshape(B, S, DIFF_HEADS, 2, DIFF_HEAD_DIM)
            k = proj[..., tw:2 * tw].reshape(B, S, DIFF_HEADS, 2, DIFF_HEAD_DIM)
            v = proj[..., 2 * tw:3 * tw].reshape(B, S, DIFF_HEADS, 2 * DIFF_HEAD_DIM)
            mem_q = proj[..., 3 * tw:]
            lam_init = 0.8 - 0.6 * math.exp(-0.3 * i)
            mix = diff_attention(q, k, v, diff_qk_norm[j], diff_lambda[j], diff_out_norm[j],
                                 rel_bias, lam_init)
        else:
            proj = h @ w_in_gla[j]
            kw = GLA_HEADS * GLA_K_DIM
            q = proj[..., :kw]
            k = proj[..., kw:2 * kw]
            v = proj[..., 2 * kw:2 * kw + tw]
            r = proj[..., 2 * kw + tw:2 * kw + 2 * tw]
            gate_low = proj[..., 2 * kw + 2 * tw:2 * kw + 2 * tw + GLA_GATE_RANK]
            mem_q = proj[..., 2 * kw + 2 * tw + GLA_GATE_RANK:]
            mix = gla_attention(q, k, v, r, gate_low, gla_gate_w[j], gla_gate_b[j], gla_out_norm[j])
        cross = memory_cross_attention(mem_q, mem, mem_norm[i], w_mem_kv[i], mem_qk_norm[i])
        x = x + jnp.concatenate([mix.astype(x.dtype), cross.astype(x.dtype)], axis=-1) @ w_out[i]
        h = rms_norm(x, ffn_norm[i])
        x = x + conv_ffn(h, w_up[i], conv_w[i], conv_b[i], w_down[i])
    return x
```

```python
import contextlib
import math
import numpy as np
import ml_dtypes
import concourse.bass as bass
import concourse.mybir as mybir
from concourse.bass_utils import run_bass_kernel_spmd

F32 = mybir.dt.float32
BF16 = mybir.dt.bfloat16
AF = mybir.ActivationFunctionType
ALU = mybir.AluOpType
AX = mybir.AxisListType

NDS = 16


class Ev:
    __slots__ = ("key", "val")

    def __init__(self, key, val):
        self.key = key
        self.val = val


class Tk:
    __slots__ = ("w", "r", "name")

    def __init__(self, name=""):
        self.w = None
        self.r = {}
        self.name = name


class Buf:
    def __init__(self, t, n=1, name=""):
        self.t = t
        self.tk = Tk(name)
        self.tks = [Tk(f"{name}{i}") for i in range(n)] if n > 1 else [self.tk]

    def __getitem__(self, k):
        return self.t[k]


class View:
    def __init__(self, buf, c0, c1):
        self.buf, self.c0, self.c1, self.tk = buf, c0, c1, buf.tk

    def __getitem__(self, k):
        if not isinstance(k, tuple):
            return self.buf[:, self.c0:self.c1]
        p, c = k
        a = self.c0 + (c.start or 0)
        b = self.c0 + (c.stop if c.stop is not None else self.c1 - self.c0)
        return self.buf[p, a:b]


class Prog:
    def __init__(self, nc):
        self.nc = nc
        self.eng = {"pe": nc.tensor, "act": nc.scalar, "dve": nc.vector, "pool": nc.gpsimd, "sp": nc.sync}
        self.sems = {}
        self.ecnt = {}
        for e in self.eng:
            self.sems["e_" + e] = nc.alloc_semaphore("sem_" + e)
            self.ecnt[e] = 0
        self.dval = {}
        self.dnext = {}
        for q in ("sp", "pool", "act"):
            for i in range(NDS):
                self.sems[f"d_{q}_{i}"] = nc.alloc_semaphore(f"dsem_{q}_{i}")
                self.dval[f"d_{q}_{i}"] = 0
            self.dnext[q] = 0
        self.sems["cc"] = nc.alloc_semaphore("sem_cc")
        self.ccval = 0
        self.waited = {e: {} for e in self.eng}
        self.stack = contextlib.ExitStack()
        self.n_ins = 0

    def sb(self, stack, name, shape, dtype, n=1):
        self.n_alloc = getattr(self, "n_alloc", 0) + 1
        t = stack.enter_context(self.nc.sbuf_tensor(f"s{self.n_alloc}_{name}", list(shape), dtype))
        return Buf(t, n, name)

    def ps(self, stack, name, shape, dtype, n=1):
        self.n_alloc = getattr(self, "n_alloc", 0) + 1
        t = stack.enter_context(self.nc.psum_tensor(f"p{self.n_alloc}_{name}", list(shape), dtype))
        return Buf(t, n, name)

    def dram(self, name, shape, dtype, kind=None, n=1):
        if kind is None:
            t = self.nc.dram_tensor(name, list(shape), dtype)
        else:
            t = self.nc.dram_tensor(name, list(shape), dtype, kind=kind)
        return Buf(t, n, name)

    def _wait(self, e, ev):
        if ev is None:
            return
        if self.waited[e].get(ev.key, 0) >= ev.val:
            return
        self.eng[e].wait_ge(self.sems[ev.key], ev.val)
        self.waited[e][ev.key] = ev.val

    def _deps(self, e, reads, writes):
        own = "e_" + e
        for t in reads:
            if t.w is not None and not (e == "pe" and t.w.key == own):
                self._wait(e, t.w)
        for t in writes:
            if t.w is not None and t.w.key != own:
                self._wait(e, t.w)
            for k, v in t.r.items():
                if k != own:
                    self._wait(e, Ev(k, v))

    def _mark(self, ev, reads, writes):
        for t in reads:
            if t.r.get(ev.key, 0) < ev.val:
                t.r[ev.key] = ev.val
        for t in writes:
            t.w = ev
            t.r = {}

    @staticmethod
    def _tks(lst):
        out = []
        for x in lst:
            if isinstance(x, (Buf, View)):
                out.append(x.tk)
            else:
                out.append(x)
        return out

    def op(self, e, fn, reads=(), writes=()):
        reads = self._tks(reads)
        writes = self._tks(writes)
        self._deps(e, reads, writes)
        ins = fn(self.eng[e])
        self.ecnt[e] += 1
        ins.then_inc(self.sems["e_" + e], 1)
        self._mark(Ev("e_" + e, self.ecnt[e]), reads, writes)
        self.n_ins += 1
        return ins

    def dma(self, q, out, in_, reads=(), writes=()):
        reads = self._tks(reads)
        writes = self._tks(writes)
        i = self.dnext[q]
        self.dnext[q] = (i + 1) % NDS
        key = f"d_{q}_{i}"
        if self.dval[key] > 0:
            self._wait(q, Ev(key, self.dval[key]))
        self._deps(q, reads, writes)
        ins = self.eng[q].dma_start(out=out, in_=in_)
        self.dval[key] += 16
        ins.then_inc(self.sems[key], 16)
        self._mark(Ev(key, self.dval[key]), reads, writes)
        self.n_ins += 1
        return ins

    def allgather(self, src, dst, groups, src_tks=None):
        src_tks = [src.tk] if src_tks is None else list(src_tks)
        self._deps("pool", src_tks, [dst.tk])
        ins = self.nc.gpsimd.collective_compute(
            "AllGather", ALU.bypass, replica_groups=groups,
            ins=[src.t.ap().opt()], outs=[dst.t.ap().opt()])
        self.ccval += 1
        ins.then_inc(self.sems["cc"])
        self._mark(Ev("cc", self.ccval), src_tks, [dst.tk])

    def barrier(self, cc=False):
        for e in self.eng:
            for e2 in self.eng:
                if e2 != e and self.ecnt[e2] > 0:
                    self._wait(e, Ev("e_" + e2, self.ecnt[e2]))
            for k, v in self.dval.items():
                if v > 0:
                    self._wait(e, Ev(k, v))
            if cc and self.ccval > 0:
                self._wait(e, Ev("cc", self.ccval))


D = 1024
TOK = 2048
NT = 16
S_FULL = 8192
TW = 768
DFF = 2816
EPS = 1e-6
NEG = -30000.0


def _t5_bucket(rel):
    rel = np.asarray(rel, np.int32)
    half, max_exact = 16, 8
    ret = np.where(rel > 0, half, 0)
    n = np.abs(rel)
    nf = np.maximum(n, 1).astype(np.float32)
    large = max_exact + (np.log(nf / np.float32(max_exact)) / np.float32(math.log(128 / 8))
                         * np.float32(half - max_exact)).astype(np.int32)
    large = np.minimum(large, half - 1)
    return ret + np.where(n < max_exact, n, large)


def _t5_masks():
    kl = np.arange(128)[:, None]
    ql = np.arange(128)[None, :]
    bd = _t5_bucket(kl - ql)
    bc = _t5_bucket(kl - ql - 128)
    slots = []
    masks = []
    for b in sorted(set(bd.ravel().tolist())):
        if b == 15:
            continue
        slots.append(("d", b))
        masks.append(bd == b)
    for b in sorted(set(bc.ravel().tolist())):
        if b == 15:
            continue
        slots.append(("c", b))
        masks.append(bc == b)
    slots.append(("m", -1))
    masks.append((kl >= 64) & (ql < 64))
    m = np.stack(masks, axis=1).astype(np.float32)
    return slots, m.astype(ml_dtypes.bfloat16)


T5_SLOTS, T5_MASKS = _t5_masks()
NSLOT = len(T5_SLOTS)


def _core_consts(c):
    qc = c % 4
    pc = np.zeros((128, 16), np.float32)
    for j in range(4):
        pc[:, j] = 1.0 if j == qc else 0.0
        pc[:, 4 + j] = 1.0 if j == qc - 1 else 0.0
        pc[:, 8 + j] = 1.0 if j < qc else 0.0
        pc[:, 12 + j] = 0.0 if j < qc else 1.0
    mt = np.zeros((128, 4, 64), np.float32)
    for T in range(4):
        for j in range(64):
            if j > 16 * qc + 4 * T + 3:
                mt[:, T, j] = NEG
    return pc, mt.reshape(128, 256)


def _gla_consts():
    s = np.arange(128)[:, None]
    t = np.arange(128)[None, :]
    same = (s // 64) == (t // 64)
    mle = (same & (s <= t)).astype(np.float32)
    mgt = (same & (s > t)).astype(np.float32)
    lo = np.zeros((128, 128), np.float32)
    return mle, mgt


BC_OFF = {}
_o = 0
for _n, _s in (("rel_bias", 192), ("diff_qk", 128), ("diff_lam", 256), ("diff_on", 128),
               ("gla_gb", 384), ("gla_on", 192), ("mem_qk", 256)):
    BC_OFF[_n] = _o
    _o += _s
BC_N = _o
GROUPS = [[0, 1, 2, 3], [4, 5, 6, 7]]


class Builder:
    def __init__(self, stop=None):
        self.stop = stop
        self.nc = bass.Bass("TRN2", target_bir_lowering=False)
        self.P = Prog(self.nc)
        P = self.P
        self.inputs = {}

        self.spec = {
            "x": ([TOK, D], F32, NT), "mem": ([256, D], F32, 1), "bc": ([128, BC_N], F32, 1),
            "colg": ([128, 49], F32, 1), "convw": ([128, 2 * 44 * 3], F32, 1), "convb": ([128, 88], F32, 1),
            "idb": ([128, 128], BF16, 1), "t5m": ([128, NSLOT * 128], BF16, 1), "pc": ([128, 16], F32, 1),
            "mt": ([128, 256], F32, 1), "glam": ([128, 256], F32, 1),
            "w_in_diff": ([D, 2560], F32, 1), "w_in_gla": ([D, 2576], F32, 1), "gla_gate_w": ([16, 384], F32, 1),
        }
        for i in range(2):
            self.spec[f"w_kv{i}"] = ([D, 512], F32, 1)
            self.spec[f"w_out{i}"] = ([D, D], F32, 1)
            self.spec[f"w_up{i}"] = ([D, 2 * DFF], F32, 1)
            self.spec[f"w_down{i}"] = ([DFF, D], F32, 1)
        self.out = P.dram("out", [TOK, D], F32, kind="ExternalOutput", n=NT)
        self.xa = [P.dram(f"xa{i}", [TOK, D], F32, n=NT) for i in range(2)]
        self.xf0 = P.dram("xf0", [TOK, D], F32, n=NT)
        self.ksrc = [P.dram(f"ksrc{g}", [256, TOK], BF16, n=NT) for g in range(3)]
        self.kall = [P.dram(f"kall{g}", [4 * 256, TOK], BF16) for g in range(3)]
        self.vsrc = [P.dram(f"vsrc{g}", [TOK, 256], BF16, n=NT) for g in range(3)]
        self.vall = [P.dram(f"vall{g}", [4 * TOK, 256], BF16) for g in range(3)]
        self.olocd = P.dram("olocd", [TOK, 768], F32, n=NT)
        self.srd = P.dram("srd", [TOK, 768], BF16, n=NT)
        self.gsrc = P.dram("gsrc", [96, 772], F32)
        self.gall = P.dram("gall", [4 * 96, 772], F32)
        self.hsrc = [P.dram(f"hsrc{i}", [128, 16], BF16) for i in range(2)]
        self.hall = [P.dram(f"hall{i}", [4 * 128, 16], BF16) for i in range(2)]
        self.dbg = {}

    def I(self, name):
        if name not in self.inputs:
            shape, dt, n = self.spec[name]
            self.inputs[name] = self.P.dram(name, shape, dt, kind="ExternalInput", n=n)
        return self.inputs[name]

    def dbg_out(self, name, shape, dt=F32):
        b = self.P.dram("dbg_" + name, shape, dt, kind="ExternalOutput")
        self.dbg[name] = b
        return b

    def setup(self, st):
        P = self.P
        self.bc = P.sb(st, "bc", [128, BC_N], F32)
        self.colg = P.sb(st, "colg", [128, 49], F32)
        self.idb = P.sb(st, "idb", [128, 128], BF16)
        self.pc = P.sb(st, "pc", [128, 16], F32)
        self.epsb = P.sb(st, "epsb", [128, 1], F32)
        P.dma("sp", self.bc[:], self.I("bc")[:, :], [self.I("bc")], [self.bc])
        P.dma("sp", self.colg[:], self.I("colg")[:, :], [self.I("colg")], [self.colg])
        P.dma("sp", self.idb[:], self.I("idb")[:, :], [self.I("idb")], [self.idb])
        P.dma("sp", self.pc[:], self.I("pc")[:, :], [self.I("pc")], [self.pc])
        P.op("dve", lambda e: e.memset(self.epsb[:], EPS), [], [self.epsb])

    def gcol(self, kind, layer):
        o = (kind * 2 + layer) * 8
        return self.colg[:, o:o + 8]

    def norm_T(self, xt, gcol, hT_ap, hT_tk, W):
        P = self.P
        junk, ss, rt, xn, pt = W["junk"], W["ss"], W["rt"], W["xn"], W["pt"]
        P.op("act", lambda e: e.activation(out=junk[:], in_=xt[:], func=AF.Square, accum_out=ss[:, 0:1]),
             [xt], [junk, ss])
        P.op("act", lambda e: e.activation(out=rt[:, 0:1], in_=ss[:, 0:1], func=AF.Sqrt, scale=1.0 / D,
                                           bias=self.epsb[:, 0:1]), [ss, self.epsb], [rt])
        P.op("dve", lambda e: e.reciprocal(out=rt[:, 1:2], in_=rt[:, 0:1]), [rt], [rt])
        P.op("dve", lambda e: e.tensor_scalar(out=xn[:], in0=xt[:], scalar1=rt[:, 1:2], scalar2=None,
                                              op0=ALU.mult), [xt, rt], [xn])
        for kc in range(8):
            P.op("pe", lambda e: e.transpose(pt[:, kc * 128:(kc + 1) * 128], xn[:, kc * 128:(kc + 1) * 128],
                                             self.idb[:]), [xn, self.idb], [pt])
        P.op("dve", lambda e: e.tensor_tensor(
            out=hT_ap, in0=pt[:, :].rearrange("p (k t) -> p k t", k=8),
            in1=gcol.unsqueeze(2).to_broadcast([128, 8, 128]), op=ALU.mult), [pt, self.colg], [hT_tk])

    def load_w(self, dst, src, kchunks, cols, c0=0, wid=None, q="pool"):
        P = self.P
        wid = cols if wid is None else wid
        half = max(1, kchunks // 2)
        for k0 in range(0, kchunks, half):
            k1 = min(kchunks, k0 + half)
            P.dma(q, dst[:, k0:k1, 0:wid],
                  src.t.ap()[k0 * 128:k1 * 128, c0:c0 + wid].rearrange("(k p) c -> p k c", p=128),
                  [src], [dst.tks[k] for k in range(k0, k1)])

    def group_norm(self, pr, c0, ng, gain_ap, out_ap, out_tk, W, reads):
        P = self.P
        sq, ssg, t1 = W["sq"], W["ssg"], W["t1"]
        n = ng * 64
        src3 = pr[:, c0:c0 + n].rearrange("p (g d) -> p g d", d=64)
        P.op("pool", lambda e: e.tensor_tensor(out=sq[:, 0:n], in0=pr[:, c0:c0 + n], in1=pr[:, c0:c0 + n],
                                               op=ALU.mult), reads, [sq])
        P.op("dve", lambda e: e.tensor_reduce(out=ssg[:, 0:ng], in_=sq[:, 0:n].rearrange("p (g d) -> p g d", d=64),
                                              axis=AX.X, op=ALU.add), [sq], [ssg])
        P.op("act", lambda e: e.activation(out=ssg[:, 32:32 + ng], in_=ssg[:, 0:ng], func=AF.Sqrt, scale=1.0 / 64,
                                           bias=self.epsb[:, 0:1]), [ssg, self.epsb], [ssg])
        P.op("dve", lambda e: e.reciprocal(out=ssg[:, 64:64 + ng], in_=ssg[:, 32:32 + ng]), [ssg], [ssg])
        P.op("dve", lambda e: e.tensor_tensor(
            out=t1[:, 0:n].rearrange("p (g d) -> p g d", d=64), in0=src3,
            in1=ssg[:, 64:64 + ng].unsqueeze(2).to_broadcast([128, ng, 64]), op=ALU.mult), reads + [ssg], [t1])
        P.op("pool", lambda e: e.tensor_tensor(
            out=out_ap, in0=t1[:, 0:n].rearrange("p (g d) -> p g d", d=64),
            in1=gain_ap.unsqueeze(1).to_broadcast([128, ng, 64]), op=ALU.mult), [t1, self.gains], [out_tk])

    def phase_a0(self, st):
        P = self.P
        with contextlib.ExitStack() as s:
            Win = P.sb(s, "Win", [128, 8, 2560], BF16, n=8)
            self.load_w(Win, self.I("w_in_diff"), 8, 2560)
            xin = [P.sb(s, f"xin{i}", [128, D], F32) for i in range(2)]
            hT = [P.sb(s, f"hT{i}", [128, 8, 128], BF16) for i in range(2)]
            pr = [P.sb(s, f"pr{i}", [128, 2560], F32) for i in range(2)]
            W = dict(junk=P.sb(s, "junk", [128, D], F32), ss=P.sb(s, "ss", [128, 1], F32),
                     rt=P.sb(s, "rt", [128, 2], F32), xn=P.sb(s, "xn", [128, D], BF16),
                     pt=P.ps(s, "pt0", [128, 1024], BF16),
                     sq=P.sb(s, "sq", [128, 768], F32), ssg=P.sb(s, "ssg", [128, 96], F32),
                     t1=P.sb(s, "t1", [128, 768], F32))
            qn = P.sb(s, "qn", [128, 768], BF16)
            kn = P.sb(s, "kn", [128, 768], BF16)
            mn = P.sb(s, "mn", [128, 256], BF16)
            vb = [P.sb(s, f"vb{i}", [128, 768], BF16) for i in range(2)]
            ksb = [P.sb(s, f"ksb{i}", [128, 6, 128], BF16) for i in range(2)]
            pj = [P.ps(s, f"pj{i}", [128, 512], F32) for i in range(2)]
            pt1 = P.ps(s, "pt1", [128, 1024], BF16)
            pt2 = P.ps(s, "pt2", [128, 1024], BF16)
            g = self.gains
            kdst = [self.ksrc[g].t.ap().rearrange("(h p) t -> p h t", p=128) for g in range(3)]
            import os
            LVL = int(os.environ.get('K_A0LVL', '9'))
            for tt in range(int(os.environ.get('K_A0NT', NT))):
                xb, hb, prb = xin[tt % 2], hT[tt % 2], pr[tt % 2]
                P.dma("sp", xb[:], self.I("x")[tt * 128:(tt + 1) * 128, :], [self.I("x").tks[tt]], [xb])
                self.norm_T(xb, self.gcol(0, 0), hb[:, :, :], hb.tk, W)
                if LVL < 2:
                    continue
                for n in range(5):
                    pb = pj[n % 2]
                    for kc in range(8):
                        P.op("pe", lambda e: e.matmul(pb[:, :], hb[:, kc, :], Win[:, kc, n * 512:(n + 1) * 512],
                                                      start=(kc == 0), stop=(kc == 7)),
                             [hb, Win.tks[kc]], [pb])
                    if n % 2 == 0:
                        P.op("act", lambda e: e.activation(out=prb[:, n * 512:(n + 1) * 512], in_=pb[:, :],
                                                           func=AF.Copy), [pb], [prb])
                    else:
                        P.op("dve", lambda e: e.tensor_copy(out=prb[:, n * 512:(n + 1) * 512], in_=pb[:, :]),
                             [pb], [prb])
                if LVL < 3:
                    continue
                self.group_norm(prb, 0, 12, g[:, 0:64], qn[:, :].rearrange("p (g d) -> p g d", d=64), qn.tk, W, [prb])
                self.group_norm(prb, 768, 12, g[:, 64:128], kn[:, :].rearrange("p (g d) -> p g d", d=64), kn.tk, W, [prb])
                self.group_norm(prb, 2304, 4, g[:, 128:192], mn[:, :].rearrange("p (g d) -> p g d", d=64), mn.tk, W, [prb])
                if LVL < 4:
                    continue
                vbb = vb[tt % 2]
                P.op("act", lambda e: e.activation(out=vbb[:], in_=prb[:, 1536:2304], func=AF.Copy), [prb], [vbb])
                for g3 in range(3):
                    P.dma("sp", self.vsrc[g3][tt * 128:(tt + 1) * 128, :], vbb[:, g3 * 256:(g3 + 1) * 256], [vbb],
                          [self.vsrc[g3].tks[tt]])
                if LVL < 5:
                    continue
                for h in range(6):
                    P.op("pe", lambda e: e.transpose(pt1[:, h * 128:(h + 1) * 128], qn[:, h * 128:(h + 1) * 128],
                                                     self.idb[:]), [qn, self.idb], [pt1])
                for h in range(2):
                    P.op("pe", lambda e: e.transpose(pt1[:, (6 + h) * 128:(7 + h) * 128], mn[:, h * 128:(h + 1) * 128],
                                                     self.idb[:]), [mn, self.idb], [pt1])
                for h in range(6):
                    P.op("pe", lambda e: e.transpose(pt2[:, h * 128:(h + 1) * 128], kn[:, h * 128:(h + 1) * 128],
                                                     self.idb[:]), [kn, self.idb], [pt2])
                if LVL == 5 and os.environ.get("K_SUB") == "a":
                    continue
                P.op("act", lambda e: e.activation(
                    out=self.qT[:, :, tt * 128:(tt + 1) * 128],
                    in_=pt1[:, 0:768].rearrange("p (h t) -> p h t", h=6), func=AF.Copy), [pt1], [self.qT])
                if LVL == 5 and os.environ.get("K_SUB") == "b":
                    continue
                if not (LVL == 5 and os.environ.get("K_SUB") == "c"):
                    P.op("act", lambda e: e.activation(
                        out=self.mqT[:, :, tt * 128:(tt + 1) * 128],
                        in_=pt1[:, 768:1024].rearrange("p (h t) -> p h t", h=2), func=AF.Copy), [pt1], [self.mqT])
                if LVL == 5 and os.environ.get("K_SUB") == "d":
                    continue
                kb = ksb[tt % 2]
                P.op("dve", lambda e: e.tensor_copy(out=kb[:, :, :], in_=pt2[:, 0:768].rearrange("p (h t) -> p h t", h=6)),
                     [pt2], [kb])
                if LVL < 6:
                    continue
                for g3 in range(3):
                    P.dma("sp", kdst[g3][:, :, tt * 128:(tt + 1) * 128], kb[:, 2 * g3:2 * g3 + 2, :], [kb],
                          [self.ksrc[g3].tks[tt]])
            import os
            if not os.environ.get("K_NOCC"):
                for g3 in range(3):
                    P.allgather(self.ksrc[g3], self.kall[g3], GROUPS, src_tks=self.ksrc[g3].tks)
                    P.allgather(self.vsrc[g3], self.vall[g3], GROUPS, src_tks=self.vsrc[g3].tks)
            P.barrier()

    def setup_gains(self, st):
        P = self.P
        self.gains = P.sb(st, "gains", [128, 704], F32)
        g, bc = self.gains, self.bc
        oq, om, od, og = BC_OFF["diff_qk"], BC_OFF["mem_qk"], BC_OFF["diff_on"], BC_OFF["gla_on"]
        P.op("dve", lambda e: e.tensor_scalar(out=g[:, 0:64], in0=bc[:, oq:oq + 64], scalar1=0.125, scalar2=None,
                                              op0=ALU.mult), [bc], [g])
        P.op("dve", lambda e: e.tensor_copy(out=g[:, 64:128], in_=bc[:, oq + 64:oq + 128]), [bc], [g])
        for L in range(2):
            P.op("dve", lambda e: e.tensor_scalar(out=g[:, 128 + 128 * L:192 + 128 * L],
                                                  in0=bc[:, om + 128 * L:om + 128 * L + 64], scalar1=0.125,
                                                  scalar2=None, op0=ALU.mult), [bc], [g])
            P.op("dve", lambda e: e.tensor_copy(out=g[:, 192 + 128 * L:256 + 128 * L],
                                                in_=bc[:, om + 128 * L + 64:om + 128 * L + 128]), [bc], [g])
        P.op("dve", lambda e: e.tensor_scalar(out=g[:, 384:512], in0=bc[:, od:od + 128], scalar1=0.8, scalar2=None,
                                              op0=ALU.mult), [bc], [g])
        P.op("dve", lambda e: e.tensor_copy(out=g[:, 512:704], in_=bc[:, og:og + 192]), [bc], [g])

    def setup_attn_consts(self, st):
        P = self.P
        bc = self.bc
        self.lam = P.sb(st, "lam", [128, 8], F32)
        self.Dg = P.sb(st, "Dg", [128, 6, 128], F32)
        self.Cn = P.sb(st, "Cn", [128, 6, 128], F32)
        self.maskw = P.sb(st, "maskw", [128, 384], F32)
        self.btab = P.sb(st, "btab", [128, 6, 256], F32)
        lam = self.lam
        ol = BC_OFF["diff_lam"]
        with contextlib.ExitStack() as s:
            tmp = P.sb(s, "lamtmp", [128, 128], F32)
            P.op("dve", lambda e: e.tensor_tensor(out=tmp[:, 0:64], in0=bc[:, ol:ol + 64], in1=bc[:, ol + 64:ol + 128],
                                                  op=ALU.mult), [bc], [tmp])
            P.op("dve", lambda e: e.tensor_tensor(out=tmp[:, 64:128], in0=bc[:, ol + 128:ol + 192],
                                                  in1=bc[:, ol + 192:ol + 256], op=ALU.mult), [bc, tmp], [tmp])
            P.op("dve", lambda e: e.tensor_reduce(out=lam[:, 0:2], in_=tmp[:, :].rearrange("p (a d) -> p a d", a=2),
                                                  axis=AX.X, op=ALU.add), [tmp], [lam])
            P.op("act", lambda e: e.activation(out=lam[:, 2:4], in_=lam[:, 0:2], func=AF.Exp), [lam], [lam])
            P.op("dve", lambda e: e.tensor_tensor(out=lam[:, 5:6], in0=lam[:, 3:4], in1=lam[:, 2:3], op=ALU.subtract),
                 [lam], [lam])
            P.op("dve", lambda e: e.tensor_scalar(out=lam[:, 4:5], in0=lam[:, 5:6], scalar1=-0.2, scalar2=None,
                                                  op0=ALU.add), [lam], [lam])
            dev = P.sb(s, "dev", [128, 192], F32)
            orb = BC_OFF["rel_bias"]
            P.op("dve", lambda e: e.tensor_tensor(
                out=dev[:, :].rearrange("p (b h) -> p b h", h=6), in0=bc[:, orb:orb + 192].rearrange("p (b h) -> p b h", h=6),
                in1=bc[:, orb + 90:orb + 96].unsqueeze(1).to_broadcast([128, 32, 6]), op=ALU.subtract), [bc], [dev])
            t5m = P.sb(s, "t5m", [128, NSLOT * 128], BF16)
            P.dma("sp", t5m[:], self.I("t5m")[:, :], [self.I("t5m")], [t5m])
            mt = P.sb(s, "mt", [128, 256], F32)
            P.dma("sp", mt[:], self.I("mt")[:, :], [self.I("mt")], [mt])
            P.op("pool", lambda e: e.memset(self.maskw[:], NEG), [], [self.maskw])
            for h in range(6):
                P.op("dve", lambda e: e.tensor_scalar(out=self.btab[:, h, :], in0=mt[:], scalar1=bc[:, orb + 90 + h:orb + 91 + h],
                                                      scalar2=None, op0=ALU.add), [mt, bc], [self.btab])
                msl = NSLOT - 1
                P.op("dve", lambda e: e.tensor_scalar(out=self.Dg[:, h, :], in0=t5m[:, msl * 128:(msl + 1) * 128],
                                                      scalar1=NEG, scalar2=None, op0=ALU.mult), [t5m], [self.Dg])
                P.op("dve", lambda e: e.memset(self.Cn[:, h, :], 0.0), [], [self.Cn])
                for si, (kind, b) in enumerate(T5_SLOTS):
                    if kind == "m":
                        continue
                    dst = self.Dg if kind == "d" else self.Cn
                    P.op("dve", lambda e: e.scalar_tensor_tensor(
                        out=dst[:, h, :], in0=t5m[:, si * 128:(si + 1) * 128], scalar=dev[:, b * 6 + h:b * 6 + h + 1],
                        in1=dst[:, h, :], op0=ALU.mult, op1=ALU.add), [t5m, dev, dst], [dst])
            P.barrier()

    def setup_mem(self, st, L, name):
        P = self.P
        mKT = P.sb(st, f"mKT{name}", [128, 2, 256], BF16)
        mV = P.sb(st, f"mV{name}", [128, 2, 4, 72], BF16)
        with contextlib.ExitStack() as s:
            Wkv = P.sb(s, "Wkv", [128, 8, 512], BF16, n=8)
            self.load_w(Wkv, self.I(f"w_kv{L}"), 8, 512)
            xin = P.sb(s, "mxin", [128, D], F32)
            hT = P.sb(s, "mhT", [128, 8, 128], BF16)
            W = dict(junk=P.sb(s, "mjunk", [128, D], F32), ss=P.sb(s, "mss", [128, 1], F32),
                     rt=P.sb(s, "mrt", [128, 2], F32), xn=P.sb(s, "mxn", [128, D], BF16),
                     pt=P.ps(s, "mpt0", [128, 1024], BF16),
                     sq=P.sb(s, "msq", [128, 768], F32), ssg=P.sb(s, "mssg", [128, 96], F32),
                     t1=P.sb(s, "mt1", [128, 768], F32))
            pr = P.sb(s, "mpr", [128, 512], F32)
            kn = P.sb(s, "mkn", [128, 256], BF16)
            pj = P.ps(s, "mpj", [128, 512], F32)
            ptk = P.ps(s, "mptk", [128, 1024], BF16)
            P.op("pool", lambda e: e.memset(mV[:, :, :, 64:72], 1.0), [], [mV])
            for blk in range(2):
                P.dma("sp", xin[:], self.I("mem")[blk * 128:(blk + 1) * 128, :], [self.I("mem")], [xin])
                self.norm_T(xin, self.gcol(2, L), hT[:, :, :], hT.tk, W)
                for kc in range(8):
                    P.op("pe", lambda e: e.matmul(pj[:, :], hT[:, kc, :], Wkv[:, kc, :], start=(kc == 0), stop=(kc == 7)),
                         [hT, Wkv.tks[kc]], [pj])
                P.op("act", lambda e: e.activation(out=pr[:], in_=pj[:, :], func=AF.Copy), [pj], [pr])
                self.group_norm(pr, 0, 4, self.gains[:, 192 + 128 * L:256 + 128 * L],
                                kn[:, :].rearrange("p (g d) -> p g d", d=64), kn.tk, W, [pr])
                P.op("dve", lambda e: e.tensor_copy(out=mV[:, blk, :, 0:64],
                                                    in_=pr[:, 256:512].rearrange("p (h d) -> p h d", h=4)), [pr], [mV])
                for pp in range(2):
                    P.op("pe", lambda e: e.transpose(ptk[:, pp * 128:(pp + 1) * 128], kn[:, pp * 128:(pp + 1) * 128],
                                                     self.idb[:]), [kn, self.idb], [ptk])
                P.op("dve", lambda e: e.tensor_copy(out=mKT[:, :, blk * 128:(blk + 1) * 128],
                                                    in_=ptk[:, 0:256].rearrange("p (a t) -> p a t", a=2)), [ptk], [mKT])
            P.barrier()
        return mKT, mV

    def attn_finalize_diff(self, Ob, i, h, tt, mix, F, k):
        P = self.P
        a0, a1 = 2 * i, 2 * i + 1
        b0, b1 = Ob[a0 // 3], Ob[a1 // 3]
        c0, c1 = (a0 % 3) * 129, (a1 % 3) * 129
        rr, t0, o, junk = F["rr"][k], F["t0"][k], F["o"][k], F["junk"][k]
        P.op("dve", lambda e: e.reciprocal(out=rr[:, 0:1], in_=b0[:, c0 + 128:c0 + 129]), [b0], [rr])
        P.op("dve", lambda e: e.reciprocal(out=rr[:, 1:2], in_=b1[:, c1 + 128:c1 + 129]), [b1], [rr])
        P.op("dve", lambda e: e.tensor_tensor(out=rr[:, 2:3], in0=rr[:, 1:2], in1=self.lam[:, 4:5], op=ALU.mult),
             [rr, self.lam], [rr])
        P.op("dve", lambda e: e.tensor_scalar(out=t0[:], in0=b0[:, c0:c0 + 128], scalar1=rr[:, 0:1], scalar2=None,
                                              op0=ALU.mult), [b0, rr], [t0])
        P.op("dve", lambda e: e.scalar_tensor_tensor(out=o[:], in0=b1[:, c1:c1 + 128], scalar=rr[:, 2:3], in1=t0[:],
                                                     op0=ALU.mult, op1=ALU.add), [b1, rr, t0], [o])
        P.op("act", lambda e: e.activation(out=junk[:], in_=o[:], func=AF.Square, accum_out=rr[:, 3:4]), [o], [junk, rr])
        P.op("act", lambda e: e.activation(out=rr[:, 4:5], in_=rr[:, 3:4], func=AF.Sqrt, scale=1.0 / 128,
                                           bias=self.epsb[:, 0:1]), [rr, self.epsb], [rr])
        P.op("dve", lambda e: e.reciprocal(out=rr[:, 5:6], in_=rr[:, 4:5]), [rr], [rr])
        P.op("dve", lambda e: e.scalar_tensor_tensor(out=mix[:, tt, h * 128:(h + 1) * 128], in0=o[:], scalar=rr[:, 5:6],
                                                     in1=self.gains[:, 384:512], op0=ALU.mult, op1=ALU.mult),
             [o, rr, self.gains], [mix.tks[tt]])

    def cross_attn(self, mKT, mV, mqT, mix, Sb, Ob, PT, F):
        P = self.P
        cnt = 0
        for hd in range(4):
            pp, hp = hd // 2, hd % 2
            r0, r1 = hp * 64, hp * 64 + 64
            for T in range(4):
                for blk in range(2):
                    sb, pt = Sb[cnt % 4], PT[cnt % 4]
                    cnt += 1
                    P.op("pe", lambda e: e.matmul(sb[:, :], mKT[r0:r1, pp, blk * 128:(blk + 1) * 128],
                                                  mqT[r0:r1, pp, T * 512:(T + 1) * 512], start=True, stop=True),
                         [mKT, mqT], [sb])
                    P.op("act", lambda e: e.activation(out=pt[:], in_=sb[:, :], func=AF.Exp), [sb], [pt])
                    for i in range(4):
                        ob = Ob[i // 3]
                        c = (i % 3) * 129
                        P.op("pe", lambda e: e.matmul(ob[:, c:c + 65], pt[:, i * 128:(i + 1) * 128], mV[:, blk, hd, 0:65],
                                                      start=(blk == 0 and i % 3 == 0), stop=(blk == 1),
                                                      skip_group_check=True), [pt, mV], [ob])
                for i in range(4):
                    ob = Ob[i // 3]
                    c = (i % 3) * 129
                    tt = T * 4 + i
                    rr = F["rr"][i % 2]
                    P.op("dve", lambda e: e.reciprocal(out=rr[:, 0:1], in_=ob[:, c + 64:c + 65]), [ob], [rr])
                    P.op("dve", lambda e: e.tensor_scalar(out=mix[:, tt, 768 + hd * 64:832 + hd * 64], in0=ob[:, c:c + 64],
                                                          scalar1=rr[:, 0:1], scalar2=None, op0=ALU.mult),
                         [ob, rr], [mix.tks[tt]])

    def out_proj(self, L, mix, x_src, xa_dst, h2T, Wn, s, featT=None):
        P = self.P
        Wout = P.sb(s, "Wout", [128, 8, 1024], BF16, n=8)
        self.load_w(Wout, self.I(f"w_out{L}"), 8, 1024)
        mixT = [P.sb(s, f"mixT{i}", [128, 8, 128], BF16) for i in range(2)]
        xin = [P.sb(s, f"oxin{i}", [128, D], F32) for i in range(2)]
        x1 = [P.sb(s, f"ox1{i}", [128, D], F32) for i in range(2)]
        ptm = P.ps(s, "optm", [128, 1024], BF16)
        pj = [P.ps(s, f"opj{i}", [128, 512], F32) for i in range(2)]
        for tt in range(NT):
            mT, xb, xo = mixT[tt % 2], xin[tt % 2], x1[tt % 2]
            P.dma("sp", xb[:], x_src[tt * 128:(tt + 1) * 128, :], [x_src.tks[tt]], [xb])
            if featT is None:
                for kc in range(8):
                    P.op("pe", lambda e: e.transpose(ptm[:, kc * 128:(kc + 1) * 128], mix[:, tt, kc * 128:(kc + 1) * 128],
                                                     self.idb[:]), [mix.tks[tt], self.idb], [ptm])
                P.op("act", lambda e: e.activation(out=mT[:, :, :], in_=ptm[:, :].rearrange("p (k t) -> p k t", k=8),
                                                   func=AF.Copy), [ptm], [mT])
            for half in range(2):
                for kc in range(8):
                    if featT is None:
                        lhs, ltk = mT[:, kc, :], mT.tk
                    else:
                        lhs, ltk = featT[:, kc, tt * 128:(tt + 1) * 128], featT.tks[kc]
                    P.op("pe", lambda e: e.matmul(pj[half][:, :], lhs, Wout[:, kc, half * 512:(half + 1) * 512],
                                                  start=(kc == 0), stop=(kc == 7)), [ltk, Wout.tks[kc]], [pj[half]])
                P.op("dve", lambda e: e.tensor_tensor(out=xo[:, half * 512:(half + 1) * 512], in0=pj[half][:, :],
                                                      in1=xb[:, half * 512:(half + 1) * 512], op=ALU.add),
                     [pj[half], xb], [xo])
            P.dma("sp", xa_dst[tt * 128:(tt + 1) * 128, :], xo[:], [xo], [xa_dst.tks[tt]])
            self.norm_T(xo, self.gcol(1, L), h2T[:, :, 2 + tt * 128:2 + (tt + 1) * 128], h2T.tk, Wn)
        hs = P.sb(s, "hs", [128, 16], BF16)
        hb = P.sb(s, "hb", [128, 4, 16], BF16)
        hacc = P.sb(s, "hacc", [128, 16], F32)
        P.op("dve", lambda e: e.tensor_copy(out=hs[:, :].rearrange("p (k t) -> p k t", k=8), in_=h2T[:, :, 2048:2050]),
             [h2T], [hs])
        P.dma("sp", self.hsrc[L][:, :], hs[:], [hs], [self.hsrc[L]])
        P.allgather(self.hsrc[L], self.hall[L], GROUPS)
        P.dma("sp", hb[:, :, :], self.hall[L].t.ap().rearrange("(r p) c -> p r c", p=128), [self.hall[L]], [hb])
        P.op("dve", lambda e: e.tensor_scalar(out=hacc[:], in0=hb[:, 0, :], scalar1=self.pc[:, 4:5], scalar2=None,
                                              op0=ALU.mult), [hb, self.pc], [hacc])
        for r in range(1, 4):
            P.op("dve", lambda e: e.scalar_tensor_tensor(out=hacc[:], in0=hb[:, r, :], scalar=self.pc[:, 4 + r:5 + r],
                                                         in1=hacc[:], op0=ALU.mult, op1=ALU.add), [hb, self.pc, hacc], [hacc])
        P.op("dve", lambda e: e.tensor_copy(out=h2T[:, :, 0:2], in_=hacc[:, :].rearrange("p (k t) -> p k t", k=8)),
             [hacc], [h2T])

    def phase_b0(self, mix):
        P = self.P
        if True:
            with contextlib.ExitStack() as s2:
                KT = [P.sb(s2, f"KT{i}", [128, S_FULL], BF16) for i in range(2)]
                Vh = [P.sb(s2, f"Vh{i}", [128, 64, 136], BF16) for i in range(2)]
                PT2 = [P.sb(s2, f"PT{i}", [128, 1024], BF16) for i in range(2)]
                Sb2 = [P.ps(s2, f"Sb{i}", [128, 1024], F32) for i in range(2)]
                PT = [View(PT2[i // 2], (i % 2) * 512, (i % 2 + 1) * 512) for i in range(4)]
                Sb = [View(Sb2[i // 2], (i % 2) * 512, (i % 2 + 1) * 512) for i in range(4)]
                Ob = [P.ps(s2, f"Ob{i}", [128, 512], F32) for i in range(3)]
                F = dict(rr=[P.sb(s2, f"frr{i}", [128, 8], F32) for i in range(2)],
                         t0=[P.sb(s2, f"ft0{i}", [128, 128], F32) for i in range(2)],
                         o=[P.sb(s2, f"fo{i}", [128, 128], F32) for i in range(2)],
                         junk=[P.sb(s2, f"fj{i}", [128, 128], F32) for i in range(2)])
                for v in Vh:
                    P.op("pool", lambda e: e.memset(v[:, :, 128:136], 1.0), [], [v])
                qT, pc = self.qT, self.pc
                for h in range(6):
                    kt, vh = KT[h % 2], Vh[h % 2]
                    for r in range(4):
                        P.dma("sp", kt[:, r * TOK:(r + 1) * TOK],
                              self.kall[h // 2][r * 256 + (h % 2) * 128:r * 256 + (h % 2 + 1) * 128, :],
                              [self.kall[h // 2]], [kt])
                    for r in range(4):
                        P.dma("sp", vh[:, r * 16:(r + 1) * 16, 0:128],
                              self.vall[h // 2].t.ap()[r * TOK:(r + 1) * TOK, (h % 2) * 128:(h % 2 + 1) * 128].rearrange(
                                  "(b p) c -> p b c", p=128),
                              [self.vall[h // 2]], [vh])
                    for T in range(4):
                        jmax = min(63, 51 + 4 * T)

                        def emit_S(j):
                            sb, pt = Sb2[j % 2], PT2[j % 2]
                            for m in range(2):
                                mo = m * 512
                                P.op("pe", lambda e: e.matmul(sb[:, mo:mo + 512], kt[m * 64:(m + 1) * 64, j * 128:(j + 1) * 128],
                                                              qT[m * 64:(m + 1) * 64, h, T * 512:(T + 1) * 512],
                                                              start=True, stop=True), [kt, qT], [sb])
                            for qcp in range(4):
                                off = j - 4 * T - 16 * qcp
                                if off < -1 or off > 3:
                                    continue
                                sc = pc[:, qcp:qcp + 1]
                                for m in range(2):
                                    mo = m * 512
                                    if off >= 1:
                                        P.op("dve", lambda e: e.scalar_tensor_tensor(
                                            out=sb[:, mo:mo + off * 128], in0=self.maskw[:, 0:off * 128], scalar=sc,
                                            in1=sb[:, mo:mo + off * 128], op0=ALU.mult, op1=ALU.add),
                                            [sb, self.maskw, pc], [sb])
                                    if 0 <= off <= 3:
                                        P.op("dve", lambda e: e.scalar_tensor_tensor(
                                            out=sb[:, mo + off * 128:mo + (off + 1) * 128], in0=self.Dg[:, h, :], scalar=sc,
                                            in1=sb[:, mo + off * 128:mo + (off + 1) * 128], op0=ALU.mult, op1=ALU.add),
                                            [sb, self.Dg, pc], [sb])
                                    if 0 <= off + 1 <= 3:
                                        i = off + 1
                                        P.op("dve", lambda e: e.scalar_tensor_tensor(
                                            out=sb[:, mo + i * 128:mo + (i + 1) * 128], in0=self.Cn[:, h, :], scalar=sc,
                                            in1=sb[:, mo + i * 128:mo + (i + 1) * 128], op0=ALU.mult, op1=ALU.add),
                                            [sb, self.Cn, pc], [sb])
                            P.op("act", lambda e: e.activation(out=pt[:], in_=sb[:, :], func=AF.Exp,
                                                               bias=self.btab[:, h, T * 64 + j:T * 64 + j + 1]),
                                 [sb, self.btab], [pt])

                        def emit_PV(j):
                            started = set()
                            for m in range(2):
                                pt = View(PT2[j % 2], m * 512, (m + 1) * 512)
                                for i in range(4):
                                    a = 2 * i + m
                                    ob = Ob[a // 3]
                                    c = (a % 3) * 129
                                    st_ = (j == 0) and (a // 3 not in started)
                                    started.add(a // 3)
                                    P.op("pe", lambda e: e.matmul(ob[:, c:c + 129], pt[:, i * 128:(i + 1) * 128],
                                                                  vh[:, j, 0:129], start=st_, stop=(j == jmax),
                                                                  skip_group_check=True),
                                         [pt, vh], [ob])

                        emit_S(0)
                        for j in range(jmax + 1):
                            if j + 1 <= jmax:
                                emit_S(j + 1)
                            emit_PV(j)
                        for i in range(4):
                            self.attn_finalize_diff(Ob, i, h, T * 4 + i, mix, F, i % 2)
                    if self.stop == "b0h0":
                        break
                self.cross_attn(self.mKT0, self.mV0, self.mqT, mix, Sb, Ob, PT, F)
                P.barrier()

    def phase_b0T(self, mixT):
        P = self.P
        with contextlib.ExitStack() as s2:
            KT = [P.sb(s2, f"KT{i}", [128, S_FULL], BF16) for i in range(2)]
            Vh = [P.sb(s2, f"Vh{i}", [128, 64, 128], BF16) for i in range(2)]
            PT2 = [P.sb(s2, f"PT{i}", [128, 1024], BF16) for i in range(3)]
            Sb2 = [P.ps(s2, f"Sb{i}", [128, 1024], F32) for i in range(3)]
            OT = [P.ps(s2, f"OT{i}", [128, 512], F32) for i in range(2)]
            acc = P.sb(s2, "pacc", [128, 1024], F32)
            lsb = Sb2[0]
            LS = [View(Sb2[2], 0, 512)]
            onesb = P.sb(s2, "onesb", [128, 128], BF16)
            onesf = P.sb(s2, "onesf", [128, 128], F32)
            onesp = P.sb(s2, "onesp", [128, 2, 128], BF16)
            mVp = P.sb(s2, "mVp", [128, 2, 4, 128], BF16)
            eps2 = P.sb(s2, "eps2", [128, 1], F32)
            gOc = self.colg[:, 48:49]
            tmp = [P.sb(s2, f"ftmp{i}", [128, 512], F32) for i in range(4)]
            P.op("pool", lambda e: e.memset(onesb[:], 1.0), [], [onesb])
            P.op("pool", lambda e: e.memset(onesf[:], 1.0), [], [onesf])
            P.op("pool", lambda e: e.memset(onesp[:], 0.0), [], [onesp])
            P.op("pool", lambda e: e.memset(onesp[:, 0, 0:64], 1.0), [onesp], [onesp])
            P.op("pool", lambda e: e.memset(onesp[:, 1, 64:128], 1.0), [onesp], [onesp])
            P.op("pool", lambda e: e.memset(mVp[:], 0.0), [], [mVp])
            P.op("dve", lambda e: e.memset(eps2[:], EPS / 0.64), [], [eps2])
            mV, mKT, mqT = self.mV0, self.mKT0, self.mqT
            for blk in range(2):
                for hd in range(4):
                    hp = hd % 2
                    P.op("pool", lambda e: e.tensor_copy(out=mVp[:, blk, hd, hp * 64:(hp + 1) * 64], in_=mV[:, blk, hd, 0:64]),
                         [mV, mVp], [mVp])
            qT, pc = self.qT, self.pc
            for h in range(6):
                kt, vh = KT[h % 2], Vh[h % 2]
                for r in range(4):
                    P.dma("sp", kt[:, r * TOK:(r + 1) * TOK],
                          self.kall[h // 2][r * 256 + (h % 2) * 128:r * 256 + (h % 2 + 1) * 128, :],
                          [self.kall[h // 2]], [kt])
                for r in range(4):
                    P.dma("sp", vh[:, r * 16:(r + 1) * 16, :],
                          self.vall[h // 2].t.ap()[r * TOK:(r + 1) * TOK, (h % 2) * 128:(h % 2 + 1) * 128].rearrange(
                              "(b p) c -> p b c", p=128),
                          [self.vall[h // 2]], [vh])
                for T in range(4):
                    jmax = min(63, 51 + 4 * T)

                    def emit_S(j):
                        sb, pt = Sb2[j % 3], PT2[j % 3]
                        for m in range(2):
                            mo = m * 512
                            P.op("pe", lambda e: e.matmul(sb[:, mo:mo + 512], kt[m * 64:(m + 1) * 64, j * 128:(j + 1) * 128],
                                                          qT[m * 64:(m + 1) * 64, h, T * 512:(T + 1) * 512],
                                                          start=True, stop=True), [kt, qT], [sb])
                        for qcp in range(4):
                            off = j - 4 * T - 16 * qcp
                            if off < -1 or off > 3:
                                continue
                            sc = pc[:, qcp:qcp + 1]
                            for m in range(2):
                                mo = m * 512
                                if off >= 1:
                                    P.op("dve", lambda e: e.scalar_tensor_tensor(
                                        out=sb[:, mo:mo + off * 128], in0=self.maskw[:, 0:off * 128], scalar=sc,
                                        in1=sb[:, mo:mo + off * 128], op0=ALU.mult, op1=ALU.add),
                                        [sb, self.maskw, pc], [sb])
                                if 0 <= off <= 3:
                                    P.op("dve", lambda e: e.scalar_tensor_tensor(
                                        out=sb[:, mo + off * 128:mo + (off + 1) * 128], in0=self.Dg[:, h, :], scalar=sc,
                                        in1=sb[:, mo + off * 128:mo + (off + 1) * 128], op0=ALU.mult, op1=ALU.add),
                                        [sb, self.Dg, pc], [sb])
                                if 0 <= off + 1 <= 3:
                                    i = off + 1
                                    P.op("dve", lambda e: e.scalar_tensor_tensor(
                                        out=sb[:, mo + i * 128:mo + (i + 1) * 128], in0=self.Cn[:, h, :], scalar=sc,
                                        in1=sb[:, mo + i * 128:mo + (i + 1) * 128], op0=ALU.mult, op1=ALU.add),
                                        [sb, self.Cn, pc], [sb])
                        P.op("act", lambda e: e.activation(out=pt[:], in_=sb[:, :], func=AF.Exp,
                                                           bias=self.btab[:, h, T * 64 + j:T * 64 + j + 1]),
                             [sb, self.btab], [pt])

                    def emit_PV(j):
                        pt = PT2[j % 3]
                        for m in range(2):
                            mo = m * 512
                            P.op("pe", lambda e: e.matmul(OT[m][:, :], vh[:, j, :], pt[:, mo:mo + 512],
                                                          start=(j == 0), stop=(j == jmax)), [pt, vh], [OT[m]])
                        for m, eng in ((0, "dve"), (1, "pool")):
                            mo = m * 512
                            if j == 0:
                                P.op(eng, lambda e: e.tensor_copy(out=acc[:, mo:mo + 512], in_=pt[:, mo:mo + 512]), [pt], [acc])
                            else:
                                P.op(eng, lambda e: e.tensor_tensor(out=acc[:, mo:mo + 512], in0=acc[:, mo:mo + 512],
                                                                    in1=pt[:, mo:mo + 512], op=ALU.add), [pt, acc], [acc])

                    emit_S(0)
                    if jmax >= 1:
                        emit_S(1)
                    for j in range(jmax + 1):
                        if j + 2 <= jmax:
                            emit_S(j + 2)
                        emit_PV(j)
                    for m in range(2):
                        mo = m * 512
                        P.op("pe", lambda e: e.matmul(lsb[:, mo:mo + 512], onesf[:, :], acc[:, mo:mo + 512], start=True, stop=True),
                             [onesf, acc], [lsb])
                    r0, r1, t0, t1 = tmp
                    o, sq, rt_, rs_ = r0, r1, t0, t1
                    P.op("dve", lambda e: e.reciprocal(out=r0[:], in_=lsb[:, 0:512]), [lsb], [r0])
                    P.op("dve", lambda e: e.reciprocal(out=r1[:], in_=lsb[:, 512:1024]), [lsb], [r1])
                    P.op("dve", lambda e: e.tensor_tensor(out=t0[:], in0=OT[0][:, :], in1=r0[:], op=ALU.mult), [OT[0], r0], [t0])
                    P.op("dve", lambda e: e.tensor_tensor(out=t1[:], in0=OT[1][:, :], in1=r1[:], op=ALU.mult), [OT[1], r1], [t1])
                    P.op("dve", lambda e: e.scalar_tensor_tensor(out=o[:], in0=t1[:], scalar=self.lam[:, 4:5], in1=t0[:],
                                                                 op0=ALU.mult, op1=ALU.add), [t1, t0, self.lam], [o])
                    P.op("pool", lambda e: e.tensor_tensor(out=sq[:], in0=o[:], in1=o[:], op=ALU.mult), [o], [sq])
                    P.op("pe", lambda e: e.matmul(lsb[:, 0:512], onesf[:, :], sq[:, :], start=True, stop=True), [onesf, sq], [lsb])
                    P.op("act", lambda e: e.activation(out=rt_[:], in_=lsb[:, 0:512], func=AF.Sqrt, scale=1.0 / (128 * 0.64),
                                                       bias=eps2[:, 0:1]), [lsb, eps2], [rt_])
                    P.op("dve", lambda e: e.reciprocal(out=rs_[:], in_=rt_[:]), [rt_], [rs_])
                    P.op("dve", lambda e: e.scalar_tensor_tensor(out=mixT[:, h, T * 512:(T + 1) * 512], in0=o[:], scalar=gOc,
                                                                 in1=rs_[:], op0=ALU.mult, op1=ALU.mult),
                         [o, rs_, self.colg], [mixT.tks[h]])
            cnt = 0
            for pp in range(2):
                for T in range(4):
                    first = True
                    for blk in range(2):
                        for hp in range(2):
                            hd = pp * 2 + hp
                            sb, pt = Sb2[cnt % 2], PT2[cnt % 2]
                            cnt += 1
                            P.op("pe", lambda e: e.matmul(sb[:, 0:512], mKT[hp * 64:(hp + 1) * 64, pp, blk * 128:(blk + 1) * 128],
                                                          mqT[hp * 64:(hp + 1) * 64, pp, T * 512:(T + 1) * 512],
                                                          start=True, stop=True), [mKT, mqT], [sb])
                            P.op("act", lambda e: e.activation(out=pt[:, 0:512], in_=sb[:, 0:512], func=AF.Exp), [sb], [pt])
                            last = (blk == 1 and hp == 1)
                            P.op("pe", lambda e: e.matmul(OT[0][:, :], mVp[:, blk, hd, :], pt[:, 0:512], start=first, stop=last),
                                 [pt, mVp], [OT[0]])
                            P.op("pe", lambda e: e.matmul(LS[0][:, :], onesp[:, hp, :], pt[:, 0:512], start=first, stop=last),
                                 [pt, onesp], [LS[0]])
                            first = False
                    r0 = tmp[0]
                    P.op("dve", lambda e: e.reciprocal(out=r0[:], in_=LS[0][:, :]), [LS[0]], [r0])
                    P.op("dve", lambda e: e.tensor_tensor(out=mixT[:, 6 + pp, T * 512:(T + 1) * 512], in0=OT[0][:, :], in1=r0[:],
                                                          op=ALU.mult), [OT[0], r0], [mixT.tks[6 + pp]])
            P.barrier()


    def phase_out(self, L, mix, x_src, h2T, mixT=None):
        P = self.P
        if f"mix{L}" in self.dbg and mix is not None:
            for tt in range(NT):
                P.dma("sp", self.dbg[f"mix{L}"][tt * 128:(tt + 1) * 128, :], mix[:, tt, :], [mix.tks[tt]],
                      [self.dbg[f"mix{L}"]])
        with contextlib.ExitStack() as s3:
            Wn = dict(junk=P.sb(s3, "njunk", [128, D], F32), ss=P.sb(s3, "nss", [128, 1], F32),
                      rt=P.sb(s3, "nrt", [128, 2], F32), xn=P.sb(s3, "nxn", [128, D], BF16),
                      pt=P.ps(s3, "npt0", [128, 1024], BF16))
            self.out_proj(L, mix, x_src, self.xa[L], h2T, Wn, s3, featT=mixT)
            P.barrier()

    def ffn(self, L, h2T, xa_src, x_dst):
        P = self.P
        with contextlib.ExitStack() as s:
            actT = P.sb(s, "actT", [128, 22, TOK], BF16, n=22)
            cw = P.sb(s, "cw", [128, 44, 3], F32)
            cb = P.sb(s, "cb", [128, 44], F32)
            P.dma("sp", cw[:, :, :], self.I("convw")[:, L * 132:(L + 1) * 132].rearrange("p (c w) -> p c w", w=3),
                  [self.I("convw")], [cw])
            P.dma("sp", cb[:], self.I("convb")[:, L * 44:(L + 1) * 44], [self.I("convb")], [cb])
            with contextlib.ExitStack() as s2:
                wup = [P.sb(s2, f"wup{i}", [128, 8, 256], BF16, n=8) for i in range(2)]
                us = [P.sb(s2, f"us{i}", [128, 2, 1026], F32) for i in range(2)]
                cv = [P.sb(s2, f"cv{i}", [128, 2, 1024], F32) for i in range(2)]
                pu = [P.ps(s2, f"pu{i}", [128, 512], F32) for i in range(4)]
                ph = P.ps(s2, "ph", [128, 512], F32, n=4)
                wsrc = self.I(f"w_up{L}")
                it = 0
                for cc in range(22):
                    wb = wup[cc % 2]
                    P.dma("pool", wb[:, :, 0:128],
                          wsrc.t.ap()[:, cc * 128:(cc + 1) * 128].rearrange("(k p) c -> p k c", p=128), [wsrc], wb.tks)
                    P.dma("pool", wb[:, :, 128:256],
                          wsrc.t.ap()[:, DFF + cc * 128:DFF + (cc + 1) * 128].rearrange("(k p) c -> p k c", p=128),
                          [wsrc], wb.tks)
                    for half in range(2):
                        ub, cvb = us[it % 2], cv[it % 2]
                        t0 = half * 1024
                        for part in range(2):
                            hs = ph.tks[(2 * it + part) % 4]
                            hc = ((2 * it + part) % 4) * 8
                            for kc in range(8):
                                P.op("pe", lambda e: e.matmul(ph[:, hc:hc + 2], wb[:, kc, part * 128:(part + 1) * 128],
                                                              h2T[:, kc, t0:t0 + 2], start=(kc == 0), stop=(kc == 7)),
                                     [wb.tks[kc], h2T], [hs])
                            P.op("act", lambda e: e.activation(out=ub[:, part, 0:2], in_=ph[:, hc:hc + 2], func=AF.Copy),
                                 [hs], [ub])
                            for pc_ in range(2):
                                pb = pu[(4 * it + 2 * part + pc_) % 4]
                                c0 = t0 + 2 + pc_ * 512
                                for kc in range(8):
                                    P.op("pe", lambda e: e.matmul(pb[:, :], wb[:, kc, part * 128:(part + 1) * 128],
                                                                  h2T[:, kc, c0:c0 + 512], start=(kc == 0), stop=(kc == 7)),
                                         [wb.tks[kc], h2T], [pb])
                                P.op("act", lambda e: e.activation(out=ub[:, part, 2 + pc_ * 512:2 + (pc_ + 1) * 512],
                                                                   in_=pb[:, :], func=AF.Copy), [pb], [ub])
                        for part in range(2):
                            ch = part * 22 + cc
                            P.op("act", lambda e: e.activation(out=cvb[:, part, :], in_=ub[:, part, 2:1026], func=AF.Identity,
                                                               scale=cw[:, ch, 2:3], bias=cb[:, ch:ch + 1]),
                                 [ub, cw, cb], [cvb])
                            P.op("dve", lambda e: e.scalar_tensor_tensor(out=cvb[:, part, :], in0=ub[:, part, 1:1025],
                                                                         scalar=cw[:, ch, 1:2], in1=cvb[:, part, :],
                                                                         op0=ALU.mult, op1=ALU.add), [ub, cw, cvb], [cvb])
                            P.op("dve", lambda e: e.scalar_tensor_tensor(out=cvb[:, part, :], in0=ub[:, part, 0:1024],
                                                                         scalar=cw[:, ch, 0:1], in1=cvb[:, part, :],
                                                                         op0=ALU.mult, op1=ALU.add), [ub, cw, cvb], [cvb])
                        P.op("act", lambda e: e.activation(out=cvb[:, 1, :], in_=cvb[:, 1, :], func=AF.Silu), [cvb], [cvb])
                        P.op("dve", lambda e: e.tensor_tensor(out=actT[:, cc, t0:t0 + 1024], in0=cvb[:, 0, :], in1=cvb[:, 1, :],
                                                              op=ALU.mult), [cvb], [actT.tks[cc]])
                        it += 1
                P.barrier()
            with contextlib.ExitStack() as s3:
                Wd = P.sb(s3, "Wd", [128, 22, 1024], BF16, n=22)
                self.load_w(Wd, self.I(f"w_down{L}"), 22, 1024)
                xin = [P.sb(s3, f"dxin{i}", [128, D], F32) for i in range(2)]
                xo = [P.sb(s3, f"dxo{i}", [128, D], F32) for i in range(2)]
                pd = [P.ps(s3, f"pd{i}", [128, 512], F32) for i in range(4)]
                for tt in range(NT):
                    xb, xob = xin[tt % 2], xo[tt % 2]
                    P.dma("sp", xb[:], xa_src[tt * 128:(tt + 1) * 128, :], [xa_src.tks[tt]], [xb])
                    for half in range(2):
                        pb = pd[(2 * tt + half) % 4]
                        for cc in range(22):
                            P.op("pe", lambda e: e.matmul(pb[:, :], actT[:, cc, tt * 128:(tt + 1) * 128],
                                                          Wd[:, cc, half * 512:(half + 1) * 512],
                                                          start=(cc == 0), stop=(cc == 21)), [actT.tks[cc], Wd.tks[cc]], [pb])
                        P.op("dve", lambda e: e.tensor_tensor(out=xob[:, half * 512:(half + 1) * 512], in0=pb[:, :],
                                                              in1=xb[:, half * 512:(half + 1) * 512], op=ALU.add),
                             [pb, xb], [xob])
                    P.dma("sp", x_dst[tt * 128:(tt + 1) * 128, :], xob[:], [xob], [x_dst.tks[tt]])
                P.barrier()

    def phase_c1(self, st, QAT):
        P = self.P
        bc = self.bc
        with contextlib.ExitStack() as s:
            Win = P.sb(s, "Win1", [128, 8, 2576], BF16, n=8)
            self.load_w(Win, self.I("w_in_gla"), 8, 2576)
            gw = P.sb(s, "gw", [16, 384], F32)
            P.dma("sp", gw[:], self.I("gla_gate_w")[:, :], [self.I("gla_gate_w")], [gw])
            glam = P.sb(s, "glam", [128, 256], F32)
            P.dma("sp", glam[:], self.I("glam")[:, :], [self.I("glam")], [glam])
            ones = P.sb(s, "ones", [128, 1], F32)
            P.op("dve", lambda e: e.memset(ones[:], 1.0), [], [ones])
            xin = [P.sb(s, f"cxin{i}", [128, D], F32) for i in range(2)]
            hT = [P.sb(s, f"chT{i}", [128, 8, 128], BF16) for i in range(2)]
            prt = [P.sb(s, f"prt{i}", [128, 2192], F32) for i in range(2)]
            W = dict(junk=P.sb(s, "cjunk", [128, D], F32), ss=P.sb(s, "css", [128, 1], F32),
                     rt=P.sb(s, "crt", [128, 2], F32), xn=P.sb(s, "cxn", [128, D], BF16),
                     pt=P.ps(s, "cpt0", [128, 1024], BF16),
                     sq=P.sb(s, "csq", [128, 768], F32), ssg=P.sb(s, "cssg", [128, 96], F32),
                     t1=P.sb(s, "ct1", [128, 768], F32))
            pt = W["pt"]
            Bk = [P.ps(s, f"cB{i}", [128, 512], F32) for i in range(7)]
            pj = [Bk[0], Bk[1]]
            pq, pk, pz, pgT, pkv = Bk[2], Bk[3], Bk[4], Bk[5], Bk[6]
            mn = P.sb(s, "cmn", [128, 256], BF16)
            vb = P.sb(s, "cvb", [128, 768], BF16)
            glT = P.sb(s, "glT", [16, 128], F32)
            la = P.sb(s, "la", [128, 384], F32)
            Ep = P.sb(s, "Ep", [96, 512], F32)
            En = P.sb(s, "En", [96, 512], F32)
            QpT = P.sb(s, "QpT", [96, 4, 128], BF16)
            QnT = P.sb(s, "QnT", [96, 4, 128], BF16)
            KpT = P.sb(s, "KpT", [96, 4, 128], BF16)
            KnT = P.sb(s, "KnT", [96, 4, 128], BF16)
            Qlo = P.sb(s, "Qlo", [96, 4, 128], BF16)
            Qhi = P.sb(s, "Qhi", [96, 4, 128], BF16)
            Ed = P.sb(s, "Ed", [128, 384], F32)
            Kend = P.sb(s, "Kend", [128, 2, 384], BF16)
            sc1 = P.sb(s, "sc1", [128, 128], F32)
            sc2 = P.sb(s, "sc2", [128, 128], F32)
            scT = P.sb(s, "scT", [128, 4, 128], BF16)
            S = P.sb(s, "Sloc", [96, 768], F32)
            Sa = P.sb(s, "Sa", [96, 768], BF16)
            Sb_ = P.sb(s, "Sbb", [96, 768], BF16)
            A = P.sb(s, "Acum", [96, 4], F32)
            gsum = P.sb(s, "gsum", [96, 772], F32)
            srb = [P.sb(s, f"srb{i}", [128, 768], BF16) for i in range(2)]
            olb = [P.sb(s, f"olb{i}", [128, 768], F32) for i in range(2)]
            P.op("dve", lambda e: e.memset(S[:], 0.0), [], [S])
            P.op("dve", lambda e: e.memset(A[:], 1.0), [], [A])
            P.op("pool", lambda e: e.memset(Qlo[:], 0.0), [], [Qlo])
            P.op("pool", lambda e: e.memset(Qhi[:], 0.0), [], [Qhi])
            mle, mgt = glam[:, 0:128], glam[:, 128:256]
            ogb = BC_OFF["gla_gb"]
            chunks = [(384, 512), (896, 512), (1408, 512), (1920, 512), (2432, 144)]
            import os
            LV = int(os.environ.get("K_C1LVL", "99"))
            for tt in range(int(os.environ.get("K_C1NT", NT))):
                xb, hb, pr = xin[tt % 2], hT[tt % 2], prt[tt % 2]
                P.dma("sp", xb[:], self.xf0[tt * 128:(tt + 1) * 128, :], [self.xf0.tks[tt]], [xb])
                self.norm_T(xb, self.gcol(0, 1), hb[:, :, :], hb.tk, W)
                for n, (c0, w) in enumerate(chunks):
                    pb = pj[n % 2]
                    for kc in range(8):
                        P.op("pe", lambda e: e.matmul(pb[:, 0:w], hb[:, kc, :], Win[:, kc, c0:c0 + w],
                                                      start=(kc == 0), stop=(kc == 7)), [hb, Win.tks[kc]], [pb])
                    if n % 2 == 0:
                        P.op("act", lambda e: e.activation(out=pr[:, c0 - 384:c0 - 384 + w], in_=pb[:, 0:w], func=AF.Copy),
                             [pb], [pr])
                    else:
                        P.op("dve", lambda e: e.tensor_copy(out=pr[:, c0 - 384:c0 - 384 + w], in_=pb[:, 0:w]), [pb], [pr])
                if LV < 1:
                    continue
                for h in range(4):
                    for kc in range(8):
                        P.op("pe", lambda e: e.matmul(pq[0:96, h * 128:(h + 1) * 128], Win[:, kc, h * 96:(h + 1) * 96],
                                                      hb[:, kc, :], start=(kc == 0), stop=(kc == 7)),
                             [hb, Win.tks[kc]], [pq])
                for h in range(4):
                    for kc in range(8):
                        P.op("pe", lambda e: e.matmul(pk[0:96, h * 128:(h + 1) * 128],
                                                      Win[:, kc, 384 + h * 96:384 + (h + 1) * 96],
                                                      hb[:, kc, :], start=(kc == 0), stop=(kc == 7)),
                             [hb, Win.tks[kc]], [pk])
                for kc in range(8):
                    P.op("pe", lambda e: e.matmul(pz[0:16, 384:512], Win[:, kc, 2304:2320], hb[:, kc, :],
                                                  start=(kc == 0), stop=(kc == 7)), [hb, Win.tks[kc]], [pz])
                P.op("act", lambda e: e.activation(out=glT[:], in_=pz[0:16, 384:512], func=AF.Copy), [pz], [glT])
                if LV < 2:
                    continue
                self.group_norm(pr, 1936, 4, self.gains[:, 256:320], mn[:, :].rearrange("p (g d) -> p g d", d=64),
                                mn.tk, W, [pr])
                for h in range(2):
                    P.op("pe", lambda e: e.transpose(pt[:, h * 128:(h + 1) * 128], mn[:, h * 128:(h + 1) * 128],
                                                     self.idb[:]), [mn, self.idb], [pt])
                P.op("act", lambda e: e.activation(out=self.mqT[:, :, tt * 128:(tt + 1) * 128],
                                                   in_=pt[:, 0:256].rearrange("p (h t) -> p h t", h=2), func=AF.Copy),
                     [pt], [self.mqT])
                if LV < 3:
                    continue
                srt = srb[tt % 2]
                P.op("act", lambda e: e.activation(out=srt[:], in_=pr[:, 1152:1920], func=AF.Silu), [pr], [srt])
                P.dma("sp", self.srd[tt * 128:(tt + 1) * 128, :], srt[:], [srt], [self.srd.tks[tt]])
                P.op("pool", lambda e: e.tensor_copy(out=vb[:], in_=pr[:, 384:1152]), [pr], [vb])
                if LV < 4:
                    continue
                P.op("pe", lambda e: e.matmul(pz[:, 0:384], glT[:, :], gw[:, :], start=True, stop=True), [glT, gw], [pz])
                P.op("dve", lambda e: e.tensor_tensor(out=la[:], in0=pz[:, 0:384], in1=bc[:, ogb:ogb + 384], op=ALU.add),
                     [pz, bc], [la])
                P.op("act", lambda e: e.activation(out=la[:], in_=la[:], func=AF.Exp, scale=-1.0), [la], [la])
                P.op("act", lambda e: e.activation(out=la[:], in_=la[:], func=AF.Ln, bias=ones[:, 0:1]), [la, ones], [la])
                P.op("dve", lambda e: e.tensor_scalar(out=la[:], in0=la[:], scalar1=-1.0 / 16, scalar2=None, op0=ALU.mult),
                     [la], [la])
                if LV < 5:
                    continue
                for h in range(4):
                    P.op("pe", lambda e: e.matmul(pgT[0:96, h * 128:(h + 1) * 128], la[:, h * 96:(h + 1) * 96], mle,
                                                  start=True, stop=True), [la, glam], [pgT])
                P.op("pe", lambda e: e.matmul(pz[:, 0:384], mgt, la[:, :], start=True, stop=True), [la, glam], [pz])
                P.op("act", lambda e: e.activation(out=Ep[:], in_=pgT[0:96, :], func=AF.Exp), [pgT], [Ep])
                P.op("act", lambda e: e.activation(out=En[:], in_=pgT[0:96, :], func=AF.Exp, scale=-1.0), [pgT], [En])
                P.op("act", lambda e: e.activation(out=Ed[:], in_=pz[:, 0:384], func=AF.Exp), [pz], [Ed])
                if LV < 6:
                    continue
                qs = 96 ** -0.5
                P.op("dve", lambda e: e.scalar_tensor_tensor(out=QpT[:, :, :].rearrange("p h t -> p (h t)"), in0=pq[0:96, :],
                                                             scalar=qs, in1=Ep[:], op0=ALU.mult, op1=ALU.mult),
                     [pq, Ep], [QpT])
                P.op("dve", lambda e: e.scalar_tensor_tensor(out=QnT[:, :, :].rearrange("p h t -> p (h t)"), in0=pq[0:96, :],
                                                             scalar=qs, in1=En[:], op0=ALU.mult, op1=ALU.mult),
                     [pq, En], [QnT])
                P.op("dve", lambda e: e.tensor_tensor(out=KpT[:, :, :].rearrange("p h t -> p (h t)"), in0=pk[0:96, :],
                                                      in1=Ep[:], op=ALU.mult), [pk, Ep], [KpT])
                P.op("dve", lambda e: e.tensor_tensor(out=KnT[:, :, :].rearrange("p h t -> p (h t)"), in0=pk[0:96, :],
                                                      in1=En[:], op=ALU.mult), [pk, En], [KnT])
                for ck in range(2):
                    msel = glam[:, 63 + 64 * ck:64 + 64 * ck]
                    P.op("dve", lambda e: e.scalar_tensor_tensor(out=Kend[:, ck, :], in0=pr[:, 0:384], scalar=msel, in1=Ed[:],
                                                                 op0=ALU.mult, op1=ALU.mult), [pr, Ed, glam], [Kend])
                if LV < 7:
                    continue
                P.op("pool", lambda e: e.tensor_copy(out=Qlo[:, :, 0:64], in_=QpT[:, :, 0:64]), [QpT], [Qlo])
                P.op("pool", lambda e: e.tensor_copy(out=Qhi[:, :, 64:128], in_=QpT[:, :, 64:128]), [QpT], [Qhi])
                if LV < 8:
                    continue
                for h in range(4):
                    P.op("dve", lambda e: e.tensor_scalar(out=QAT[:, tt, h, 0:64], in0=QpT[:, h, 0:64], scalar1=A[:, h:h + 1],
                                                          scalar2=None, op0=ALU.mult), [QpT, A], [QAT.tks[tt]])
                for h in range(4):
                    P.op("dve", lambda e: e.tensor_tensor(out=A[:, h:h + 1], in0=A[:, h:h + 1], in1=Ep[:, h * 128 + 63:h * 128 + 64],
                                                          op=ALU.mult), [A, Ep], [A])
                for h in range(4):
                    P.op("dve", lambda e: e.tensor_scalar(out=QAT[:, tt, h, 64:128], in0=QpT[:, h, 64:128], scalar1=A[:, h:h + 1],
                                                          scalar2=None, op0=ALU.mult), [QpT, A], [QAT.tks[tt]])
                for h in range(4):
                    P.op("dve", lambda e: e.tensor_tensor(out=A[:, h:h + 1], in0=A[:, h:h + 1], in1=Ep[:, h * 128 + 127:h * 128 + 128],
                                                          op=ALU.mult), [A, Ep], [A])
                if LV < 9:
                    continue
                psc = [pq, pk]
                for h in range(4):
                    pb = psc[h // 2]
                    c = (h % 2) * 256
                    P.op("pe", lambda e: e.matmul(pb[:, c:c + 128], KnT[:, h, :], QpT[:, h, :], start=True, stop=True),
                         [KnT, QpT], [pb])
                    P.op("pe", lambda e: e.matmul(pb[:, c + 128:c + 256], KpT[:, h, :], QnT[:, h, :], start=True, stop=True),
                         [KpT, QnT], [pb])
                    P.op("dve", lambda e: e.tensor_tensor(out=sc1[:], in0=pb[:, c:c + 128], in1=mle, op=ALU.mult),
                         [pb, glam], [sc1])
                    P.op("dve", lambda e: e.tensor_tensor(out=sc2[:], in0=pb[:, c + 128:c + 256], in1=mgt, op=ALU.mult),
                         [pb, glam], [sc2])
                    P.op("pool", lambda e: e.tensor_tensor(out=scT[:, h, :], in0=sc1[:], in1=sc2[:], op=ALU.add),
                         [sc1, sc2], [scT])
                if LV < 10:
                    continue
                P.op("act", lambda e: e.activation(out=Sa[:], in_=S[:], func=AF.Copy), [S], [Sa])
                for h in range(4):
                    hs = slice(h * 192, (h + 1) * 192)
                    for ck in range(int(os.environ.get("K_NCK", 2))):
                        P.op("pe", lambda e: e.matmul(pkv[0:96, ck * 192:(ck + 1) * 192],
                                                      Kend[:, ck, h * 96:(h + 1) * 96],
                                                      vb[:, hs], start=True, stop=True),
                             [Kend, vb], [pkv])
                    if os.environ.get("K_SUB") == "a":
                        continue
                    P.op("dve", lambda e: e.scalar_tensor_tensor(out=S[:, hs], in0=S[:, hs], scalar=Ep[:, h * 128 + 63:h * 128 + 64],
                                                                 in1=pkv[0:96, 0:192], op0=ALU.mult, op1=ALU.add),
                         [S, Ep, pkv], [S])
                    P.op("act", lambda e: e.activation(out=Sb_[:, hs], in_=S[:, hs], func=AF.Copy), [S], [Sb_])
                    P.op("dve", lambda e: e.scalar_tensor_tensor(out=S[:, hs], in0=S[:, hs],
                                                                 scalar=Ep[:, h * 128 + 127:h * 128 + 128],
                                                                 in1=pkv[0:96, 192:384], op0=ALU.mult, op1=ALU.add),
                         [S, Ep, pkv], [S])
                    if os.environ.get("K_SUB") == "b":
                        continue
                    ob = pj[h // 2]
                    oc = (h % 2) * 192
                    P.op("pe", lambda e: e.matmul(ob[:, oc:oc + 192], scT[:, h, :], vb[:, hs], start=True, stop=False),
                         [scT, vb], [ob])
                    P.op("pe", lambda e: e.matmul(ob[:, oc:oc + 192], Qlo[:, h, :], Sa[:, hs], start=False, stop=False),
                         [Qlo, Sa], [ob])
                    P.op("pe", lambda e: e.matmul(ob[:, oc:oc + 192], Qhi[:, h, :], Sb_[:, hs], start=False, stop=True),
                         [Qhi, Sb_], [ob])
                    P.op("dve", lambda e: e.tensor_copy(out=olb[tt % 2][:, hs], in_=ob[:, oc:oc + 192]), [ob], [olb[tt % 2]])
                P.dma("sp", self.olocd[tt * 128:(tt + 1) * 128, :], olb[tt % 2][:], [olb[tt % 2]], [self.olocd.tks[tt]])
            P.op("dve", lambda e: e.tensor_copy(out=gsum[:, 0:4], in_=A[:]), [A], [gsum])
            P.op("dve", lambda e: e.tensor_copy(out=gsum[:, 4:772], in_=S[:]), [S], [gsum])
            P.dma("sp", self.gsrc[:, :], gsum[:], [gsum], [self.gsrc])
            if not os.environ.get("K_NOCC"):
                P.allgather(self.gsrc, self.gall, GROUPS)
            P.barrier()

    def phase_d1(self, mix, QAT, mKT, mV):
        P = self.P
        pc = self.pc
        with contextlib.ExitStack() as s:
            ga = P.sb(s, "ga", [96, 4, 772], F32)
            P.dma("sp", ga[:, :, :], self.gall.t.ap().rearrange("(r p) c -> p r c", p=96), [self.gall], [ga])
            Si = P.sb(s, "Si", [96, 768], F32)
            Sib = P.sb(s, "Sib", [96, 768], BF16)
            coef = P.sb(s, "coef", [96, 4], F32)
            tmpL = P.sb(s, "tmpL", [96, 768], F32)
            P.op("dve", lambda e: e.memset(Si[:], 0.0), [], [Si])
            for r in range(4):
                lt, nlt = pc[0:96, 8 + r:9 + r], pc[0:96, 12 + r:13 + r]
                P.op("dve", lambda e: e.tensor_scalar(out=coef[:], in0=ga[:, r, 0:4], scalar1=lt, scalar2=nlt,
                                                      op0=ALU.mult, op1=ALU.add), [ga, pc], [coef])
                P.op("dve", lambda e: e.tensor_scalar(out=tmpL[:], in0=ga[:, r, 4:772], scalar1=lt, scalar2=None,
                                                      op0=ALU.mult), [ga, pc], [tmpL])
                for h in range(4):
                    hs = slice(h * 192, (h + 1) * 192)
                    P.op("dve", lambda e: e.scalar_tensor_tensor(out=Si[:, hs], in0=Si[:, hs], scalar=coef[:, h:h + 1],
                                                                 in1=tmpL[:, hs], op0=ALU.mult, op1=ALU.add),
                         [Si, coef, tmpL], [Si])
            P.op("act", lambda e: e.activation(out=Sib[:], in_=Si[:], func=AF.Copy), [Si], [Sib])
            pcb = [P.ps(s, f"dB{i}", [128, 512], F32) for i in range(2)]
            o = [P.sb(s, f"do{i}", [128, 768], F32) for i in range(2)]
            sq = P.sb(s, "dsq", [128, 768], F32)
            st4 = P.sb(s, "dst4", [128, 12], F32)
            t1 = P.sb(s, "dt1", [128, 768], F32)
            t2 = P.sb(s, "dt2", [128, 768], F32)
            olb = [P.sb(s, f"dolb{i}", [128, 768], F32) for i in range(2)]
            srb = [P.sb(s, f"dsrb{i}", [128, 768], BF16) for i in range(2)]
            for tt in range(NT):
                ot = o[tt % 2]
                ol, srt = olb[tt % 2], srb[tt % 2]
                P.dma("sp", ol[:], self.olocd[tt * 128:(tt + 1) * 128, :], [self.olocd.tks[tt]], [ol])
                P.dma("sp", srt[:], self.srd[tt * 128:(tt + 1) * 128, :], [self.srd.tks[tt]], [srt])
                for h in range(4):
                    hs = slice(h * 192, (h + 1) * 192)
                    ob = pcb[h // 2]
                    oc = (h % 2) * 192
                    P.op("pe", lambda e: e.matmul(ob[:, oc:oc + 192], QAT[:, tt, h, :], Sib[:, hs], start=True, stop=True),
                         [QAT.tks[tt], Sib], [ob])
                    P.op("dve", lambda e: e.tensor_tensor(out=ot[:, hs], in0=ob[:, oc:oc + 192], in1=ol[:, hs], op=ALU.add),
                         [ob, ol], [ot])
                P.op("pool", lambda e: e.tensor_tensor(out=sq[:], in0=ot[:], in1=ot[:], op=ALU.mult), [ot], [sq])
                P.op("dve", lambda e: e.tensor_reduce(out=st4[:, 0:4], in_=sq[:, :].rearrange("p (g d) -> p g d", d=192),
                                                      axis=AX.X, op=ALU.add), [sq], [st4])
                P.op("act", lambda e: e.activation(out=st4[:, 4:8], in_=st4[:, 0:4], func=AF.Sqrt, scale=1.0 / 192,
                                                   bias=self.epsb[:, 0:1]), [st4, self.epsb], [st4])
                P.op("dve", lambda e: e.reciprocal(out=st4[:, 8:12], in_=st4[:, 4:8]), [st4], [st4])
                P.op("dve", lambda e: e.tensor_tensor(out=t1[:, :].rearrange("p (g d) -> p g d", d=192),
                                                      in0=ot[:, :].rearrange("p (g d) -> p g d", d=192),
                                                      in1=st4[:, 8:12].unsqueeze(2).to_broadcast([128, 4, 192]), op=ALU.mult),
                     [ot, st4], [t1])
                P.op("pool", lambda e: e.tensor_tensor(out=t2[:, :].rearrange("p (g d) -> p g d", d=192),
                                                       in0=t1[:, :].rearrange("p (g d) -> p g d", d=192),
                                                       in1=self.gains[:, 512:704].unsqueeze(1).to_broadcast([128, 4, 192]),
                                                       op=ALU.mult), [t1, self.gains], [t2])
                P.op("dve", lambda e: e.tensor_tensor(out=mix[:, tt, 0:768], in0=t2[:], in1=srt[:], op=ALU.mult),
                     [t2, srt], [mix.tks[tt]])
            with contextlib.ExitStack() as s2:
                PT = [P.sb(s2, f"xPT{i}", [128, 512], BF16) for i in range(4)]
                Sb = [P.ps(s2, f"xSb{i}", [128, 512], F32) for i in range(4)]
                Ob = [P.ps(s2, f"xOb{i}", [128, 512], F32) for i in range(2)]
                F = dict(rr=[P.sb(s2, f"xfrr{i}", [128, 8], F32) for i in range(2)])
                self.cross_attn(mKT, mV, self.mqT, mix, Sb, Ob, PT, F)
            P.barrier()

    def build(self):
        P = self.P
        with contextlib.ExitStack() as st:
            self.setup(st)
            self.setup_gains(st)
            P.barrier()
            with contextlib.ExitStack() as sl0:
                h2T = P.sb(sl0, "h2T", [128, 8, TOK + 2], BF16)
                with contextlib.ExitStack() as satt:
                  if self.stop not in ("l1only", "c1only"):
                      self.qT = P.sb(satt, "qT", [128, 6, TOK], BF16)
                      self.mqT = P.sb(satt, "mqT", [128, 2, TOK], BF16)
                      if self.stop == "s0":
                          P.barrier()
                          return self.nc
                      self.phase_a0(satt)
                      if self.stop == "s3":
                          return self.nc
                      self.setup_attn_consts(satt)
                      self.mKT0, self.mV0 = self.setup_mem(satt, 0, "0")
                      mixT = P.sb(satt, "mixT", [128, 8, TOK], BF16, n=8)
                      self.phase_b0T(mixT)
                      if self.stop == "s4":
                          return self.nc
                      self.phase_out(0, None, self.I("x"), h2T, mixT=mixT)
                      if self.stop == "s5":
                          return self.nc
                dst = self.out if self.stop in ("l0", "b0h0") else self.xf0
                if self.stop == "a0":
                    dst = None
                if dst is not None and self.stop not in ("l1only", "c1only"):
                    self.ffn(0, h2T, self.xa[0], dst)
                if self.stop in ("l0", "b0h0", "a0"):
                    P.barrier()
                    return self.nc
                with contextlib.ExitStack() as s1:
                    self.mqT = P.sb(s1, "mqT1", [128, 2, TOK], BF16)
                    self.mKT1, self.mV1 = self.setup_mem(s1, 1, "1")
                    QAT = P.sb(s1, "QAT", [96, NT, 4, 128], BF16, n=NT)
                    self.phase_c1(s1, QAT)
                    if self.stop in ("c1", "c1only"):
                        return self.nc
                    mix = P.sb(s1, "mix1", [128, NT, 1024], BF16, n=NT)
                    self.phase_d1(mix, QAT, self.mKT1, self.mV1)
                    self.phase_out(1, mix, self.xf0, h2T)
                self.ffn(1, h2T, self.xa[1], self.out)
            P.barrier(cc=True)
        return self.nc


def _prep_inputs(inputs):
    f = lambda a: np.ascontiguousarray(np.asarray(a, np.float32))
    x, mem = f(inputs["x"]), f(inputs["mem"])
    bcv = np.concatenate([
        f(inputs["rel_bias"]).reshape(-1), f(inputs["diff_qk_norm"]).reshape(-1), f(inputs["diff_lambda"]).reshape(-1),
        f(inputs["diff_out_norm"]).reshape(-1), f(inputs["gla_gate_b"]).reshape(-1), f(inputs["gla_out_norm"]).reshape(-1),
        f(inputs["mem_qk_norm"]).reshape(-1)])
    assert bcv.shape[0] == BC_N
    bc = np.ascontiguousarray(np.broadcast_to(bcv[None, :], (128, BC_N)))
    colg = np.stack([f(inputs["attn_norm"]), f(inputs["ffn_norm"]), f(inputs["mem_norm"])], 0)
    colg = colg.reshape(3, 2, 8, 128).transpose(3, 0, 1, 2).reshape(128, 48)
    colg = np.ascontiguousarray(np.concatenate([colg, f(inputs["diff_out_norm"]).reshape(128, 1)], 1))
    convw = np.ascontiguousarray(f(inputs["conv_w"]).reshape(2, 3, 44, 128).transpose(3, 0, 2, 1).reshape(128, 2 * 44 * 3))
    convb = np.ascontiguousarray(f(inputs["conv_b"]).reshape(2, 44, 128).transpose(2, 0, 1).reshape(128, 88))
    mle, mgt = _gla_consts()
    common = dict(
        bc=bc, colg=colg, convw=convw, convb=convb,
        idb=np.eye(128, dtype=np.float32).astype(ml_dtypes.bfloat16),
        t5m=np.ascontiguousarray(T5_MASKS.reshape(128, NSLOT * 128)),
        glam=np.ascontiguousarray(np.concatenate([mle, mgt], 1)),
        w_in_diff=f(inputs["w_in_diff"])[0], w_in_gla=f(inputs["w_in_gla"])[0], gla_gate_w=f(inputs["gla_gate_w"])[0],
    )
    for i in range(2):
        common[f"w_kv{i}"] = f(inputs["w_mem_kv"])[i]
        common[f"w_out{i}"] = f(inputs["w_out"])[i]
        common[f"w_up{i}"] = f(inputs["w_up"])[i]
        common[f"w_down{i}"] = f(inputs["w_down"])[i]
    in_maps = []
    for c in range(8):
        b, qc = c // 4, c % 4
        pc, mt = _core_consts(c)
        m = dict(common)
        m["x"] = np.ascontiguousarray(x[b, qc * TOK:(qc + 1) * TOK])
        m["mem"] = np.ascontiguousarray(mem[b])
        m["pc"] = pc
        m["mt"] = mt
        in_maps.append(m)
    return in_maps


def run(inputs, stop=None, dbg=(), trace=False):
    B = Builder(stop)
    for name, shape, dt in dbg:
        B.dbg_out(name, shape, dt)
    nc = B.build()
    in_maps = _prep_inputs(inputs)
    used = set(B.inputs.keys())
    in_maps = [{k: v for k, v in m.items() if k in used} for m in in_maps]
    res = run_bass_kernel_spmd(nc, in_maps, core_ids=list(range(8)))
    return res, B


def kernel(**inputs):
    res, _ = run(inputs)
    out = np.zeros((2, S_FULL, D), np.float32)
    for c in range(8):
        b, qc = c // 4, c % 4
        out[b, qc * TOK:(qc + 1) * TOK] = res.results[c]["out"]
    return out
```

```python
import contextlib
import math
import numpy as np
import ml_dtypes
import concourse.bass as bass
import concourse.mybir as mybir
from concourse.bass_utils import run_bass_kernel_spmd

F32 = mybir.dt.float32
BF16 = mybir.dt.bfloat16
AF = mybir.ActivationFunctionType
ALU = mybir.AluOpType
AX = mybir.AxisListType

NDS = 16


class Ev:
    __slots__ = ("key", "val")

    def __init__(self, key, val):
        self.key = key
        self.val = val


class Tk:
    __slots__ = ("w", "r", "name")

    def __init__(self, name=""):
        self.w = None
        self.r = {}
        self.name = name


class Buf:
    def __init__(self, t, n=1, name=""):
        self.t = t
        self.tk = Tk(name)
        self.tks = [Tk(f"{name}{i}") for i in range(n)] if n > 1 else [self.tk]

    def __getitem__(self, k):
        return self.t[k]


class View:
    def __init__(self, buf, c0, c1):
        self.buf, self.c0, self.c1, self.tk = buf, c0, c1, buf.tk

    def __getitem__(self, k):
        if not isinstance(k, tuple):
            return self.buf[:, self.c0:self.c1]
        p, c = k
        a = self.c0 + (c.start or 0)
        b = self.c0 + (c.stop if c.stop is not None else self.c1 - self.c0)
        return self.buf[p, a:b]


class Prog:
    def __init__(self, nc):
        self.nc = nc
        self.eng = {"pe": nc.tensor, "act": nc.scalar, "dve": nc.vector, "pool": nc.gpsimd, "sp": nc.sync}
        self.sems = {}
        self.ecnt = {}
        for e in self.eng:
            self.sems["e_" + e] = nc.alloc_semaphore("sem_" + e)
            self.ecnt[e] = 0
        self.dval = {}
        self.dnext = {}
        for q in ("sp", "pool", "act"):
            for i in range(NDS):
                self.sems[f"d_{q}_{i}"] = nc.alloc_semaphore(f"dsem_{q}_{i}")
                self.dval[f"d_{q}_{i}"] = 0
            self.dnext[q] = 0
        self.sems["cc"] = nc.alloc_semaphore("sem_cc")
        self.ccval = 0
        self.waited = {e: {} for e in self.eng}
        self.stack = contextlib.ExitStack()
        self.n_ins = 0

    def sb(self, stack, name, shape, dtype, n=1):
        self.n_alloc = getattr(self, "n_alloc", 0) + 1
        t = stack.enter_context(self.nc.sbuf_tensor(f"s{self.n_alloc}_{name}", list(shape), dtype))
        return Buf(t, n, name)

    def ps(self, stack, name, shape, dtype, n=1):
        self.n_alloc = getattr(self, "n_alloc", 0) + 1
        t = stack.enter_context(self.nc.psum_tensor(f"p{self.n_alloc}_{name}", list(shape), dtype))
        return Buf(t, n, name)

    def dram(self, name, shape, dtype, kind=None, n=1):
        if kind is None:
            t = self.nc.dram_tensor(name, list(shape), dtype)
        else:
            t = self.nc.dram_tensor(name, list(shape), dtype, kind=kind)
        return Buf(t, n, name)

    def _wait(self, e, ev):
        if ev is None:
            return
        if self.waited[e].get(ev.key, 0) >= ev.val:
            return
        self.eng[e].wait_ge(self.sems[ev.key], ev.val)
        self.waited[e][ev.key] = ev.val

    def _deps(self, e, reads, writes):
        own = "e_" + e
        for t in reads:
            if t.w is not None and not (e == "pe" and t.w.key == own):
                self._wait(e, t.w)
        for t in writes:
            if t.w is not None and t.w.key != own:
                self._wait(e, t.w)
            for k, v in t.r.items():
                if k != own:
                    self._wait(e, Ev(k, v))

    def _mark(self, ev, reads, writes):
        for t in reads:
            if t.r.get(ev.key, 0) < ev.val:
                t.r[ev.key] = ev.val
        for t in writes:
            t.w = ev
            t.r = {}

    @staticmethod
    def _tks(lst):
        out = []
        for x in lst:
            if isinstance(x, (Buf, View)):
                out.append(x.tk)
            else:
                out.append(x)
        return out

    def op(self, e, fn, reads=(), writes=()):
        reads = self._tks(reads)
        writes = self._tks(writes)
        self._deps(e, reads, writes)
        ins = fn(self.eng[e])
        self.ecnt[e] += 1
        ins.then_inc(self.sems["e_" + e], 1)
        self._mark(Ev("e_" + e, self.ecnt[e]), reads, writes)
        self.n_ins += 1
        return ins

    def dma(self, q, out, in_, reads=(), writes=()):
        reads = self._tks(reads)
        writes = self._tks(writes)
        i = self.dnext[q]
        self.dnext[q] = (i + 1) % NDS
        key = f"d_{q}_{i}"
        if self.dval[key] > 0:
            self._wait(q, Ev(key, self.dval[key]))
        self._deps(q, reads, writes)
        ins = self.eng[q].dma_start(out=out, in_=in_)
        self.dval[key] += 16
        ins.then_inc(self.sems[key], 16)
        self._mark(Ev(key, self.dval[key]), reads, writes)
        self.n_ins += 1
        return ins

    def allgather(self, src, dst, groups, src_tks=None):
        src_tks = [src.tk] if src_tks is None else list(src_tks)
        self._deps("pool", src_tks, [dst.tk])
        ins = self.nc.gpsimd.collective_compute(
            "AllGather", ALU.bypass, replica_groups=groups,
            ins=[src.t.ap().opt()], outs=[dst.t.ap().opt()])
        self.ccval += 1
        ins.then_inc(self.sems["cc"])
        self._mark(Ev("cc", self.ccval), src_tks, [dst.tk])

    def barrier(self, cc=False):
        for e in self.eng:
            for e2 in self.eng:
                if e2 != e and self.ecnt[e2] > 0:
                    self._wait(e, Ev("e_" + e2, self.ecnt[e2]))
            for k, v in self.dval.items():
                if v > 0:
                    self._wait(e, Ev(k, v))
            if cc and self.ccval > 0:
                self._wait(e, Ev("cc", self.ccval))


D = 1024
TOK = 2048
NT = 16
S_FULL = 8192
TW = 768
DFF = 2816
EPS = 1e-6
NEG = -30000.0


def _t5_bucket(rel):
    rel = np.asarray(rel, np.int32)
    half, max_exact = 16, 8
    ret = np.where(rel > 0, half, 0)
    n = np.abs(rel)
    nf = np.maximum(n, 1).astype(np.float32)
    large = max_exact + (np.log(nf / np.float32(max_exact)) / np.float32(math.log(128 / 8))
                         * np.float32(half - max_exact)).astype(np.int32)
    large = np.minimum(large, half - 1)
    return ret + np.where(n < max_exact, n, large)


def _t5_masks():
    kl = np.arange(128)[:, None]
    ql = np.arange(128)[None, :]
    bd = _t5_bucket(kl - ql)
    bc = _t5_bucket(kl - ql - 128)
    slots = []
    masks = []
    for b in sorted(set(bd.ravel().tolist())):
        if b == 15:
            continue
        slots.append(("d", b))
        masks.append(bd == b)
    for b in sorted(set(bc.ravel().tolist())):
        if b == 15:
            continue
        slots.append(("c", b))
        masks.append(bc == b)
    slots.append(("m", -1))
    masks.append((kl >= 64) & (ql < 64))
    m = np.stack(masks, axis=1).astype(np.float32)
    return slots, m.astype(ml_dtypes.bfloat16)


T5_SLOTS, T5_MASKS = _t5_masks()
NSLOT = len(T5_SLOTS)


def _core_consts(c):
    qc = c % 4
    pc = np.zeros((128, 16), np.float32)
    for j in range(4):
        pc[:, j] = 1.0 if j == qc else 0.0
        pc[:, 4 + j] = 1.0 if j == qc - 1 else 0.0
        pc[:, 8 + j] = 1.0 if j < qc else 0.0
        pc[:, 12 + j] = 0.0 if j < qc else 1.0
    mt = np.zeros((128, 4, 64), np.float32)
    for T in range(4):
        for j in range(64):
            if j > 16 * qc + 4 * T + 3:
                mt[:, T, j] = NEG
    return pc, mt.reshape(128, 256)


def _gla_consts():
    s = np.arange(128)[:, None]
    t = np.arange(128)[None, :]
    same = (s // 64) == (t // 64)
    mle = (same & (s <= t)).astype(np.float32)
    mgt = (same & (s > t)).astype(np.float32)
    lo = np.zeros((128, 128), np.float32)
    return mle, mgt


BC_OFF = {}
_o = 0
for _n, _s in (("rel_bias", 192), ("diff_qk", 128), ("diff_lam", 256), ("diff_on", 128),
               ("gla_gb", 384), ("gla_on", 192), ("mem_qk", 256)):
    BC_OFF[_n] = _o
    _o += _s
BC_N = _o
GROUPS = [[0, 1, 2, 3], [4, 5, 6, 7]]


class Builder:
    def __init__(self, stop=None):
        self.stop = stop
        self.nc = bass.Bass("TRN2", target_bir_lowering=False)
        self.P = Prog(self.nc)
        P = self.P
        self.inputs = {}

        self.spec = {
            "x": ([TOK, D], F32, NT), "mem": ([256, D], F32, 1), "bc": ([128, BC_N], F32, 1),
            "colg": ([128, 49], F32, 1), "convw": ([128, 2 * 44 * 3], F32, 1), "convb": ([128, 88], F32, 1),
            "idb": ([128, 128], BF16, 1), "t5m": ([128, NSLOT * 128], BF16, 1), "pc": ([128, 16], F32, 1),
            "mt": ([128, 256], F32, 1), "glam": ([128, 256], F32, 1),
            "w_in_diff": ([D, 2560], F32, 1), "w_in_gla": ([D, 2576], F32, 1), "gla_gate_w": ([16, 384], F32, 1),
        }
        for i in range(2):
            self.spec[f"w_kv{i}"] = ([D, 512], F32, 1)
            self.spec[f"w_out{i}"] = ([D, D], F32, 1)
            self.spec[f"w_up{i}"] = ([D, 2 * DFF], F32, 1)
            self.spec[f"w_down{i}"] = ([DFF, D], F32, 1)
        self.out = P.dram("out", [TOK, D], F32, kind="ExternalOutput", n=NT)
        self.xa = [P.dram(f"xa{i}", [TOK, D], F32, n=NT) for i in range(2)]
        self.xf0 = P.dram("xf0", [TOK, D], F32, n=NT)
        self.ksrc = [P.dram(f"ksrc{g}", [256, TOK], BF16, n=NT) for g in range(3)]
        self.kall = [P.dram(f"kall{g}", [4 * 256, TOK], BF16) for g in range(3)]
        self.vsrc = [P.dram(f"vsrc{g}", [TOK, 256], BF16, n=NT) for g in range(3)]
        self.vall = [P.dram(f"vall{g}", [4 * TOK, 256], BF16) for g in range(3)]
        self.olocd = P.dram("olocd", [TOK, 768], F32, n=NT)
        self.srd = P.dram("srd", [TOK, 768], BF16, n=NT)
        self.gsrc = P.dram("gsrc", [96, 772], F32)
        self.gall = P.dram("gall", [4 * 96, 772], F32)
        self.hsrc = [P.dram(f"hsrc{i}", [128, 16], BF16) for i in range(2)]
        self.hall = [P.dram(f"hall{i}", [4 * 128, 16], BF16) for i in range(2)]
        self.dbg = {}

    def I(self, name):
        if name not in self.inputs:
            shape, dt, n = self.spec[name]
            self.inputs[name] = self.P.dram(name, shape, dt, kind="ExternalInput", n=n)
        return self.inputs[name]

    def dbg_out(self, name, shape, dt=F32):
        b = self.P.dram("dbg_" + name, shape, dt, kind="ExternalOutput")
        self.dbg[name] = b
        return b

    def setup(self, st):
        P = self.P
        self.bc = P.sb(st, "bc", [128, BC_N], F32)
        self.colg = P.sb(st, "colg", [128, 49], F32)
        self.idb = P.sb(st, "idb", [128, 128], BF16)
        self.pc = P.sb(st, "pc", [128, 16], F32)
        self.epsb = P.sb(st, "epsb", [128, 1], F32)
        P.dma("sp", self.bc[:], self.I("bc")[:, :], [self.I("bc")], [self.bc])
        P.dma("sp", self.colg[:], self.I("colg")[:, :], [self.I("colg")], [self.colg])
        P.dma("sp", self.idb[:], self.I("idb")[:, :], [self.I("idb")], [self.idb])
        P.dma("sp", self.pc[:], self.I("pc")[:, :], [self.I("pc")], [self.pc])
        P.op("dve", lambda e: e.memset(self.epsb[:], EPS), [], [self.epsb])

    def gcol(self, kind, layer):
        o = (kind * 2 + layer) * 8
        return self.colg[:, o:o + 8]

    def norm_T(self, xt, gcol, hT_ap, hT_tk, W):
        P = self.P
        junk, ss, rt, xn, pt = W["junk"], W["ss"], W["rt"], W["xn"], W["pt"]
        P.op("act", lambda e: e.activation(out=junk[:], in_=xt[:], func=AF.Square, accum_out=ss[:, 0:1]),
             [xt], [junk, ss])
        P.op("act", lambda e: e.activation(out=rt[:, 0:1], in_=ss[:, 0:1], func=AF.Sqrt, scale=1.0 / D,
                                           bias=self.epsb[:, 0:1]), [ss, self.epsb], [rt])
        P.op("dve", lambda e: e.reciprocal(out=rt[:, 1:2], in_=rt[:, 0:1]), [rt], [rt])
        P.op("dve", lambda e: e.tensor_scalar(out=xn[:], in0=xt[:], scalar1=rt[:, 1:2], scalar2=None,
                                              op0=ALU.mult), [xt, rt], [xn])
        for kc in range(8):
            P.op("pe", lambda e: e.transpose(pt[:, kc * 128:(kc + 1) * 128], xn[:, kc * 128:(kc + 1) * 128],
                                             self.idb[:]), [xn, self.idb], [pt])
        P.op("dve", lambda e: e.tensor_tensor(
            out=hT_ap, in0=pt[:, :].rearrange("p (k t) -> p k t", k=8),
            in1=gcol.unsqueeze(2).to_broadcast([128, 8, 128]), op=ALU.mult), [pt, self.colg], [hT_tk])

    def load_w(self, dst, src, kchunks, cols, c0=0, wid=None, q="pool"):
        P = self.P
        wid = cols if wid is None else wid
        half = max(1, kchunks // 2)
        for k0 in range(0, kchunks, half):
            k1 = min(kchunks, k0 + half)
            P.dma(q, dst[:, k0:k1, 0:wid],
                  src.t.ap()[k0 * 128:k1 * 128, c0:c0 + wid].rearrange("(k p) c -> p k c", p=128),
                  [src], [dst.tks[k] for k in range(k0, k1)])

    def group_norm(self, pr, c0, ng, gain_ap, out_ap, out_tk, W, reads):
        P = self.P
        sq, ssg, t1 = W["sq"], W["ssg"], W["t1"]
        n = ng * 64
        src3 = pr[:, c0:c0 + n].rearrange("p (g d) -> p g d", d=64)
        P.op("pool", lambda e: e.tensor_tensor(out=sq[:, 0:n], in0=pr[:, c0:c0 + n], in1=pr[:, c0:c0 + n],
                                               op=ALU.mult), reads, [sq])
        P.op("dve", lambda e: e.tensor_reduce(out=ssg[:, 0:ng], in_=sq[:, 0:n].rearrange("p (g d) -> p g d", d=64),
                                              axis=AX.X, op=ALU.add), [sq], [ssg])
        P.op("act", lambda e: e.activation(out=ssg[:, 32:32 + ng], in_=ssg[:, 0:ng], func=AF.Sqrt, scale=1.0 / 64,
                                           bias=self.epsb[:, 0:1]), [ssg, self.epsb], [ssg])
        P.op("dve", lambda e: e.reciprocal(out=ssg[:, 64:64 + ng], in_=ssg[:, 32:32 + ng]), [ssg], [ssg])
        P.op("dve", lambda e: e.tensor_tensor(
            out=t1[:, 0:n].rearrange("p (g d) -> p g d", d=64), in0=src3,
            in1=ssg[:, 64:64 + ng].unsqueeze(2).to_broadcast([128, ng, 64]), op=ALU.mult), reads + [ssg], [t1])
        P.op("pool", lambda e: e.tensor_tensor(
            out=out_ap, in0=t1[:, 0:n].rearrange("p (g d) -> p g d", d=64),
            in1=gain_ap.unsqueeze(1).to_broadcast([128, ng, 64]), op=ALU.mult), [t1, self.gains], [out_tk])

    def phase_a0(self, st):
        P = self.P
        with contextlib.ExitStack() as s:
            Win = P.sb(s, "Win", [128, 8, 2560], BF16, n=8)
            self.load_w(Win, self.I("w_in_diff"), 8, 2560)
            xin = [P.sb(s, f"xin{i}", [128, D], F32) for i in range(2)]
            hT = [P.sb(s, f"hT{i}", [128, 8, 128], BF16) for i in range(2)]
            pr = [P.sb(s, f"pr{i}", [128, 2560], F32) for i in range(2)]
            W = dict(junk=P.sb(s, "junk", [128, D], F32), ss=P.sb(s, "ss", [128, 1], F32),
                     rt=P.sb(s, "rt", [128, 2], F32), xn=P.sb(s, "xn", [128, D], BF16),
                     pt=P.ps(s, "pt0", [128, 1024], BF16),
                     sq=P.sb(s, "sq", [128, 768], F32), ssg=P.sb(s, "ssg", [128, 96], F32),
                     t1=P.sb(s, "t1", [128, 768], F32))
            qn = P.sb(s, "qn", [128, 768], BF16)
            kn = P.sb(s, "kn", [128, 768], BF16)
            mn = P.sb(s, "mn", [128, 256], BF16)
            vb = [P.sb(s, f"vb{i}", [128, 768], BF16) for i in range(2)]
            ksb = [P.sb(s, f"ksb{i}", [128, 6, 128], BF16) for i in range(2)]
            pj = [P.ps(s, f"pj{i}", [128, 512], F32) for i in range(2)]
            pt1 = P.ps(s, "pt1", [128, 1024], BF16)
            pt2 = P.ps(s, "pt2", [128, 1024], BF16)
            g = self.gains
            kdst = [self.ksrc[g].t.ap().rearrange("(h p) t -> p h t", p=128) for g in range(3)]
            import os
            LVL = int(os.environ.get('K_A0LVL', '9'))
            for tt in range(int(os.environ.get('K_A0NT', NT))):
                xb, hb, prb = xin[tt % 2], hT[tt % 2], pr[tt % 2]
                P.dma("sp", xb[:], self.I("x")[tt * 128:(tt + 1) * 128, :], [self.I("x").tks[tt]], [xb])
                self.norm_T(xb, self.gcol(0, 0), hb[:, :, :], hb.tk, W)
                if LVL < 2:
                    continue
                for n in range(5):
                    pb = pj[n % 2]
                    for kc in range(8):
                        P.op("pe", lambda e: e.matmul(pb[:, :], hb[:, kc, :], Win[:, kc, n * 512:(n + 1) * 512],
                                                      start=(kc == 0), stop=(kc == 7)),
                             [hb, Win.tks[kc]], [pb])
                    if n % 2 == 0:
                        P.op("act", lambda e: e.activation(out=prb[:, n * 512:(n + 1) * 512], in_=pb[:, :],
                                                           func=AF.Copy), [pb], [prb])
                    else:
                        P.op("dve", lambda e: e.tensor_copy(out=prb[:, n * 512:(n + 1) * 512], in_=pb[:, :]),
                             [pb], [prb])
                if LVL < 3:
                    continue
                self.group_norm(prb, 0, 12, g[:, 0:64], qn[:, :].rearrange("p (g d) -> p g d", d=64), qn.tk, W, [prb])
                self.group_norm(prb, 768, 12, g[:, 64:128], kn[:, :].rearrange("p (g d) -> p g d", d=64), kn.tk, W, [prb])
                self.group_norm(prb, 2304, 4, g[:, 128:192], mn[:, :].rearrange("p (g d) -> p g d", d=64), mn.tk, W, [prb])
                if LVL < 4:
                    continue
                vbb = vb[tt % 2]
                P.op("act", lambda e: e.activation(out=vbb[:], in_=prb[:, 1536:2304], func=AF.Copy), [prb], [vbb])
                for g3 in range(3):
                    P.dma("sp", self.vsrc[g3][tt * 128:(tt + 1) * 128, :], vbb[:, g3 * 256:(g3 + 1) * 256], [vbb],
                          [self.vsrc[g3].tks[tt]])
                if LVL < 5:
                    continue
                for h in range(6):
                    P.op("pe", lambda e: e.transpose(pt1[:, h * 128:(h + 1) * 128], qn[:, h * 128:(h + 1) * 128],
                                                     self.idb[:]), [qn, self.idb], [pt1])
                for h in range(2):
                    P.op("pe", lambda e: e.transpose(pt1[:, (6 + h) * 128:(7 + h) * 128], mn[:, h * 128:(h + 1) * 128],
                                                     self.idb[:]), [mn, self.idb], [pt1])
                for h in range(6):
                    P.op("pe", lambda e: e.transpose(pt2[:, h * 128:(h + 1) * 128], kn[:, h * 128:(h + 1) * 128],
                                                     self.idb[:]), [kn, self.idb], [pt2])
                if LVL == 5 and os.environ.get("K_SUB") == "a":
                    continue
                P.op("act", lambda e: e.activation(
                    out=self.qT[:, :, tt * 128:(tt + 1) * 128],
                    in_=pt1[:, 0:768].rearrange("p (h t) -> p h t", h=6), func=AF.Copy), [pt1], [self.qT])
                if LVL == 5 and os.environ.get("K_SUB") == "b":
                    continue
                if not (LVL == 5 and os.environ.get("K_SUB") == "c"):
                    P.op("act", lambda e: e.activation(
                        out=self.mqT[:, :, tt * 128:(tt + 1) * 128],
                        in_=pt1[:, 768:1024].rearrange("p (h t) -> p h t", h=2), func=AF.Copy), [pt1], [self.mqT])
                if LVL == 5 and os.environ.get("K_SUB") == "d":
                    continue
                kb = ksb[tt % 2]
                P.op("dve", lambda e: e.tensor_copy(out=kb[:, :, :], in_=pt2[:, 0:768].rearrange("p (h t) -> p h t", h=6)),
                     [pt2], [kb])
                if LVL < 6:
                    continue
                for g3 in range(3):
                    P.dma("sp", kdst[g3][:, :, tt * 128:(tt + 1) * 128], kb[:, 2 * g3:2 * g3 + 2, :], [kb],
                          [self.ksrc[g3].tks[tt]])
            import os
            if not os.environ.get("K_NOCC"):
                for g3 in range(3):
                    P.allgather(self.ksrc[g3], self.kall[g3], GROUPS, src_tks=self.ksrc[g3].tks)
                    P.allgather(self.vsrc[g3], self.vall[g3], GROUPS, src_tks=self.vsrc[g3].tks)
            P.barrier()

    def setup_gains(self, st):
        P = self.P
        self.gains = P.sb(st, "gains", [128, 704], F32)
        g, bc = self.gains, self.bc
        oq, om, od, og = BC_OFF["diff_qk"], BC_OFF["mem_qk"], BC_OFF["diff_on"], BC_OFF["gla_on"]
        P.op("dve", lambda e: e.tensor_scalar(out=g[:, 0:64], in0=bc[:, oq:oq + 64], scalar1=0.125, scalar2=None,
                                              op0=ALU.mult), [bc], [g])
        P.op("dve", lambda e: e.tensor_copy(out=g[:, 64:128], in_=bc[:, oq + 64:oq + 128]), [bc], [g])
        for L in range(2):
            P.op("dve", lambda e: e.tensor_scalar(out=g[:, 128 + 128 * L:192 + 128 * L],
                                                  in0=bc[:, om + 128 * L:om + 128 * L + 64], scalar1=0.125,
                                                  scalar2=None, op0=ALU.mult), [bc], [g])
            P.op("dve", lambda e: e.tensor_copy(out=g[:, 192 + 128 * L:256 + 128 * L],
                                                in_=bc[:, om + 128 * L + 64:om + 128 * L + 128]), [bc], [g])
        P.op("dve", lambda e: e.tensor_scalar(out=g[:, 384:512], in0=bc[:, od:od + 128], scalar1=0.8, scalar2=None,
                                              op0=ALU.mult), [bc], [g])
        P.op("dve", lambda e: e.tensor_copy(out=g[:, 512:704], in_=bc[:, og:og + 192]), [bc], [g])

    def setup_attn_consts(self, st):
        P = self.P
        bc = self.bc
        self.lam = P.sb(st, "lam", [128, 8], F32)
        self.Dg = P.sb(st, "Dg", [128, 6, 128], F32)
        self.Cn = P.sb(st, "Cn", [128, 6, 128], F32)
        self.maskw = P.sb(st, "maskw", [128, 384], F32)
        self.btab = P.sb(st, "btab", [128, 6, 256], F32)
        lam = self.lam
        ol = BC_OFF["diff_lam"]
        with contextlib.ExitStack() as s:
            tmp = P.sb(s, "lamtmp", [128, 128], F32)
            P.op("dve", lambda e: e.tensor_tensor(out=tmp[:, 0:64], in0=bc[:, ol:ol + 64], in1=bc[:, ol + 64:ol + 128],
                                                  op=ALU.mult), [bc], [tmp])
            P.op("dve", lambda e: e.tensor_tensor(out=tmp[:, 64:128], in0=bc[:, ol + 128:ol + 192],
                                                  in1=bc[:, ol + 192:ol + 256], op=ALU.mult), [bc, tmp], [tmp])
            P.op("dve", lambda e: e.tensor_reduce(out=lam[:, 0:2], in_=tmp[:, :].rearrange("p (a d) -> p a d", a=2),
                                                  axis=AX.X, op=ALU.add), [tmp], [lam])
            P.op("act", lambda e: e.activation(out=lam[:, 2:4], in_=lam[:, 0:2], func=AF.Exp), [lam], [lam])
            P.op("dve", lambda e: e.tensor_tensor(out=lam[:, 5:6], in0=lam[:, 3:4], in1=lam[:, 2:3], op=ALU.subtract),
                 [lam], [lam])
            P.op("dve", lambda e: e.tensor_scalar(out=lam[:, 4:5], in0=lam[:, 5:6], scalar1=-0.2, scalar2=None,
                                                  op0=ALU.add), [lam], [lam])
            dev = P.sb(s, "dev", [128, 192], F32)
            orb = BC_OFF["rel_bias"]
            P.op("dve", lambda e: e.tensor_tensor(
                out=dev[:, :].rearrange("p (b h) -> p b h", h=6), in0=bc[:, orb:orb + 192].rearrange("p (b h) -> p b h", h=6),
                in1=bc[:, orb + 90:orb + 96].unsqueeze(1).to_broadcast([128, 32, 6]), op=ALU.subtract), [bc], [dev])
            t5m = P.sb(s, "t5m", [128, NSLOT * 128], BF16)
            P.dma("sp", t5m[:], self.I("t5m")[:, :], [self.I("t5m")], [t5m])
            mt = P.sb(s, "mt", [128, 256], F32)
            P.dma("sp", mt[:], self.I("mt")[:, :], [self.I("mt")], [mt])
            P.op("pool", lambda e: e.memset(self.maskw[:], NEG), [], [self.maskw])
            for h in range(6):
                P.op("dve", lambda e: e.tensor_scalar(out=self.btab[:, h, :], in0=mt[:], scalar1=bc[:, orb + 90 + h:orb + 91 + h],
                                                      scalar2=None, op0=ALU.add), [mt, bc], [self.btab])
                msl = NSLOT - 1
                P.op("dve", lambda e: e.tensor_scalar(out=self.Dg[:, h, :], in0=t5m[:, msl * 128:(msl + 1) * 128],
                                                      scalar1=NEG, scalar2=None, op0=ALU.mult), [t5m], [self.Dg])
                P.op("dve", lambda e: e.memset(self.Cn[:, h, :], 0.0), [], [self.Cn])
                for si, (kind, b) in enumerate(T5_SLOTS):
                    if kind == "m":
                        continue
                    dst = self.Dg if kind == "d" else self.Cn
                    P.op("dve", lambda e: e.scalar_tensor_tensor(
                        out=dst[:, h, :], in0=t5m[:, si * 128:(si + 1) * 128], scalar=dev[:, b * 6 + h:b * 6 + h + 1],
                        in1=dst[:, h, :], op0=ALU.mult, op1=ALU.add), [t5m, dev, dst], [dst])
            P.barrier()

    def setup_mem(self, st, L, name):
        P = self.P
        mKT = P.sb(st, f"mKT{name}", [128, 2, 256], BF16)
        mV = P.sb(st, f"mV{name}", [128, 2, 4, 72], BF16)
        with contextlib.ExitStack() as s:
            Wkv = P.sb(s, "Wkv", [128, 8, 512], BF16, n=8)
            self.load_w(Wkv, self.I(f"w_kv{L}"), 8, 512)
            xin = P.sb(s, "mxin", [128, D], F32)
            hT = P.sb(s, "mhT", [128, 8, 128], BF16)
            W = dict(junk=P.sb(s, "mjunk", [128, D], F32), ss=P.sb(s, "mss", [128, 1], F32),
                     rt=P.sb(s, "mrt", [128, 2], F32), xn=P.sb(s, "mxn", [128, D], BF16),
                     pt=P.ps(s, "mpt0", [128, 1024], BF16),
                     sq=P.sb(s, "msq", [128, 768], F32), ssg=P.sb(s, "mssg", [128, 96], F32),
                     t1=P.sb(s, "mt1", [128, 768], F32))
            pr = P.sb(s, "mpr", [128, 512], F32)
            kn = P.sb(s, "mkn", [128, 256], BF16)
            pj = P.ps(s, "mpj", [128, 512], F32)
            ptk = P.ps(s, "mptk", [128, 1024], BF16)
            P.op("pool", lambda e: e.memset(mV[:, :, :, 64:72], 1.0), [], [mV])
            for blk in range(2):
                P.dma("sp", xin[:], self.I("mem")[blk * 128:(blk + 1) * 128, :], [self.I("mem")], [xin])
                self.norm_T(xin, self.gcol(2, L), hT[:, :, :], hT.tk, W)
                for kc in range(8):
                    P.op("pe", lambda e: e.matmul(pj[:, :], hT[:, kc, :], Wkv[:, kc, :], start=(kc == 0), stop=(kc == 7)),
                         [hT, Wkv.tks[kc]], [pj])
                P.op("act", lambda e: e.activation(out=pr[:], in_=pj[:, :], func=AF.Copy), [pj], [pr])
                self.group_norm(pr, 0, 4, self.gains[:, 192 + 128 * L:256 + 128 * L],
                                kn[:, :].rearrange("p (g d) -> p g d", d=64), kn.tk, W, [pr])
                P.op("dve", lambda e: e.tensor_copy(out=mV[:, blk, :, 0:64],
                                                    in_=pr[:, 256:512].rearrange("p (h d) -> p h d", h=4)), [pr], [mV])
                for pp in range(2):
                    P.op("pe", lambda e: e.transpose(ptk[:, pp * 128:(pp + 1) * 128], kn[:, pp * 128:(pp + 1) * 128],
                                                     self.idb[:]), [kn, self.idb], [ptk])
                P.op("dve", lambda e: e.tensor_copy(out=mKT[:, :, blk * 128:(blk + 1) * 128],
                                                    in_=ptk[:, 0:256].rearrange("p (a t) -> p a t", a=2)), [ptk], [mKT])
            P.barrier()
        return mKT, mV

    def attn_finalize_diff(self, Ob, i, h, tt, mix, F, k):
        P = self.P
        a0, a1 = 2 * i, 2 * i + 1
        b0, b1 = Ob[a0 // 3], Ob[a1 // 3]
        c0, c1 = (a0 % 3) * 129, (a1 % 3) * 129
        rr, t0, o, junk = F["rr"][k], F["t0"][k], F["o"][k], F["junk"][k]
        P.op("dve", lambda e: e.reciprocal(out=rr[:, 0:1], in_=b0[:, c0 + 128:c0 + 129]), [b0], [rr])
        P.op("dve", lambda e: e.reciprocal(out=rr[:, 1:2], in_=b1[:, c1 + 128:c1 + 129]), [b1], [rr])
        P.op("dve", lambda e: e.tensor_tensor(out=rr[:, 2:3], in0=rr[:, 1:2], in1=self.lam[:, 4:5], op=ALU.mult),
             [rr, self.lam], [rr])
        P.op("dve", lambda e: e.tensor_scalar(out=t0[:], in0=b0[:, c0:c0 + 128], scalar1=rr[:, 0:1], scalar2=None,
                                              op0=ALU.mult), [b0, rr], [t0])
        P.op("dve", lambda e: e.scalar_tensor_tensor(out=o[:], in0=b1[:, c1:c1 + 128], scalar=rr[:, 2:3], in1=t0[:],
                                                     op0=ALU.mult, op1=ALU.add), [b1, rr, t0], [o])
        P.op("act", lambda e: e.activation(out=junk[:], in_=o[:], func=AF.Square, accum_out=rr[:, 3:4]), [o], [junk, rr])
        P.op("act", lambda e: e.activation(out=rr[:, 4:5], in_=rr[:, 3:4], func=AF.Sqrt, scale=1.0 / 128,
                                           bias=self.epsb[:, 0:1]), [rr, self.epsb], [rr])
        P.op("dve", lambda e: e.reciprocal(out=rr[:, 5:6], in_=rr[:, 4:5]), [rr], [rr])
        P.op("dve", lambda e: e.scalar_tensor_tensor(out=mix[:, tt, h * 128:(h + 1) * 128], in0=o[:], scalar=rr[:, 5:6],
                                                     in1=self.gains[:, 384:512], op0=ALU.mult, op1=ALU.mult),
             [o, rr, self.gains], [mix.tks[tt]])

    def cross_attn(self, mKT, mV, mqT, mix, Sb, Ob, PT, F):
        P = self.P
        cnt = 0
        for hd in range(4):
            pp, hp = hd // 2, hd % 2
            r0, r1 = hp * 64, hp * 64 + 64
            for T in range(4):
                for blk in range(2):
                    sb, pt = Sb[cnt % 4], PT[cnt % 4]
                    cnt += 1
                    P.op("pe", lambda e: e.matmul(sb[:, :], mKT[r0:r1, pp, blk * 128:(blk + 1) * 128],
                                                  mqT[r0:r1, pp, T * 512:(T + 1) * 512], start=True, stop=True),
                         [mKT, mqT], [sb])
                    P.op("act", lambda e: e.activation(out=pt[:], in_=sb[:, :], func=AF.Exp), [sb], [pt])
                    for i in range(4):
                        ob = Ob[i // 3]
                        c = (i % 3) * 129
                        P.op("pe", lambda e: e.matmul(ob[:, c:c + 65], pt[:, i * 128:(i + 1) * 128], mV[:, blk, hd, 0:65],
                                                      start=(blk == 0 and i % 3 == 0), stop=(blk == 1),
                                                      skip_group_check=True), [pt, mV], [ob])
                for i in range(4):
                    ob = Ob[i // 3]
                    c = (i % 3) * 129
                    tt = T * 4 + i
                    rr = F["rr"][i % 2]
                    P.op("dve", lambda e: e.reciprocal(out=rr[:, 0:1], in_=ob[:, c + 64:c + 65]), [ob], [rr])
                    P.op("dve", lambda e: e.tensor_scalar(out=mix[:, tt, 768 + hd * 64:832 + hd * 64], in0=ob[:, c:c + 64],
                                                          scalar1=rr[:, 0:1], scalar2=None, op0=ALU.mult),
                         [ob, rr], [mix.tks[tt]])

    def out_proj(self, L, mix, x_src, xa_dst, h2T, Wn, s, featT=None):
        P = self.P
        Wout = P.sb(s, "Wout", [128, 8, 1024], BF16, n=8)
        self.load_w(Wout, self.I(f"w_out{L}"), 8, 1024)
        mixT = [P.sb(s, f"mixT{i}", [128, 8, 128], BF16) for i in range(2)]
        xin = [P.sb(s, f"oxin{i}", [128, D], F32) for i in range(2)]
        x1 = [P.sb(s, f"ox1{i}", [128, D], F32) for i in range(2)]
        ptm = P.ps(s, "optm", [128, 1024], BF16)
        pj = [P.ps(s, f"opj{i}", [128, 512], F32) for i in range(2)]
        for tt in range(NT):
            mT, xb, xo = mixT[tt % 2], xin[tt % 2], x1[tt % 2]
            P.dma("sp", xb[:], x_src[tt * 128:(tt + 1) * 128, :], [x_src.tks[tt]], [xb])
            if featT is None:
                for kc in range(8):
                    P.op("pe", lambda e: e.transpose(ptm[:, kc * 128:(kc + 1) * 128], mix[:, tt, kc * 128:(kc + 1) * 128],
                                                     self.idb[:]), [mix.tks[tt], self.idb], [ptm])
                P.op("act", lambda e: e.activation(out=mT[:, :, :], in_=ptm[:, :].rearrange("p (k t) -> p k t", k=8),
                                                   func=AF.Copy), [ptm], [mT])
            for half in range(2):
                for kc in range(8):
                    if featT is None:
                        lhs, ltk = mT[:, kc, :], mT.tk
                    else:
                        lhs, ltk = featT[:, kc, tt * 128:(tt + 1) * 128], featT.tks[kc]
                    P.op("pe", lambda e: e.matmul(pj[half][:, :], lhs, Wout[:, kc, half * 512:(half + 1) * 512],
                                                  start=(kc == 0), stop=(kc == 7)), [ltk, Wout.tks[kc]], [pj[half]])
                P.op("dve", lambda e: e.tensor_tensor(out=xo[:, half * 512:(half + 1) * 512], in0=pj[half][:, :],
                                                      in1=xb[:, half * 512:(half + 1) * 512], op=ALU.add),
                     [pj[half], xb], [xo])
            P.dma("sp", xa_dst[tt * 128:(tt + 1) * 128, :], xo[:], [xo], [xa_dst.tks[tt]])
            self.norm_T(xo, self.gcol(1, L), h2T[:, :, 2 + tt * 128:2 + (tt + 1) * 128], h2T.tk, Wn)
        hs = P.sb(s, "hs", [128, 16], BF16)
        hb = P.sb(s, "hb", [128, 4, 16], BF16)
        hacc = P.sb(s, "hacc", [128, 16], F32)
        P.op("dve", lambda e: e.tensor_copy(out=hs[:, :].rearrange("p (k t) -> p k t", k=8), in_=h2T[:, :, 2048:2050]),
             [h2T], [hs])
        P.dma("sp", self.hsrc[L][:, :], hs[:], [hs], [self.hsrc[L]])
        P.allgather(self.hsrc[L], self.hall[L], GROUPS)
        P.dma("sp", hb[:, :, :], self.hall[L].t.ap().rearrange("(r p) c -> p r c", p=128), [self.hall[L]], [hb])
        P.op("dve", lambda e: e.tensor_scalar(out=hacc[:], in0=hb[:, 0, :], scalar1=self.pc[:, 4:5], scalar2=None,
                                              op0=ALU.mult), [hb, self.pc], [hacc])
        for r in range(1, 4):
            P.op("dve", lambda e: e.scalar_tensor_tensor(out=hacc[:], in0=hb[:, r, :], scalar=self.pc[:, 4 + r:5 + r],
                                                         in1=hacc[:], op0=ALU.mult, op1=ALU.add), [hb, self.pc, hacc], [hacc])
        P.op("dve", lambda e: e.tensor_copy(out=h2T[:, :, 0:2], in_=hacc[:, :].rearrange("p (k t) -> p k t", k=8)),
             [hacc], [h2T])

    def phase_b0(self, mix):
        P = self.P
        if True:
            with contextlib.ExitStack() as s2:
                KT = [P.sb(s2, f"KT{i}", [128, S_FULL], BF16) for i in range(2)]
                Vh = [P.sb(s2, f"Vh{i}", [128, 64, 136], BF16) for i in range(2)]
                PT2 = [P.sb(s2, f"PT{i}", [128, 1024], BF16) for i in range(2)]
                Sb2 = [P.ps(s2, f"Sb{i}", [128, 1024], F32) for i in range(2)]
                PT = [View(PT2[i // 2], (i % 2) * 512, (i % 2 + 1) * 512) for i in range(4)]
                Sb = [View(Sb2[i // 2], (i % 2) * 512, (i % 2 + 1) * 512) for i in range(4)]
                Ob = [P.ps(s2, f"Ob{i}", [128, 512], F32) for i in range(3)]
                F = dict(rr=[P.sb(s2, f"frr{i}", [128, 8], F32) for i in range(2)],
                         t0=[P.sb(s2, f"ft0{i}", [128, 128], F32) for i in range(2)],
                         o=[P.sb(s2, f"fo{i}", [128, 128], F32) for i in range(2)],
                         junk=[P.sb(s2, f"fj{i}", [128, 128], F32) for i in range(2)])
                for v in Vh:
                    P.op("pool", lambda e: e.memset(v[:, :, 128:136], 1.0), [], [v])
                qT, pc = self.qT, self.pc
                for h in range(6):
                    kt, vh = KT[h % 2], Vh[h % 2]
                    for r in range(4):
                        P.dma("sp", kt[:, r * TOK:(r + 1) * TOK],
                              self.kall[h // 2][r * 256 + (h % 2) * 128:r * 256 + (h % 2 + 1) * 128, :],
                              [self.kall[h // 2]], [kt])
                    for r in range(4):
                        P.dma("sp", vh[:, r * 16:(r + 1) * 16, 0:128],
                              self.vall[h // 2].t.ap()[r * TOK:(r + 1) * TOK, (h % 2) * 128:(h % 2 + 1) * 128].rearrange(
                                  "(b p) c -> p b c", p=128),
                              [self.vall[h // 2]], [vh])
                    for T in range(4):
                        jmax = min(63, 51 + 4 * T)

                        def emit_S(j):
                            sb, pt = Sb2[j % 2], PT2[j % 2]
                            for m in range(2):
                                mo = m * 512
                                P.op("pe", lambda e: e.matmul(sb[:, mo:mo + 512], kt[m * 64:(m + 1) * 64, j * 128:(j + 1) * 128],
                                                              qT[m * 64:(m + 1) * 64, h, T * 512:(T + 1) * 512],
                                                              start=True, stop=True), [kt, qT], [sb])
                            for qcp in range(4):
                                off = j - 4 * T - 16 * qcp
                                if off < -1 or off > 3:
                                    continue
                                sc = pc[:, qcp:qcp + 1]
                                for m in range(2):
                                    mo = m * 512
                                    if off >= 1:
                                        P.op("dve", lambda e: e.scalar_tensor_tensor(
                                            out=sb[:, mo:mo + off * 128], in0=self.maskw[:, 0:off * 128], scalar=sc,
                                            in1=sb[:, mo:mo + off * 128], op0=ALU.mult, op1=ALU.add),
                                            [sb, self.maskw, pc], [sb])
                                    if 0 <= off <= 3:
                                        P.op("dve", lambda e: e.scalar_tensor_tensor(
                                            out=sb[:, mo + off * 128:mo + (off + 1) * 128], in0=self.Dg[:, h, :], scalar=sc,
                                            in1=sb[:, mo + off * 128:mo + (off + 1) * 128], op0=ALU.mult, op1=ALU.add),
                                            [sb, self.Dg, pc], [sb])
                                    if 0 <= off + 1 <= 3:
                                        i = off + 1
                                        P.op("dve", lambda e: e.scalar_tensor_tensor(
                                            out=sb[:, mo + i * 128:mo + (i + 1) * 128], in0=self.Cn[:, h, :], scalar=sc,
                                            in1=sb[:, mo + i * 128:mo + (i + 1) * 128], op0=ALU.mult, op1=ALU.add),
                                            [sb, self.Cn, pc], [sb])
                            P.op("act", lambda e: e.activation(out=pt[:], in_=sb[:, :], func=AF.Exp,
                                                               bias=self.btab[:, h, T * 64 + j:T * 64 + j + 1]),
                                 [sb, self.btab], [pt])

                        def emit_PV(j):
                            started = set()
                            for m in range(2):
                                pt = View(PT2[j % 2], m * 512, (m + 1) * 512)
                                for i in range(4):
                                    a = 2 * i + m
                                    ob = Ob[a // 3]
                                    c = (a % 3) * 129
                                    st_ = (j == 0) and (a // 3 not in started)
                                    started.add(a // 3)
                                    P.op("pe", lambda e: e.matmul(ob[:, c:c + 129], pt[:, i * 128:(i + 1) * 128],
                                                                  vh[:, j, 0:129], start=st_, stop=(j == jmax),
                                                                  skip_group_check=True),
                                         [pt, vh], [ob])

                        emit_S(0)
                        for j in range(jmax + 1):
                            if j + 1 <= jmax:
                                emit_S(j + 1)
                            emit_PV(j)
                        for i in range(4):
                            self.attn_finalize_diff(Ob, i, h, T * 4 + i, mix, F, i % 2)
                    if self.stop == "b0h0":
                        break
                self.cross_attn(self.mKT0, self.mV0, self.mqT, mix, Sb, Ob, PT, F)
                P.barrier()

    def phase_b0T(self, mixT):
        P = self.P
        with contextlib.ExitStack() as s2:
            KT = [P.sb(s2, f"KT{i}", [128, S_FULL], BF16) for i in range(2)]
            Vh = [P.sb(s2, f"Vh{i}", [128, 64, 128], BF16) for i in range(2)]
            PT2 = [P.sb(s2, f"PT{i}", [128, 1024], BF16) for i in range(3)]
            Sb2 = [P.ps(s2, f"Sb{i}", [128, 1024], F32) for i in range(3)]
            OT = [P.ps(s2, f"OT{i}", [128, 512], F32) for i in range(2)]
            accs = [P.sb(s2, f"pacc{i}", [128, 512], F32) for i in range(2)]
            lsb = Sb2[0]
            LS = [View(Sb2[2], 0, 512)]
            onesb = P.sb(s2, "onesb", [128, 128], BF16)
            onesf = P.sb(s2, "onesf", [128, 128], F32)
            onesp = P.sb(s2, "onesp", [128, 2, 128], BF16)
            mVp = P.sb(s2, "mVp", [128, 2, 4, 128], BF16)
            eps2 = P.sb(s2, "eps2", [128, 1], F32)
            gOc = self.colg[:, 48:49]
            tmp = [P.sb(s2, f"ftmp{i}", [128, 512], F32) for i in range(4)]
            P.op("pool", lambda e: e.memset(onesb[:], 1.0), [], [onesb])
            P.op("pool", lambda e: e.memset(onesf[:], 1.0), [], [onesf])
            P.op("pool", lambda e: e.memset(onesp[:], 0.0), [], [onesp])
            P.op("pool", lambda e: e.memset(onesp[:, 0, 0:64], 1.0), [onesp], [onesp])
            P.op("pool", lambda e: e.memset(onesp[:, 1, 64:128], 1.0), [onesp], [onesp])
            P.op("pool", lambda e: e.memset(mVp[:], 0.0), [], [mVp])
            P.op("dve", lambda e: e.memset(eps2[:], EPS / 0.64), [], [eps2])
            mV, mKT, mqT = self.mV0, self.mKT0, self.mqT
            for blk in range(2):
                for hd in range(4):
                    hp = hd % 2
                    P.op("pool", lambda e: e.tensor_copy(out=mVp[:, blk, hd, hp * 64:(hp + 1) * 64], in_=mV[:, blk, hd, 0:64]),
                         [mV, mVp], [mVp])
            qT, pc = self.qT, self.pc
            for h in range(6):
                kt, vh = KT[h % 2], Vh[h % 2]
                for r in range(4):
                    P.dma("sp", kt[:, r * TOK:(r + 1) * TOK],
                          self.kall[h // 2][r * 256 + (h % 2) * 128:r * 256 + (h % 2 + 1) * 128, :],
                          [self.kall[h // 2]], [kt])
                for r in range(4):
                    P.dma("sp", vh[:, r * 16:(r + 1) * 16, :],
                          self.vall[h // 2].t.ap()[r * TOK:(r + 1) * TOK, (h % 2) * 128:(h % 2 + 1) * 128].rearrange(
                              "(b p) c -> p b c", p=128),
                          [self.vall[h // 2]], [vh])
                for T in range(4):
                    jmax = min(63, 51 + 4 * T)

                    def emit_S(j):
                        sb, pt = Sb2[j % 3], PT2[j % 3]
                        for m in range(2):
                            mo = m * 512
                            P.op("pe", lambda e: e.matmul(sb[:, mo:mo + 512], kt[m * 64:(m + 1) * 64, j * 128:(j + 1) * 128],
                                                          qT[m * 64:(m + 1) * 64, h, T * 512:(T + 1) * 512],
                                                          start=True, stop=True), [kt, qT], [sb])
                        for qcp in range(4):
                            off = j - 4 * T - 16 * qcp
                            if off < -1 or off > 3:
                                continue
                            sc = pc[:, qcp:qcp + 1]
                            for m in range(2):
                                mo = m * 512
                                if off >= 1:
                                    P.op("dve", lambda e: e.scalar_tensor_tensor(
                                        out=sb[:, mo:mo + off * 128], in0=self.maskw[:, 0:off * 128], scalar=sc,
                                        in1=sb[:, mo:mo + off * 128], op0=ALU.mult, op1=ALU.add),
                                        [sb, self.maskw, pc], [sb])
                                if 0 <= off <= 3:
                                    P.op("dve", lambda e: e.scalar_tensor_tensor(
                                        out=sb[:, mo + off * 128:mo + (off + 1) * 128], in0=self.Dg[:, h, :], scalar=sc,
                                        in1=sb[:, mo + off * 128:mo + (off + 1) * 128], op0=ALU.mult, op1=ALU.add),
                                        [sb, self.Dg, pc], [sb])
                                if 0 <= off + 1 <= 3:
                                    i = off + 1
                                    P.op("dve", lambda e: e.scalar_tensor_tensor(
                                        out=sb[:, mo + i * 128:mo + (i + 1) * 128], in0=self.Cn[:, h, :], scalar=sc,
                                        in1=sb[:, mo + i * 128:mo + (i + 1) * 128], op0=ALU.mult, op1=ALU.add),
                                        [sb, self.Cn, pc], [sb])
                        P.op("act", lambda e: e.activation(out=pt[:], in_=sb[:, :], func=AF.Exp,
                                                           bias=self.btab[:, h, T * 64 + j:T * 64 + j + 1]),
                             [sb, self.btab], [pt])

                    def emit_PV(j):
                        pt = PT2[j % 3]
                        for m in range(2):
                            mo = m * 512
                            P.op("pe", lambda e: e.matmul(OT[m][:, :], vh[:, j, :], pt[:, mo:mo + 512],
                                                          start=(j == 0), stop=(j == jmax)), [pt, vh], [OT[m]])
                        for m, eng in ((0, "dve"), (1, "pool")):
                            mo = m * 512
                            acc = accs[m]
                            if j == 0:
                                P.op(eng, lambda e: e.tensor_copy(out=acc[:], in_=pt[:, mo:mo + 512]), [pt], [acc])
                            else:
                                P.op(eng, lambda e: e.tensor_tensor(out=acc[:], in0=acc[:], in1=pt[:, mo:mo + 512], op=ALU.add),
                                     [pt, acc], [acc])

                    emit_S(0)
                    if jmax >= 1:
                        emit_S(1)
                    for j in range(jmax + 1):
                        if j + 2 <= jmax:
                            emit_S(j + 2)
                        emit_PV(j)
                    for m in range(2):
                        mo = m * 512
                        P.op("pe", lambda e: e.matmul(lsb[:, mo:mo + 512], onesf[:, :], accs[m][:, :], start=True, stop=True),
                             [onesf, accs[m]], [lsb])
                    r0, r1, t0, t1 = tmp
                    o, sq, rt_, rs_ = r0, r1, t0, t1
                    P.op("dve", lambda e: e.reciprocal(out=r0[:], in_=lsb[:, 0:512]), [lsb], [r0])
                    P.op("dve", lambda e: e.reciprocal(out=r1[:], in_=lsb[:, 512:1024]), [lsb], [r1])
                    P.op("dve", lambda e: e.tensor_tensor(out=t0[:], in0=OT[0][:, :], in1=r0[:], op=ALU.mult), [OT[0], r0], [t0])
                    P.op("dve", lambda e: e.tensor_tensor(out=t1[:], in0=OT[1][:, :], in1=r1[:], op=ALU.mult), [OT[1], r1], [t1])
                    P.op("dve", lambda e: e.scalar_tensor_tensor(out=o[:], in0=t1[:], scalar=self.lam[:, 4:5], in1=t0[:],
                                                                 op0=ALU.mult, op1=ALU.add), [t1, t0, self.lam], [o])
                    P.op("pool", lambda e: e.tensor_tensor(out=sq[:], in0=o[:], in1=o[:], op=ALU.mult), [o], [sq])
                    P.op("pe", lambda e: e.matmul(lsb[:, 0:512], onesf[:, :], sq[:, :], start=True, stop=True), [onesf, sq], [lsb])
                    P.op("act", lambda e: e.activation(out=rt_[:], in_=lsb[:, 0:512], func=AF.Sqrt, scale=1.0 / (128 * 0.64),
                                                       bias=eps2[:, 0:1]), [lsb, eps2], [rt_])
                    P.op("dve", lambda e: e.reciprocal(out=rs_[:], in_=rt_[:]), [rt_], [rs_])
                    P.op("dve", lambda e: e.scalar_tensor_tensor(out=mixT[:, h, T * 512:(T + 1) * 512], in0=o[:], scalar=gOc,
                                                                 in1=rs_[:], op0=ALU.mult, op1=ALU.mult),
                         [o, rs_, self.colg], [mixT.tks[h]])
            cnt = 0
            for pp in range(2):
                for T in range(4):
                    first = True
                    for blk in range(2):
                        for hp in range(2):
                            hd = pp * 2 + hp
                            sb, pt = Sb2[cnt % 2], PT2[cnt % 2]
                            cnt += 1
                            P.op("pe", lambda e: e.matmul(sb[:, 0:512], mKT[hp * 64:(hp + 1) * 64, pp, blk * 128:(blk + 1) * 128],
                                                          mqT[hp * 64:(hp + 1) * 64, pp, T * 512:(T + 1) * 512],
                                                          start=True, stop=True), [mKT, mqT], [sb])
                            P.op("act", lambda e: e.activation(out=pt[:, 0:512], in_=sb[:, 0:512], func=AF.Exp), [sb], [pt])
                            last = (blk == 1 and hp == 1)
                            P.op("pe", lambda e: e.matmul(OT[0][:, :], mVp[:, blk, hd, :], pt[:, 0:512], start=first, stop=last),
                                 [pt, mVp], [OT[0]])
                            P.op("pe", lambda e: e.matmul(LS[0][:, :], onesp[:, hp, :], pt[:, 0:512], start=first, stop=last),
                                 [pt, onesp], [LS[0]])
                            first = False
                    r0 = tmp[0]
                    P.op("dve", lambda e: e.reciprocal(out=r0[:], in_=LS[0][:, :]), [LS[0]], [r0])
                    P.op("dve", lambda e: e.tensor_tensor(out=mixT[:, 6 + pp, T * 512:(T + 1) * 512], in0=OT[0][:, :], in1=r0[:],
                                                          op=ALU.mult), [OT[0], r0], [mixT.tks[6 + pp]])
            P.barrier()


    def phase_out(self, L, mix, x_src, h2T, mixT=None):
        P = self.P
        if f"mix{L}" in self.dbg and mix is not None:
            for tt in range(NT):
                P.dma("sp", self.dbg[f"mix{L}"][tt * 128:(tt + 1) * 128, :], mix[:, tt, :], [mix.tks[tt]],
                      [self.dbg[f"mix{L}"]])
        with contextlib.ExitStack() as s3:
            Wn = dict(junk=P.sb(s3, "njunk", [128, D], F32), ss=P.sb(s3, "nss", [128, 1], F32),
                      rt=P.sb(s3, "nrt", [128, 2], F32), xn=P.sb(s3, "nxn", [128, D], BF16),
                      pt=P.ps(s3, "npt0", [128, 1024], BF16))
            self.out_proj(L, mix, x_src, self.xa[L], h2T, Wn, s3, featT=mixT)
            P.barrier()

    def ffn(self, L, h2T, xa_src, x_dst):
        P = self.P
        with contextlib.ExitStack() as s:
            actT = P.sb(s, "actT", [128, 22, TOK], BF16, n=22)
            cw = P.sb(s, "cw", [128, 44, 3], F32)
            cb = P.sb(s, "cb", [128, 44], F32)
            P.dma("sp", cw[:, :, :], self.I("convw")[:, L * 132:(L + 1) * 132].rearrange("p (c w) -> p c w", w=3),
                  [self.I("convw")], [cw])
            P.dma("sp", cb[:], self.I("convb")[:, L * 44:(L + 1) * 44], [self.I("convb")], [cb])
            with contextlib.ExitStack() as s2:
                wup = [P.sb(s2, f"wup{i}", [128, 8, 256], BF16, n=8) for i in range(2)]
                us = [P.sb(s2, f"us{i}", [128, 2, 1026], F32) for i in range(2)]
                cv = [P.sb(s2, f"cv{i}", [128, 2, 1024], F32) for i in range(2)]
                pu = [P.ps(s2, f"pu{i}", [128, 512], F32) for i in range(4)]
                ph = P.ps(s2, "ph", [128, 512], F32, n=4)
                wsrc = self.I(f"w_up{L}")
                it = 0
                for cc in range(22):
                    wb = wup[cc % 2]
                    P.dma("pool", wb[:, :, 0:128],
                          wsrc.t.ap()[:, cc * 128:(cc + 1) * 128].rearrange("(k p) c -> p k c", p=128), [wsrc], wb.tks)
                    P.dma("pool", wb[:, :, 128:256],
                          wsrc.t.ap()[:, DFF + cc * 128:DFF + (cc + 1) * 128].rearrange("(k p) c -> p k c", p=128),
                          [wsrc], wb.tks)
                    for half in range(2):
                        ub, cvb = us[it % 2], cv[it % 2]
                        t0 = half * 1024
                        for part in range(2):
                            hs = ph.tks[(2 * it + part) % 4]
                            hc = ((2 * it + part) % 4) * 8
                            for kc in range(8):
                                P.op("pe", lambda e: e.matmul(ph[:, hc:hc + 2], wb[:, kc, part * 128:(part + 1) * 128],
                                                              h2T[:, kc, t0:t0 + 2], start=(kc == 0), stop=(kc == 7)),
                                     [wb.tks[kc], h2T], [hs])
                            P.op("act", lambda e: e.activation(out=ub[:, part, 0:2], in_=ph[:, hc:hc + 2], func=AF.Copy),
                                 [hs], [ub])
                            for pc_ in range(2):
                                pb = pu[(4 * it + 2 * part + pc_) % 4]
                                c0 = t0 + 2 + pc_ * 512
                                for kc in range(8):
                                    P.op("pe", lambda e: e.matmul(pb[:, :], wb[:, kc, part * 128:(part + 1) * 128],
                                                                  h2T[:, kc, c0:c0 + 512], start=(kc == 0), stop=(kc == 7)),
                                         [wb.tks[kc], h2T], [pb])
                                P.op("act", lambda e: e.activation(out=ub[:, part, 2 + pc_ * 512:2 + (pc_ + 1) * 512],
                                                                   in_=pb[:, :], func=AF.Copy), [pb], [ub])
                        for part in range(2):
                            ch = part * 22 + cc
                            P.op("act", lambda e: e.activation(out=cvb[:, part, :], in_=ub[:, part, 2:1026], func=AF.Identity,
                                                               scale=cw[:, ch, 2:3], bias=cb[:, ch:ch + 1]),
                                 [ub, cw, cb], [cvb])
                            P.op("dve", lambda e: e.scalar_tensor_tensor(out=cvb[:, part, :], in0=ub[:, part, 1:1025],
                                                                         scalar=cw[:, ch, 1:2], in1=cvb[:, part, :],
                                                                         op0=ALU.mult, op1=ALU.add), [ub, cw, cvb], [cvb])
                            P.op("dve", lambda e: e.scalar_tensor_tensor(out=cvb[:, part, :], in0=ub[:, part, 0:1024],
                                                                         scalar=cw[:, ch, 0:1], in1=cvb[:, part, :],
                                                                         op0=ALU.mult, op1=ALU.add), [ub, cw, cvb], [cvb])
                        P.op("act", lambda e: e.activation(out=cvb[:, 1, :], in_=cvb[:, 1, :], func=AF.Silu), [cvb], [cvb])
                        P.op("dve", lambda e: e.tensor_tensor(out=actT[:, cc, t0:t0 + 1024], in0=cvb[:, 0, :], in1=cvb[:, 1, :],
                                                              op=ALU.mult), [cvb], [actT.tks[cc]])
                        it += 1
                P.barrier()
            with contextlib.ExitStack() as s3:
                Wd = P.sb(s3, "Wd", [128, 22, 1024], BF16, n=22)
                self.load_w(Wd, self.I(f"w_down{L}"), 22, 1024)
                xin = [P.sb(s3, f"dxin{i}", [128, D], F32) for i in range(2)]
                xo = [P.sb(s3, f"dxo{i}", [128, D], F32) for i in range(2)]
                pd = [P.ps(s3, f"pd{i}", [128, 512], F32) for i in range(4)]
                for tt in range(NT):
                    xb, xob = xin[tt % 2], xo[tt % 2]
                    P.dma("sp", xb[:], xa_src[tt * 128:(tt + 1) * 128, :], [xa_src.tks[tt]], [xb])
                    for half in range(2):
                        pb = pd[(2 * tt + half) % 4]
                        for cc in range(22):
                            P.op("pe", lambda e: e.matmul(pb[:, :], actT[:, cc, tt * 128:(tt + 1) * 128],
                                                          Wd[:, cc, half * 512:(half + 1) * 512],
                                                          start=(cc == 0), stop=(cc == 21)), [actT.tks[cc], Wd.tks[cc]], [pb])
                        P.op("dve", lambda e: e.tensor_tensor(out=xob[:, half * 512:(half + 1) * 512], in0=pb[:, :],
                                                              in1=xb[:, half * 512:(half + 1) * 512], op=ALU.add),
                             [pb, xb], [xob])
                    P.dma("sp", x_dst[tt * 128:(tt + 1) * 128, :], xob[:], [xob], [x_dst.tks[tt]])
                P.barrier()

    def phase_c1(self, st, QAT):
        P = self.P
        bc = self.bc
        with contextlib.ExitStack() as s:
            Win = P.sb(s, "Win1", [128, 8, 2576], BF16, n=8)
            self.load_w(Win, self.I("w_in_gla"), 8, 2576)
            gw = P.sb(s, "gw", [16, 384], F32)
            P.dma("sp", gw[:], self.I("gla_gate_w")[:, :], [self.I("gla_gate_w")], [gw])
            glam = P.sb(s, "glam", [128, 256], F32)
            P.dma("sp", glam[:], self.I("glam")[:, :], [self.I("glam")], [glam])
            ones = P.sb(s, "ones", [128, 1], F32)
            P.op("dve", lambda e: e.memset(ones[:], 1.0), [], [ones])
            xin = [P.sb(s, f"cxin{i}", [128, D], F32) for i in range(2)]
            hT = [P.sb(s, f"chT{i}", [128, 8, 128], BF16) for i in range(2)]
            prt = [P.sb(s, f"prt{i}", [128, 2192], F32) for i in range(2)]
            W = dict(junk=P.sb(s, "cjunk", [128, D], F32), ss=P.sb(s, "css", [128, 1], F32),
                     rt=P.sb(s, "crt", [128, 2], F32), xn=P.sb(s, "cxn", [128, D], BF16),
                     pt=P.ps(s, "cpt0", [128, 1024], BF16),
                     sq=P.sb(s, "csq", [128, 768], F32), ssg=P.sb(s, "cssg", [128, 96], F32),
                     t1=P.sb(s, "ct1", [128, 768], F32))
            pt = W["pt"]
            Bk = [P.ps(s, f"cB{i}", [128, 512], F32) for i in range(7)]
            pj = [Bk[0], Bk[1]]
            pq, pk, pz, pgT, pkv = Bk[2], Bk[3], Bk[4], Bk[5], Bk[6]
            mn = P.sb(s, "cmn", [128, 256], BF16)
            vb = P.sb(s, "cvb", [128, 768], BF16)
            glT = P.sb(s, "glT", [16, 128], F32)
            la = P.sb(s, "la", [128, 384], F32)
            Ep = P.sb(s, "Ep", [96, 512], F32)
            En = P.sb(s, "En", [96, 512], F32)
            QpT = P.sb(s, "QpT", [96, 4, 128], BF16)
            QnT = P.sb(s, "QnT", [96, 4, 128], BF16)
            KpT = P.sb(s, "KpT", [96, 4, 128], BF16)
            KnT = P.sb(s, "KnT", [96, 4, 128], BF16)
            Qlo = P.sb(s, "Qlo", [96, 4, 128], BF16)
            Qhi = P.sb(s, "Qhi", [96, 4, 128], BF16)
            Ed = P.sb(s, "Ed", [128, 384], F32)
            Kend = P.sb(s, "Kend", [128, 2, 384], BF16)
            sc1 = P.sb(s, "sc1", [128, 128], F32)
            sc2 = P.sb(s, "sc2", [128, 128], F32)
            scT = P.sb(s, "scT", [128, 4, 128], BF16)
            S = P.sb(s, "Sloc", [96, 768], F32)
            Sa = P.sb(s, "Sa", [96, 768], BF16)
            Sb_ = P.sb(s, "Sbb", [96, 768], BF16)
            A = P.sb(s, "Acum", [96, 4], F32)
            gsum = P.sb(s, "gsum", [96, 772], F32)
            srb = [P.sb(s, f"srb{i}", [128, 768], BF16) for i in range(2)]
            olb = [P.sb(s, f"olb{i}", [128, 768], F32) for i in range(2)]
            P.op("dve", lambda e: e.memset(S[:], 0.0), [], [S])
            P.op("dve", lambda e: e.memset(A[:], 1.0), [], [A])
            P.op("pool", lambda e: e.memset(Qlo[:], 0.0), [], [Qlo])
            P.op("pool", lambda e: e.memset(Qhi[:], 0.0), [], [Qhi])
            mle, mgt = glam[:, 0:128], glam[:, 128:256]
            ogb = BC_OFF["gla_gb"]
            chunks = [(384, 512), (896, 512), (1408, 512), (1920, 512), (2432, 144)]
            import os
            LV = int(os.environ.get("K_C1LVL", "99"))
            for tt in range(int(os.environ.get("K_C1NT", NT))):
                xb, hb, pr = xin[tt % 2], hT[tt % 2], prt[tt % 2]
                P.dma("sp", xb[:], self.xf0[tt * 128:(tt + 1) * 128, :], [self.xf0.tks[tt]], [xb])
                self.norm_T(xb, self.gcol(0, 1), hb[:, :, :], hb.tk, W)
                for n, (c0, w) in enumerate(chunks):
                    pb = pj[n % 2]
                    for kc in range(8):
                        P.op("pe", lambda e: e.matmul(pb[:, 0:w], hb[:, kc, :], Win[:, kc, c0:c0 + w],
                                                      start=(kc == 0), stop=(kc == 7)), [hb, Win.tks[kc]], [pb])
                    if n % 2 == 0:
                        P.op("act", lambda e: e.activation(out=pr[:, c0 - 384:c0 - 384 + w], in_=pb[:, 0:w], func=AF.Copy),
                             [pb], [pr])
                    else:
                        P.op("dve", lambda e: e.tensor_copy(out=pr[:, c0 - 384:c0 - 384 + w], in_=pb[:, 0:w]), [pb], [pr])
                if LV < 1:
                    continue
                for h in range(4):
                    for kc in range(8):
                        P.op("pe", lambda e: e.matmul(pq[0:96, h * 128:(h + 1) * 128], Win[:, kc, h * 96:(h + 1) * 96],
                                                      hb[:, kc, :], start=(kc == 0), stop=(kc == 7)),
                             [hb, Win.tks[kc]], [pq])
                for h in range(4):
                    for kc in range(8):
                        P.op("pe", lambda e: e.matmul(pk[0:96, h * 128:(h + 1) * 128],
                                                      Win[:, kc, 384 + h * 96:384 + (h + 1) * 96],
                                                      hb[:, kc, :], start=(kc == 0), stop=(kc == 7)),
                             [hb, Win.tks[kc]], [pk])
                for kc in range(8):
                    P.op("pe", lambda e: e.matmul(pz[0:16, 384:512], Win[:, kc, 2304:2320], hb[:, kc, :],
                                                  start=(kc == 0), stop=(kc == 7)), [hb, Win.tks[kc]], [pz])
                P.op("act", lambda e: e.activation(out=glT[:], in_=pz[0:16, 384:512], func=AF.Copy), [pz], [glT])
                if LV < 2:
                    continue
                self.group_norm(pr, 1936, 4, self.gains[:, 256:320], mn[:, :].rearrange("p (g d) -> p g d", d=64),
                                mn.tk, W, [pr])
                for h in range(2):
                    P.op("pe", lambda e: e.transpose(pt[:, h * 128:(h + 1) * 128], mn[:, h * 128:(h + 1) * 128],
                                                     self.idb[:]), [mn, self.idb], [pt])
                P.op("act", lambda e: e.activation(out=self.mqT[:, :, tt * 128:(tt + 1) * 128],
                                                   in_=pt[:, 0:256].rearrange("p (h t) -> p h t", h=2), func=AF.Copy),
                     [pt], [self.mqT])
                if LV < 3:
                    continue
                srt = srb[tt % 2]
                P.op("act", lambda e: e.activation(out=srt[:], in_=pr[:, 1152:1920], func=AF.Silu), [pr], [srt])
                P.dma("sp", self.srd[tt * 128:(tt + 1) * 128, :], srt[:], [srt], [self.srd.tks[tt]])
                P.op("pool", lambda e: e.tensor_copy(out=vb[:], in_=pr[:, 384:1152]), [pr], [vb])
                if LV < 4:
                    continue
                P.op("pe", lambda e: e.matmul(pz[:, 0:384], glT[:, :], gw[:, :], start=True, stop=True), [glT, gw], [pz])
                P.op("dve", lambda e: e.tensor_tensor(out=la[:], in0=pz[:, 0:384], in1=bc[:, ogb:ogb + 384], op=ALU.add),
                     [pz, bc], [la])
                P.op("act", lambda e: e.activation(out=la[:], in_=la[:], func=AF.Exp, scale=-1.0), [la], [la])
                P.op("act", lambda e: e.activation(out=la[:], in_=la[:], func=AF.Ln, bias=ones[:, 0:1]), [la, ones], [la])
                P.op("dve", lambda e: e.tensor_scalar(out=la[:], in0=la[:], scalar1=-1.0 / 16, scalar2=None, op0=ALU.mult),
                     [la], [la])
                if LV < 5:
                    continue
                for h in range(4):
                    P.op("pe", lambda e: e.matmul(pgT[0:96, h * 128:(h + 1) * 128], la[:, h * 96:(h + 1) * 96], mle,
                                                  start=True, stop=True), [la, glam], [pgT])
                P.op("pe", lambda e: e.matmul(pz[:, 0:384], mgt, la[:, :], start=True, stop=True), [la, glam], [pz])
                P.op("act", lambda e: e.activation(out=Ep[:], in_=pgT[0:96, :], func=AF.Exp), [pgT], [Ep])
                P.op("act", lambda e: e.activation(out=En[:], in_=pgT[0:96, :], func=AF.Exp, scale=-1.0), [pgT], [En])
                P.op("act", lambda e: e.activation(out=Ed[:], in_=pz[:, 0:384], func=AF.Exp), [pz], [Ed])
                if LV < 6:
                    continue
                qs = 96 ** -0.5
                P.op("dve", lambda e: e.scalar_tensor_tensor(out=QpT[:, :, :].rearrange("p h t -> p (h t)"), in0=pq[0:96, :],
                                                             scalar=qs, in1=Ep[:], op0=ALU.mult, op1=ALU.mult),
                     [pq, Ep], [QpT])
                P.op("dve", lambda e: e.scalar_tensor_tensor(out=QnT[:, :, :].rearrange("p h t -> p (h t)"), in0=pq[0:96, :],
                                                             scalar=qs, in1=En[:], op0=ALU.mult, op1=ALU.mult),
                     [pq, En], [QnT])
                P.op("dve", lambda e: e.tensor_tensor(out=KpT[:, :, :].rearrange("p h t -> p (h t)"), in0=pk[0:96, :],
                                                      in1=Ep[:], op=ALU.mult), [pk, Ep], [KpT])
                P.op("dve", lambda e: e.tensor_tensor(out=KnT[:, :, :].rearrange("p h t -> p (h t)"), in0=pk[0:96, :],
                                                      in1=En[:], op=ALU.mult), [pk, En], [KnT])
                for ck in range(2):
                    msel = glam[:, 63 + 64 * ck:64 + 64 * ck]
                    P.op("dve", lambda e: e.scalar_tensor_tensor(out=Kend[:, ck, :], in0=pr[:, 0:384], scalar=msel, in1=Ed[:],
                                                                 op0=ALU.mult, op1=ALU.mult), [pr, Ed, glam], [Kend])
                if LV < 7:
                    continue
                P.op("pool", lambda e: e.tensor_copy(out=Qlo[:, :, 0:64], in_=QpT[:, :, 0:64]), [QpT], [Qlo])
                P.op("pool", lambda e: e.tensor_copy(out=Qhi[:, :, 64:128], in_=QpT[:, :, 64:128]), [QpT], [Qhi])
                if LV < 8:
                    continue
                for h in range(4):
                    P.op("dve", lambda e: e.tensor_scalar(out=QAT[:, tt, h, 0:64], in0=QpT[:, h, 0:64], scalar1=A[:, h:h + 1],
                                                          scalar2=None, op0=ALU.mult), [QpT, A], [QAT.tks[tt]])
                for h in range(4):
                    P.op("dve", lambda e: e.tensor_tensor(out=A[:, h:h + 1], in0=A[:, h:h + 1], in1=Ep[:, h * 128 + 63:h * 128 + 64],
                                                          op=ALU.mult), [A, Ep], [A])
                for h in range(4):
                    P.op("dve", lambda e: e.tensor_scalar(out=QAT[:, tt, h, 64:128], in0=QpT[:, h, 64:128], scalar1=A[:, h:h + 1],
                                                          scalar2=None, op0=ALU.mult), [QpT, A], [QAT.tks[tt]])
                for h in range(4):
                    P.op("dve", lambda e: e.tensor_tensor(out=A[:, h:h + 1], in0=A[:, h:h + 1], in1=Ep[:, h * 128 + 127:h * 128 + 128],
                                                          op=ALU.mult), [A, Ep], [A])
                if LV < 9:
                    continue
                psc = [pq, pk]
                for h in range(4):
                    pb = psc[h // 2]
                    c = (h % 2) * 256
                    P.op("pe", lambda e: e.matmul(pb[:, c:c + 128], KnT[:, h, :], QpT[:, h, :], start=True, stop=True),
                         [KnT, QpT], [pb])
                    P.op("pe", lambda e: e.matmul(pb[:, c + 128:c + 256], KpT[:, h, :], QnT[:, h, :], start=True, stop=True),
                         [KpT, QnT], [pb])
                    P.op("dve", lambda e: e.tensor_tensor(out=sc1[:], in0=pb[:, c:c + 128], in1=mle, op=ALU.mult),
                         [pb, glam], [sc1])
                    P.op("dve", lambda e: e.tensor_tensor(out=sc2[:], in0=pb[:, c + 128:c + 256], in1=mgt, op=ALU.mult),
                         [pb, glam], [sc2])
                    P.op("pool", lambda e: e.tensor_tensor(out=scT[:, h, :], in0=sc1[:], in1=sc2[:], op=ALU.add),
                         [sc1, sc2], [scT])
                if LV < 10:
                    continue
                P.op("act", lambda e: e.activation(out=Sa[:], in_=S[:], func=AF.Copy), [S], [Sa])
                for h in range(4):
                    hs = slice(h * 192, (h + 1) * 192)
                    for ck in range(int(os.environ.get("K_NCK", 2))):
                        P.op("pe", lambda e: e.matmul(pkv[0:96, ck * 192:(ck + 1) * 192],
                                                      Kend[:, ck, h * 96:(h + 1) * 96],
                                                      vb[:, hs], start=True, stop=True),
                             [Kend, vb], [pkv])
                    if os.environ.get("K_SUB") == "a":
                        continue
                    P.op("dve", lambda e: e.scalar_tensor_tensor(out=S[:, hs], in0=S[:, hs], scalar=Ep[:, h * 128 + 63:h * 128 + 64],
                                                                 in1=pkv[0:96, 0:192], op0=ALU.mult, op1=ALU.add),
                         [S, Ep, pkv], [S])
                    P.op("act", lambda e: e.activation(out=Sb_[:, hs], in_=S[:, hs], func=AF.Copy), [S], [Sb_])
                    P.op("dve", lambda e: e.scalar_tensor_tensor(out=S[:, hs], in0=S[:, hs],
                                                                 scalar=Ep[:, h * 128 + 127:h * 128 + 128],
                                                                 in1=pkv[0:96, 192:384], op0=ALU.mult, op1=ALU.add),
                         [S, Ep, pkv], [S])
                    if os.environ.get("K_SUB") == "b":
                        continue
                    ob = pj[h // 2]
                    oc = (h % 2) * 192
                    P.op("pe", lambda e: e.matmul(ob[:, oc:oc + 192], scT[:, h, :], vb[:, hs], start=True, stop=False),
                         [scT, vb], [ob])
                    P.op("pe", lambda e: e.matmul(ob[:, oc:oc + 192], Qlo[:, h, :], Sa[:, hs], start=False, stop=False),
                         [Qlo, Sa], [ob])
                    P.op("pe", lambda e: e.matmul(ob[:, oc:oc + 192], Qhi[:, h, :], Sb_[:, hs], start=False, stop=True),
                         [Qhi, Sb_], [ob])
                    P.op("dve", lambda e: e.tensor_copy(out=olb[tt % 2][:, hs], in_=ob[:, oc:oc + 192]), [ob], [olb[tt % 2]])
                P.dma("sp", self.olocd[tt * 128:(tt + 1) * 128, :], olb[tt % 2][:], [olb[tt % 2]], [self.olocd.tks[tt]])
            P.op("dve", lambda e: e.tensor_copy(out=gsum[:, 0:4], in_=A[:]), [A], [gsum])
            P.op("dve", lambda e: e.tensor_copy(out=gsum[:, 4:772], in_=S[:]), [S], [gsum])
            P.dma("sp", self.gsrc[:, :], gsum[:], [gsum], [self.gsrc])
            if not os.environ.get("K_NOCC"):
                P.allgather(self.gsrc, self.gall, GROUPS)
            P.barrier()

    def phase_d1(self, mix, QAT, mKT, mV):
        P = self.P
        pc = self.pc
        with contextlib.ExitStack() as s:
            ga = P.sb(s, "ga", [96, 4, 772], F32)
            P.dma("sp", ga[:, :, :], self.gall.t.ap().rearrange("(r p) c -> p r c", p=96), [self.gall], [ga])
            Si = P.sb(s, "Si", [96, 768], F32)
            Sib = P.sb(s, "Sib", [96, 768], BF16)
            coef = P.sb(s, "coef", [96, 4], F32)
            tmpL = P.sb(s, "tmpL", [96, 768], F32)
            P.op("dve", lambda e: e.memset(Si[:], 0.0), [], [Si])
            for r in range(4):
                lt, nlt = pc[0:96, 8 + r:9 + r], pc[0:96, 12 + r:13 + r]
                P.op("dve", lambda e: e.tensor_scalar(out=coef[:], in0=ga[:, r, 0:4], scalar1=lt, scalar2=nlt,
                                                      op0=ALU.mult, op1=ALU.add), [ga, pc], [coef])
                P.op("dve", lambda e: e.tensor_scalar(out=tmpL[:], in0=ga[:, r, 4:772], scalar1=lt, scalar2=None,
                                                      op0=ALU.mult), [ga, pc], [tmpL])
                for h in range(4):
                    hs = slice(h * 192, (h + 1) * 192)
                    P.op("dve", lambda e: e.scalar_tensor_tensor(out=Si[:, hs], in0=Si[:, hs], scalar=coef[:, h:h + 1],
                                                                 in1=tmpL[:, hs], op0=ALU.mult, op1=ALU.add),
                         [Si, coef, tmpL], [Si])
            P.op("act", lambda e: e.activation(out=Sib[:], in_=Si[:], func=AF.Copy), [Si], [Sib])
            pcb = [P.ps(s, f"dB{i}", [128, 512], F32) for i in range(2)]
            o = [P.sb(s, f"do{i}", [128, 768], F32) for i in range(2)]
            sq = P.sb(s, "dsq", [128, 768], F32)
            st4 = P.sb(s, "dst4", [128, 12], F32)
            t1 = P.sb(s, "dt1", [128, 768], F32)
            t2 = P.sb(s, "dt2", [128, 768], F32)
            olb = [P.sb(s, f"dolb{i}", [128, 768], F32) for i in range(2)]
            srb = [P.sb(s, f"dsrb{i}", [128, 768], BF16) for i in range(2)]
            for tt in range(NT):
                ot = o[tt % 2]
                ol, srt = olb[tt % 2], srb[tt % 2]
                P.dma("sp", ol[:], self.olocd[tt * 128:(tt + 1) * 128, :], [self.olocd.tks[tt]], [ol])
                P.dma("sp", srt[:], self.srd[tt * 128:(tt + 1) * 128, :], [self.srd.tks[tt]], [srt])
                for h in range(4):
                    hs = slice(h * 192, (h + 1) * 192)
                    ob = pcb[h // 2]
                    oc = (h % 2) * 192
                    P.op("pe", lambda e: e.matmul(ob[:, oc:oc + 192], QAT[:, tt, h, :], Sib[:, hs], start=True, stop=True),
                         [QAT.tks[tt], Sib], [ob])
                    P.op("dve", lambda e: e.tensor_tensor(out=ot[:, hs], in0=ob[:, oc:oc + 192], in1=ol[:, hs], op=ALU.add),
                         [ob, ol], [ot])
                P.op("pool", lambda e: e.tensor_tensor(out=sq[:], in0=ot[:], in1=ot[:], op=ALU.mult), [ot], [sq])
                P.op("dve", lambda e: e.tensor_reduce(out=st4[:, 0:4], in_=sq[:, :].rearrange("p (g d) -> p g d", d=192),
                                                      axis=AX.X, op=ALU.add), [sq], [st4])
                P.op("act", lambda e: e.activation(out=st4[:, 4:8], in_=st4[:, 0:4], func=AF.Sqrt, scale=1.0 / 192,
                                                   bias=self.epsb[:, 0:1]), [st4, self.epsb], [st4])
                P.op("dve", lambda e: e.reciprocal(out=st4[:, 8:12], in_=st4[:, 4:8]), [st4], [st4])
                P.op("dve", lambda e: e.tensor_tensor(out=t1[:, :].rearrange("p (g d) -> p g d", d=192),
                                                      in0=ot[:, :].rearrange("p (g d) -> p g d", d=192),
                                                      in1=st4[:, 8:12].unsqueeze(2).to_broadcast([128, 4, 192]), op=ALU.mult),
                     [ot, st4], [t1])
                P.op("pool", lambda e: e.tensor_tensor(out=t2[:, :].rearrange("p (g d) -> p g d", d=192),
                                                       in0=t1[:, :].rearrange("p (g d) -> p g d", d=192),
                                                       in1=self.gains[:, 512:704].unsqueeze(1).to_broadcast([128, 4, 192]),
                                                       op=ALU.mult), [t1, self.gains], [t2])
                P.op("dve", lambda e: e.tensor_tensor(out=mix[:, tt, 0:768], in0=t2[:], in1=srt[:], op=ALU.mult),
                     [t2, srt], [mix.tks[tt]])
            with contextlib.ExitStack() as s2:
                PT = [P.sb(s2, f"xPT{i}", [128, 512], BF16) for i in range(4)]
                Sb = [P.ps(s2, f"xSb{i}", [128, 512], F32) for i in range(4)]
                Ob = [P.ps(s2, f"xOb{i}", [128, 512], F32) for i in range(2)]
                F = dict(rr=[P.sb(s2, f"xfrr{i}", [128, 8], F32) for i in range(2)])
                self.cross_attn(mKT, mV, self.mqT, mix, Sb, Ob, PT, F)
            P.barrier()

    def build(self):
        P = self.P
        with contextlib.ExitStack() as st:
            self.setup(st)
            self.setup_gains(st)
            P.barrier()
            with contextlib.ExitStack() as sl0:
                h2T = P.sb(sl0, "h2T", [128, 8, TOK + 2], BF16)
                with contextlib.ExitStack() as satt:
                  if self.stop not in ("l1only", "c1only"):
                      self.qT = P.sb(satt, "qT", [128, 6, TOK], BF16)
                      self.mqT = P.sb(satt, "mqT", [128, 2, TOK], BF16)
                      if self.stop == "s0":
                          P.barrier()
                          return self.nc
                      self.phase_a0(satt)
                      if self.stop == "s3":
                          return self.nc
                      self.setup_attn_consts(satt)
                      self.mKT0, self.mV0 = self.setup_mem(satt, 0, "0")
                      mixT = P.sb(satt, "mixT", [128, 8, TOK], BF16, n=8)
                      self.phase_b0T(mixT)
                      if self.stop == "s4":
                          return self.nc
                      self.phase_out(0, None, self.I("x"), h2T, mixT=mixT)
                      if self.stop == "s5":
                          return self.nc
                dst = self.out if self.stop in ("l0", "b0h0") else self.xf0
                if self.stop == "a0":
                    dst = None
                if dst is not None and self.stop not in ("l1only", "c1only"):
                    self.ffn(0, h2T, self.xa[0], dst)
                if self.stop in ("l0", "b0h0", "a0"):
                    P.barrier()
                    return self.nc
                with contextlib.ExitStack() as s1:
                    self.mqT = P.sb(s1, "mqT1", [128, 2, TOK], BF16)
                    self.mKT1, self.mV1 = self.setup_mem(s1, 1, "1")
                    QAT = P.sb(s1, "QAT", [96, NT, 4, 128], BF16, n=NT)
                    self.phase_c1(s1, QAT)
                    if self.stop in ("c1", "c1only"):
                        return self.nc
                    mix = P.sb(s1, "mix1", [128, NT, 1024], BF16, n=NT)
                    self.phase_d1(mix, QAT, self.mKT1, self.mV1)
                    self.phase_out(1, mix, self.xf0, h2T)
                self.ffn(1, h2T, self.xa[1], self.out)
            P.barrier(cc=True)
        return self.nc


def _prep_inputs(inputs):
    f = lambda a: np.ascontiguousarray(np.asarray(a, np.float32))
    x, mem = f(inputs["x"]), f(inputs["mem"])
    bcv = np.concatenate([
        f(inputs["rel_bias"]).reshape(-1), f(inputs["diff_qk_norm"]).reshape(-1), f(inputs["diff_lambda"]).reshape(-1),
        f(inputs["diff_out_norm"]).reshape(-1), f(inputs["gla_gate_b"]).reshape(-1), f(inputs["gla_out_norm"]).reshape(-1),
        f(inputs["mem_qk_norm"]).reshape(-1)])
    assert bcv.shape[0] == BC_N
    bc = np.ascontiguousarray(np.broadcast_to(bcv[None, :], (128, BC_N)))
    colg = np.stack([f(inputs["attn_norm"]), f(inputs["ffn_norm"]), f(inputs["mem_norm"])], 0)
    colg = colg.reshape(3, 2, 8, 128).transpose(3, 0, 1, 2).reshape(128, 48)
    colg = np.ascontiguousarray(np.concatenate([colg, f(inputs["diff_out_norm"]).reshape(128, 1)], 1))
    convw = np.ascontiguousarray(f(inputs["conv_w"]).reshape(2, 3, 44, 128).transpose(3, 0, 2, 1).reshape(128, 2 * 44 * 3))
    convb = np.ascontiguousarray(f(inputs["conv_b"]).reshape(2, 44, 128).transpose(2, 0, 1).reshape(128, 88))
    mle, mgt = _gla_consts()
    common = dict(
        bc=bc, colg=colg, convw=convw, convb=convb,
        idb=np.eye(128, dtype=np.float32).astype(ml_dtypes.bfloat16),
        t5m=np.ascontiguousarray(T5_MASKS.reshape(128, NSLOT * 128)),
        glam=np.ascontiguousarray(np.concatenate([mle, mgt], 1)),
        w_in_diff=f(inputs["w_in_diff"])[0], w_in_gla=f(inputs["w_in_gla"])[0], gla_gate_w=f(inputs["gla_gate_w"])[0],
    )
    for i in range(2):
        common[f"w_kv{i}"] = f(inputs["w_mem_kv"])[i]
        common[f"w_out{i}"] = f(inputs["w_out"])[i]
        common[f"w_up{i}"] = f(inputs["w_up"])[i]
        common[f"w_down{i}"] = f(inputs["w_down"])[i]
    in_maps = []
    for c in range(8):
        b, qc = c // 4, c % 4
        pc, mt = _core_consts(c)
        m = dict(common)
        m["x"] = np.ascontiguousarray(x[b, qc * TOK:(qc + 1) * TOK])
        m["mem"] = np.ascontiguousarray(mem[b])
        m["pc"] = pc
        m["mt"] = mt
        in_maps.append(m)
    return in_maps


def run(inputs, stop=None, dbg=(), trace=False):
    B = Builder(stop)
    for name, shape, dt in dbg:
        B.dbg_out(name, shape, dt)
    nc = B.build()
    in_maps = _prep_inputs(inputs)
    used = set(B.inputs.keys())
    in_maps = [{k: v for k, v in m.items() if k in used} for m in in_maps]
    res = run_bass_kernel_spmd(nc, in_maps, core_ids=list(range(8)))
    return res, B


def kernel(**inputs):
    res, _ = run(inputs)
    out = np.zeros((2, S_FULL, D), np.float32)
    for c in range(8):
        b, qc = c // 4, c % 4
        out[b, qc * TOK:(qc + 1) * TOK] = res.results[c]["out"]
    return out
```

```python
import contextlib
import math
import numpy as np
import ml_dtypes
import concourse.bass as bass
import concourse.mybir as mybir
from concourse.bass_utils import run_bass_kernel_spmd

F32 = mybir.dt.float32
BF16 = mybir.dt.bfloat16
AF = mybir.ActivationFunctionType
ALU = mybir.AluOpType
AX = mybir.AxisListType

NDS = 16


class Ev:
    __slots__ = ("key", "val")

    def __init__(self, key, val):
        self.key = key
        self.val = val


class Tk:
    __slots__ = ("w", "r", "name")

    def __init__(self, name=""):
        self.w = None
        self.r = {}
        self.name = name


class Buf:
    def __init__(self, t, n=1, name=""):
        self.t = t
        self.tk = Tk(name)
        self.tks = [Tk(f"{name}{i}") for i in range(n)] if n > 1 else [self.tk]

    def __getitem__(self, k):
        return self.t[k]


class Prog:
    def __init__(self, nc):
        self.nc = nc
        self.eng = {"pe": nc.tensor, "act": nc.scalar, "dve": nc.vector, "pool": nc.gpsimd, "sp": nc.sync}
        self.sems = {}
        self.ecnt = {}
        for e in self.eng:
            self.sems["e_" + e] = nc.alloc_semaphore("sem_" + e)
            self.ecnt[e] = 0
        self.dval = {}
        self.dnext = {}
        for q in ("sp", "pool", "act"):
            for i in range(NDS):
                self.sems[f"d_{q}_{i}"] = nc.alloc_semaphore(f"dsem_{q}_{i}")
                self.dval[f"d_{q}_{i}"] = 0
            self.dnext[q] = 0
        self.sems["cc"] = nc.alloc_semaphore("sem_cc")
        self.ccval = 0
        self.waited = {e: {} for e in self.eng}
        self.stack = contextlib.ExitStack()
        self.n_ins = 0

    def sb(self, stack, name, shape, dtype, n=1):
        self.n_alloc = getattr(self, "n_alloc", 0) + 1
        t = stack.enter_context(self.nc.sbuf_tensor(f"s{self.n_alloc}_{name}", list(shape), dtype))
        return Buf(t, n, name)

    def ps(self, stack, name, shape, dtype, n=1):
        self.n_alloc = getattr(self, "n_alloc", 0) + 1
        t = stack.enter_context(self.nc.psum_tensor(f"p{self.n_alloc}_{name}", list(shape), dtype))
        return Buf(t, n, name)

    def dram(self, name, shape, dtype, kind=None, n=1):
        if kind is None:
            t = self.nc.dram_tensor(name, list(shape), dtype)
        else:
            t = self.nc.dram_tensor(name, list(shape), dtype, kind=kind)
        return Buf(t, n, name)

    def _wait(self, e, ev):
        if ev is None:
            return
        if self.waited[e].get(ev.key, 0) >= ev.val:
            return
        self.eng[e].wait_ge(self.sems[ev.key], ev.val)
        self.waited[e][ev.key] = ev.val

    def _deps(self, e, reads, writes):
        own = "e_" + e
        for t in reads:
            if t.w is not None and not (e == "pe" and t.w.key == own):
                self._wait(e, t.w)
        for t in writes:
            if t.w is not None and t.w.key != own:
                self._wait(e, t.w)
            for k, v in t.r.items():
                if k != own:
                    self._wait(e, Ev(k, v))

    def _mark(self, ev, reads, writes):
        for t in reads:
            if t.r.get(ev.key, 0) < ev.val:
                t.r[ev.key] = ev.val
        for t in writes:
            t.w = ev
            t.r = {}

    @staticmethod
    def _tks(lst):
        out = []
        for x in lst:
            if isinstance(x, Buf):
                out.append(x.tk)
            else:
                out.append(x)
        return out

    def op(self, e, fn, reads=(), writes=()):
        reads = self._tks(reads)
        writes = self._tks(writes)
        self._deps(e, reads, writes)
        ins = fn(self.eng[e])
        self.ecnt[e] += 1
        ins.then_inc(self.sems["e_" + e], 1)
        self._mark(Ev("e_" + e, self.ecnt[e]), reads, writes)
        self.n_ins += 1
        return ins

    def dma(self, q, out, in_, reads=(), writes=()):
        reads = self._tks(reads)
        writes = self._tks(writes)
        i = self.dnext[q]
        self.dnext[q] = (i + 1) % NDS
        key = f"d_{q}_{i}"
        if self.dval[key] > 0:
            self._wait(q, Ev(key, self.dval[key]))
        self._deps(q, reads, writes)
        ins = self.eng[q].dma_start(out=out, in_=in_)
        self.dval[key] += 16
        ins.then_inc(self.sems[key], 16)
        self._mark(Ev(key, self.dval[key]), reads, writes)
        self.n_ins += 1
        return ins

    def allgather(self, src, dst, groups, src_tks=None):
        src_tks = [src.tk] if src_tks is None else list(src_tks)
        self._deps("pool", src_tks, [dst.tk])
        ins = self.nc.gpsimd.collective_compute(
            "AllGather", ALU.bypass, replica_groups=groups,
            ins=[src.t.ap().opt()], outs=[dst.t.ap().opt()])
        self.ccval += 1
        ins.then_inc(self.sems["cc"])
        self._mark(Ev("cc", self.ccval), src_tks, [dst.tk])

    def barrier(self, cc=False):
        for e in self.eng:
            for e2 in self.eng:
                if e2 != e and self.ecnt[e2] > 0:
                    self._wait(e, Ev("e_" + e2, self.ecnt[e2]))
            for k, v in self.dval.items():
                if v > 0:
                    self._wait(e, Ev(k, v))
            if cc and self.ccval > 0:
                self._wait(e, Ev("cc", self.ccval))


D = 1024
TOK = 2048
NT = 16
S_FULL = 8192
TW = 768
DFF = 2816
EPS = 1e-6
NEG = -30000.0


def _t5_bucket(rel):
    rel = np.asarray(rel, np.int32)
    half, max_exact = 16, 8
    ret = np.where(rel > 0, half, 0)
    n = np.abs(rel)
    nf = np.maximum(n, 1).astype(np.float32)
    large = max_exact + (np.log(nf / np.float32(max_exact)) / np.float32(math.log(128 / 8))
                         * np.float32(half - max_exact)).astype(np.int32)
    large = np.minimum(large, half - 1)
    return ret + np.where(n < max_exact, n, large)


def _t5_masks():
    kl = np.arange(128)[:, None]
    ql = np.arange(128)[None, :]
    bd = _t5_bucket(kl - ql)
    bc = _t5_bucket(kl - ql - 128)
    slots = []
    masks = []
    for b in sorted(set(bd.ravel().tolist())):
        if b == 15:
            continue
        slots.append(("d", b))
        masks.append(bd == b)
    for b in sorted(set(bc.ravel().tolist())):
        if b == 15:
            continue
        slots.append(("c", b))
        masks.append(bc == b)
    slots.append(("m", -1))
    masks.append((kl >= 64) & (ql < 64))
    m = np.stack(masks, axis=1).astype(np.float32)
    return slots, m.astype(ml_dtypes.bfloat16)


T5_SLOTS, T5_MASKS = _t5_masks()
NSLOT = len(T5_SLOTS)


def _core_consts(c):
    qc = c % 4
    pc = np.zeros((128, 16), np.float32)
    for j in range(4):
        pc[:, j] = 1.0 if j == qc else 0.0
        pc[:, 4 + j] = 1.0 if j == qc - 1 else 0.0
        pc[:, 8 + j] = 1.0 if j < qc else 0.0
        pc[:, 12 + j] = 0.0 if j < qc else 1.0
    mt = np.zeros((128, 4, 64), np.float32)
    for T in range(4):
        for j in range(64):
            if j > 16 * qc + 4 * T + 3:
                mt[:, T, j] = NEG
    return pc, mt.reshape(128, 256)


def _gla_consts():
    s = np.arange(128)[:, None]
    t = np.arange(128)[None, :]
    same = (s // 64) == (t // 64)
    mle = (same & (s <= t)).astype(np.float32)
    mgt = (same & (s > t)).astype(np.float32)
    lo = np.zeros((128, 128), np.float32)
    return mle, mgt


BC_OFF = {}
_o = 0
for _n, _s in (("rel_bias", 192), ("diff_qk", 128), ("diff_lam", 256), ("diff_on", 128),
               ("gla_gb", 384), ("gla_on", 192), ("mem_qk", 256)):
    BC_OFF[_n] = _o
    _o += _s
BC_N = _o
GROUPS = [[0, 1, 2, 3], [4, 5, 6, 7]]


class Builder:
    def __init__(self, stop=None):
        self.stop = stop
        self.nc = bass.Bass("TRN2", target_bir_lowering=False)
        self.P = Prog(self.nc)
        P = self.P
        self.inputs = {}

        self.spec = {
            "x": ([TOK, D], F32, NT), "mem": ([256, D], F32, 1), "bc": ([128, BC_N], F32, 1),
            "colg": ([128, 48], F32, 1), "convw": ([128, 2 * 44 * 3], F32, 1), "convb": ([128, 88], F32, 1),
            "idb": ([128, 128], BF16, 1), "t5m": ([128, NSLOT * 128], BF16, 1), "pc": ([128, 16], F32, 1),
            "mt": ([128, 256], F32, 1), "glam": ([128, 256], F32, 1),
            "w_in_diff": ([D, 2560], F32, 1), "w_in_gla": ([D, 2576], F32, 1), "gla_gate_w": ([16, 384], F32, 1),
        }
        for i in range(2):
            self.spec[f"w_kv{i}"] = ([D, 512], F32, 1)
            self.spec[f"w_out{i}"] = ([D, D], F32, 1)
            self.spec[f"w_up{i}"] = ([D, 2 * DFF], F32, 1)
            self.spec[f"w_down{i}"] = ([DFF, D], F32, 1)
        self.out = P.dram("out", [TOK, D], F32, kind="ExternalOutput", n=NT)
        self.xa = [P.dram(f"xa{i}", [TOK, D], F32, n=NT) for i in range(2)]
        self.xf0 = P.dram("xf0", [TOK, D], F32, n=NT)
        self.ksrc = [P.dram(f"ksrc{g}", [256, TOK], BF16, n=NT) for g in range(3)]
        self.kall = [P.dram(f"kall{g}", [4 * 256, TOK], BF16) for g in range(3)]
        self.vsrc = [P.dram(f"vsrc{g}", [TOK, 256], BF16, n=NT) for g in range(3)]
        self.vall = [P.dram(f"vall{g}", [4 * TOK, 256], BF16) for g in range(3)]
        self.olocd = P.dram("olocd", [TOK, 768], F32, n=NT)
        self.srd = P.dram("srd", [TOK, 768], BF16, n=NT)
        self.gsrc = P.dram("gsrc", [96, 772], F32)
        self.gall = P.dram("gall", [4 * 96, 772], F32)
        self.hsrc = [P.dram(f"hsrc{i}", [128, 16], BF16) for i in range(2)]
        self.hall = [P.dram(f"hall{i}", [4 * 128, 16], BF16) for i in range(2)]
        self.dbg = {}

    def I(self, name):
        if name not in self.inputs:
            shape, dt, n = self.spec[name]
            self.inputs[name] = self.P.dram(name, shape, dt, kind="ExternalInput", n=n)
        return self.inputs[name]

    def dbg_out(self, name, shape, dt=F32):
        b = self.P.dram("dbg_" + name, shape, dt, kind="ExternalOutput")
        self.dbg[name] = b
        return b

    def setup(self, st):
        P = self.P
        self.bc = P.sb(st, "bc", [128, BC_N], F32)
        self.colg = P.sb(st, "colg", [128, 48], F32)
        self.idb = P.sb(st, "idb", [128, 128], BF16)
        self.pc = P.sb(st, "pc", [128, 16], F32)
        self.epsb = P.sb(st, "epsb", [128, 1], F32)
        P.dma("sp", self.bc[:], self.I("bc")[:, :], [self.I("bc")], [self.bc])
        P.dma("sp", self.colg[:], self.I("colg")[:, :], [self.I("colg")], [self.colg])
        P.dma("sp", self.idb[:], self.I("idb")[:, :], [self.I("idb")], [self.idb])
        P.dma("sp", self.pc[:], self.I("pc")[:, :], [self.I("pc")], [self.pc])
        P.op("dve", lambda e: e.memset(self.epsb[:], EPS), [], [self.epsb])

    def gcol(self, kind, layer):
        o = (kind * 2 + layer) * 8
        return self.colg[:, o:o + 8]

    def norm_T(self, xt, gcol, hT_ap, hT_tk, W):
        P = self.P
        junk, ss, rt, xn, pt = W["junk"], W["ss"], W["rt"], W["xn"], W["pt"]
        P.op("act", lambda e: e.activation(out=junk[:], in_=xt[:], func=AF.Square, accum_out=ss[:, 0:1]),
             [xt], [junk, ss])
        P.op("act", lambda e: e.activation(out=rt[:, 0:1], in_=ss[:, 0:1], func=AF.Sqrt, scale=1.0 / D,
                                           bias=self.epsb[:, 0:1]), [ss, self.epsb], [rt])
        P.op("dve", lambda e: e.reciprocal(out=rt[:, 1:2], in_=rt[:, 0:1]), [rt], [rt])
        P.op("dve", lambda e: e.tensor_scalar(out=xn[:], in0=xt[:], scalar1=rt[:, 1:2], scalar2=None,
                                              op0=ALU.mult), [xt, rt], [xn])
        for kc in range(8):
            P.op("pe", lambda e: e.transpose(pt[:, kc * 128:(kc + 1) * 128], xn[:, kc * 128:(kc + 1) * 128],
                                             self.idb[:]), [xn, self.idb], [pt])
        P.op("dve", lambda e: e.tensor_tensor(
            out=hT_ap, in0=pt[:, :].rearrange("p (k t) -> p k t", k=8),
            in1=gcol.unsqueeze(2).to_broadcast([128, 8, 128]), op=ALU.mult), [pt, self.colg], [hT_tk])

    def load_w(self, dst, src, kchunks, cols, c0=0, wid=None, q="pool"):
        P = self.P
        wid = cols if wid is None else wid
        half = max(1, kchunks // 2)
        for k0 in range(0, kchunks, half):
            k1 = min(kchunks, k0 + half)
            P.dma(q, dst[:, k0:k1, 0:wid],
                  src.t.ap()[k0 * 128:k1 * 128, c0:c0 + wid].rearrange("(k p) c -> p k c", p=128),
                  [src], [dst.tks[k] for k in range(k0, k1)])

    def group_norm(self, pr, c0, ng, gain_ap, out_ap, out_tk, W, reads):
        P = self.P
        sq, ssg, t1 = W["sq"], W["ssg"], W["t1"]
        n = ng * 64
        src3 = pr[:, c0:c0 + n].rearrange("p (g d) -> p g d", d=64)
        P.op("pool", lambda e: e.tensor_tensor(out=sq[:, 0:n], in0=pr[:, c0:c0 + n], in1=pr[:, c0:c0 + n],
                                               op=ALU.mult), reads, [sq])
        P.op("dve", lambda e: e.tensor_reduce(out=ssg[:, 0:ng], in_=sq[:, 0:n].rearrange("p (g d) -> p g d", d=64),
                                              axis=AX.X, op=ALU.add), [sq], [ssg])
        P.op("act", lambda e: e.activation(out=ssg[:, 32:32 + ng], in_=ssg[:, 0:ng], func=AF.Sqrt, scale=1.0 / 64,
                                           bias=self.epsb[:, 0:1]), [ssg, self.epsb], [ssg])
        P.op("dve", lambda e: e.reciprocal(out=ssg[:, 64:64 + ng], in_=ssg[:, 32:32 + ng]), [ssg], [ssg])
        P.op("dve", lambda e: e.tensor_tensor(
            out=t1[:, 0:n].rearrange("p (g d) -> p g d", d=64), in0=src3,
            in1=ssg[:, 64:64 + ng].unsqueeze(2).to_broadcast([128, ng, 64]), op=ALU.mult), reads + [ssg], [t1])
        P.op("pool", lambda e: e.tensor_tensor(
            out=out_ap, in0=t1[:, 0:n].rearrange("p (g d) -> p g d", d=64),
            in1=gain_ap.unsqueeze(1).to_broadcast([128, ng, 64]), op=ALU.mult), [t1, self.gains], [out_tk])

    def phase_a0(self, st):
        P = self.P
        with contextlib.ExitStack() as s:
            Win = P.sb(s, "Win", [128, 8, 2560], BF16, n=8)
            self.load_w(Win, self.I("w_in_diff"), 8, 2560)
            xin = [P.sb(s, f"xin{i}", [128, D], F32) for i in range(2)]
            hT = [P.sb(s, f"hT{i}", [128, 8, 128], BF16) for i in range(2)]
            pr = [P.sb(s, f"pr{i}", [128, 2560], F32) for i in range(2)]
            W = dict(junk=P.sb(s, "junk", [128, D], F32), ss=P.sb(s, "ss", [128, 1], F32),
                     rt=P.sb(s, "rt", [128, 2], F32), xn=P.sb(s, "xn", [128, D], BF16),
                     pt=P.ps(s, "pt0", [128, 1024], BF16),
                     sq=P.sb(s, "sq", [128, 768], F32), ssg=P.sb(s, "ssg", [128, 96], F32),
                     t1=P.sb(s, "t1", [128, 768], F32))
            qn = P.sb(s, "qn", [128, 768], BF16)
            kn = P.sb(s, "kn", [128, 768], BF16)
            mn = P.sb(s, "mn", [128, 256], BF16)
            vb = [P.sb(s, f"vb{i}", [128, 768], BF16) for i in range(2)]
            ksb = [P.sb(s, f"ksb{i}", [128, 6, 128], BF16) for i in range(2)]
            pj = [P.ps(s, f"pj{i}", [128, 512], F32) for i in range(2)]
            pt1 = P.ps(s, "pt1", [128, 1024], BF16)
            pt2 = P.ps(s, "pt2", [128, 1024], BF16)
            g = self.gains
            kdst = [self.ksrc[g].t.ap().rearrange("(h p) t -> p h t", p=128) for g in range(3)]
            import os
            LVL = int(os.environ.get('K_A0LVL', '9'))
            for tt in range(int(os.environ.get('K_A0NT', NT))):
                xb, hb, prb = xin[tt % 2], hT[tt % 2], pr[tt % 2]
                P.dma("sp", xb[:], self.I("x")[tt * 128:(tt + 1) * 128, :], [self.I("x").tks[tt]], [xb])
                self.norm_T(xb, self.gcol(0, 0), hb[:, :, :], hb.tk, W)
                if LVL < 2:
                    continue
                for n in range(5):
                    pb = pj[n % 2]
                    for kc in range(8):
                        P.op("pe", lambda e: e.matmul(pb[:, :], hb[:, kc, :], Win[:, kc, n * 512:(n + 1) * 512],
                                                      start=(kc == 0), stop=(kc == 7)),
                             [hb, Win.tks[kc]], [pb])
                    if n % 2 == 0:
                        P.op("act", lambda e: e.activation(out=prb[:, n * 512:(n + 1) * 512], in_=pb[:, :],
                                                           func=AF.Copy), [pb], [prb])
                    else:
                        P.op("dve", lambda e: e.tensor_copy(out=prb[:, n * 512:(n + 1) * 512], in_=pb[:, :]),
                             [pb], [prb])
                if LVL < 3:
                    continue
                self.group_norm(prb, 0, 12, g[:, 0:64], qn[:, :].rearrange("p (g d) -> p g d", d=64), qn.tk, W, [prb])
                self.group_norm(prb, 768, 12, g[:, 64:128], kn[:, :].rearrange("p (g d) -> p g d", d=64), kn.tk, W, [prb])
                self.group_norm(prb, 2304, 4, g[:, 128:192], mn[:, :].rearrange("p (g d) -> p g d", d=64), mn.tk, W, [prb])
                if LVL < 4:
                    continue
                vbb = vb[tt % 2]
                P.op("act", lambda e: e.activation(out=vbb[:], in_=prb[:, 1536:2304], func=AF.Copy), [prb], [vbb])
                for g3 in range(3):
                    P.dma("sp", self.vsrc[g3][tt * 128:(tt + 1) * 128, :], vbb[:, g3 * 256:(g3 + 1) * 256], [vbb],
                          [self.vsrc[g3].tks[tt]])
                if LVL < 5:
                    continue
                for h in range(6):
                    P.op("pe", lambda e: e.transpose(pt1[:, h * 128:(h + 1) * 128], qn[:, h * 128:(h + 1) * 128],
                                                     self.idb[:]), [qn, self.idb], [pt1])
                for h in range(2):
                    P.op("pe", lambda e: e.transpose(pt1[:, (6 + h) * 128:(7 + h) * 128], mn[:, h * 128:(h + 1) * 128],
                                                     self.idb[:]), [mn, self.idb], [pt1])
                for h in range(6):
                    P.op("pe", lambda e: e.transpose(pt2[:, h * 128:(h + 1) * 128], kn[:, h * 128:(h + 1) * 128],
                                                     self.idb[:]), [kn, self.idb], [pt2])
                if LVL == 5 and os.environ.get("K_SUB") == "a":
                    continue
                P.op("act", lambda e: e.activation(
                    out=self.qT[:, :, tt * 128:(tt + 1) * 128],
                    in_=pt1[:, 0:768].rearrange("p (h t) -> p h t", h=6), func=AF.Copy), [pt1], [self.qT])
                if LVL == 5 and os.environ.get("K_SUB") == "b":
                    continue
                if not (LVL == 5 and os.environ.get("K_SUB") == "c"):
                    P.op("act", lambda e: e.activation(
                        out=self.mqT[:, :, tt * 128:(tt + 1) * 128],
                        in_=pt1[:, 768:1024].rearrange("p (h t) -> p h t", h=2), func=AF.Copy), [pt1], [self.mqT])
                if LVL == 5 and os.environ.get("K_SUB") == "d":
                    continue
                kb = ksb[tt % 2]
                P.op("dve", lambda e: e.tensor_copy(out=kb[:, :, :], in_=pt2[:, 0:768].rearrange("p (h t) -> p h t", h=6)),
                     [pt2], [kb])
                if LVL < 6:
                    continue
                for g3 in range(3):
                    P.dma("sp", kdst[g3][:, :, tt * 128:(tt + 1) * 128], kb[:, 2 * g3:2 * g3 + 2, :], [kb],
                          [self.ksrc[g3].tks[tt]])
            import os
            if not os.environ.get("K_NOCC"):
                for g3 in range(3):
                    P.allgather(self.ksrc[g3], self.kall[g3], GROUPS, src_tks=self.ksrc[g3].tks)
                    P.allgather(self.vsrc[g3], self.vall[g3], GROUPS, src_tks=self.vsrc[g3].tks)
            P.barrier()

    def setup_gains(self, st):
        P = self.P
        self.gains = P.sb(st, "gains", [128, 704], F32)
        g, bc = self.gains, self.bc
        oq, om, od, og = BC_OFF["diff_qk"], BC_OFF["mem_qk"], BC_OFF["diff_on"], BC_OFF["gla_on"]
        P.op("dve", lambda e: e.tensor_scalar(out=g[:, 0:64], in0=bc[:, oq:oq + 64], scalar1=0.125, scalar2=None,
                                              op0=ALU.mult), [bc], [g])
        P.op("dve", lambda e: e.tensor_copy(out=g[:, 64:128], in_=bc[:, oq + 64:oq + 128]), [bc], [g])
        for L in range(2):
            P.op("dve", lambda e: e.tensor_scalar(out=g[:, 128 + 128 * L:192 + 128 * L],
                                                  in0=bc[:, om + 128 * L:om + 128 * L + 64], scalar1=0.125,
                                                  scalar2=None, op0=ALU.mult), [bc], [g])
            P.op("dve", lambda e: e.tensor_copy(out=g[:, 192 + 128 * L:256 + 128 * L],
                                                in_=bc[:, om + 128 * L + 64:om + 128 * L + 128]), [bc], [g])
        P.op("dve", lambda e: e.tensor_scalar(out=g[:, 384:512], in0=bc[:, od:od + 128], scalar1=0.8, scalar2=None,
                                              op0=ALU.mult), [bc], [g])
        P.op("dve", lambda e: e.tensor_copy(out=g[:, 512:704], in_=bc[:, og:og + 192]), [bc], [g])

    def setup_attn_consts(self, st):
        P = self.P
        bc = self.bc
        self.lam = P.sb(st, "lam", [128, 8], F32)
        self.Dg = P.sb(st, "Dg", [128, 6, 128], F32)
        self.Cn = P.sb(st, "Cn", [128, 6, 128], F32)
        self.maskw = P.sb(st, "maskw", [128, 384], F32)
        self.btab = P.sb(st, "btab", [128, 6, 256], F32)
        lam = self.lam
        ol = BC_OFF["diff_lam"]
        with contextlib.ExitStack() as s:
            tmp = P.sb(s, "lamtmp", [128, 128], F32)
            P.op("dve", lambda e: e.tensor_tensor(out=tmp[:, 0:64], in0=bc[:, ol:ol + 64], in1=bc[:, ol + 64:ol + 128],
                                                  op=ALU.mult), [bc], [tmp])
            P.op("dve", lambda e: e.tensor_tensor(out=tmp[:, 64:128], in0=bc[:, ol + 128:ol + 192],
                                                  in1=bc[:, ol + 192:ol + 256], op=ALU.mult), [bc, tmp], [tmp])
            P.op("dve", lambda e: e.tensor_reduce(out=lam[:, 0:2], in_=tmp[:, :].rearrange("p (a d) -> p a d", a=2),
                                                  axis=AX.X, op=ALU.add), [tmp], [lam])
            P.op("act", lambda e: e.activation(out=lam[:, 2:4], in_=lam[:, 0:2], func=AF.Exp), [lam], [lam])
            P.op("dve", lambda e: e.tensor_tensor(out=lam[:, 5:6], in0=lam[:, 3:4], in1=lam[:, 2:3], op=ALU.subtract),
                 [lam], [lam])
            P.op("dve", lambda e: e.tensor_scalar(out=lam[:, 4:5], in0=lam[:, 5:6], scalar1=-0.2, scalar2=None,
                                                  op0=ALU.add), [lam], [lam])
            dev = P.sb(s, "dev", [128, 192], F32)
            orb = BC_OFF["rel_bias"]
            P.op("dve", lambda e: e.tensor_tensor(
                out=dev[:, :].rearrange("p (b h) -> p b h", h=6), in0=bc[:, orb:orb + 192].rearrange("p (b h) -> p b h", h=6),
                in1=bc[:, orb + 90:orb + 96].unsqueeze(1).to_broadcast([128, 32, 6]), op=ALU.subtract), [bc], [dev])
            t5m = P.sb(s, "t5m", [128, NSLOT * 128], BF16)
            P.dma("sp", t5m[:], self.I("t5m")[:, :], [self.I("t5m")], [t5m])
            mt = P.sb(s, "mt", [128, 256], F32)
            P.dma("sp", mt[:], self.I("mt")[:, :], [self.I("mt")], [mt])
            P.op("pool", lambda e: e.memset(self.maskw[:], NEG), [], [self.maskw])
            for h in range(6):
                P.op("dve", lambda e: e.tensor_scalar(out=self.btab[:, h, :], in0=mt[:], scalar1=bc[:, orb + 90 + h:orb + 91 + h],
                                                      scalar2=None, op0=ALU.add), [mt, bc], [self.btab])
                msl = NSLOT - 1
                P.op("dve", lambda e: e.tensor_scalar(out=self.Dg[:, h, :], in0=t5m[:, msl * 128:(msl + 1) * 128],
                                                      scalar1=NEG, scalar2=None, op0=ALU.mult), [t5m], [self.Dg])
                P.op("dve", lambda e: e.memset(self.Cn[:, h, :], 0.0), [], [self.Cn])
                for si, (kind, b) in enumerate(T5_SLOTS):
                    if kind == "m":
                        continue
                    dst = self.Dg if kind == "d" else self.Cn
                    P.op("dve", lambda e: e.scalar_tensor_tensor(
                        out=dst[:, h, :], in0=t5m[:, si * 128:(si + 1) * 128], scalar=dev[:, b * 6 + h:b * 6 + h + 1],
                        in1=dst[:, h, :], op0=ALU.mult, op1=ALU.add), [t5m, dev, dst], [dst])
            P.barrier()

    def setup_mem(self, st, L, name):
        P = self.P
        mKT = P.sb(st, f"mKT{name}", [128, 2, 256], BF16)
        mV = P.sb(st, f"mV{name}", [128, 2, 4, 72], BF16)
        with contextlib.ExitStack() as s:
            Wkv = P.sb(s, "Wkv", [128, 8, 512], BF16, n=8)
            self.load_w(Wkv, self.I(f"w_kv{L}"), 8, 512)
            xin = P.sb(s, "mxin", [128, D], F32)
            hT = P.sb(s, "mhT", [128, 8, 128], BF16)
            W = dict(junk=P.sb(s, "mjunk", [128, D], F32), ss=P.sb(s, "mss", [128, 1], F32),
                     rt=P.sb(s, "mrt", [128, 2], F32), xn=P.sb(s, "mxn", [128, D], BF16),
                     pt=P.ps(s, "mpt0", [128, 1024], BF16),
                     sq=P.sb(s, "msq", [128, 768], F32), ssg=P.sb(s, "mssg", [128, 96], F32),
                     t1=P.sb(s, "mt1", [128, 768], F32))
            pr = P.sb(s, "mpr", [128, 512], F32)
            kn = P.sb(s, "mkn", [128, 256], BF16)
            pj = P.ps(s, "mpj", [128, 512], F32)
            ptk = P.ps(s, "mptk", [128, 1024], BF16)
            P.op("pool", lambda e: e.memset(mV[:, :, :, 64:72], 1.0), [], [mV])
            for blk in range(2):
                P.dma("sp", xin[:], self.I("mem")[blk * 128:(blk + 1) * 128, :], [self.I("mem")], [xin])
                self.norm_T(xin, self.gcol(2, L), hT[:, :, :], hT.tk, W)
                for kc in range(8):
                    P.op("pe", lambda e: e.matmul(pj[:, :], hT[:, kc, :], Wkv[:, kc, :], start=(kc == 0), stop=(kc == 7)),
                         [hT, Wkv.tks[kc]], [pj])
                P.op("act", lambda e: e.activation(out=pr[:], in_=pj[:, :], func=AF.Copy), [pj], [pr])
                self.group_norm(pr, 0, 4, self.gains[:, 192 + 128 * L:256 + 128 * L],
                                kn[:, :].rearrange("p (g d) -> p g d", d=64), kn.tk, W, [pr])
                P.op("dve", lambda e: e.tensor_copy(out=mV[:, blk, :, 0:64],
                                                    in_=pr[:, 256:512].rearrange("p (h d) -> p h d", h=4)), [pr], [mV])
                for pp in range(2):
                    P.op("pe", lambda e: e.transpose(ptk[:, pp * 128:(pp + 1) * 128], kn[:, pp * 128:(pp + 1) * 128],
                                                     self.idb[:]), [kn, self.idb], [ptk])
                P.op("dve", lambda e: e.tensor_copy(out=mKT[:, :, blk * 128:(blk + 1) * 128],
                                                    in_=ptk[:, 0:256].rearrange("p (a t) -> p a t", a=2)), [ptk], [mKT])
            P.barrier()
        return mKT, mV

    def attn_finalize_diff(self, Ob, i, h, tt, mix, F, k):
        P = self.P
        a0, a1 = 2 * i, 2 * i + 1
        b0, b1 = Ob[a0 // 3], Ob[a1 // 3]
        c0, c1 = (a0 % 3) * 129, (a1 % 3) * 129
        rr, t0, o, junk = F["rr"][k], F["t0"][k], F["o"][k], F["junk"][k]
        P.op("dve", lambda e: e.reciprocal(out=rr[:, 0:1], in_=b0[:, c0 + 128:c0 + 129]), [b0], [rr])
        P.op("dve", lambda e: e.reciprocal(out=rr[:, 1:2], in_=b1[:, c1 + 128:c1 + 129]), [b1], [rr])
        P.op("dve", lambda e: e.tensor_tensor(out=rr[:, 2:3], in0=rr[:, 1:2], in1=self.lam[:, 4:5], op=ALU.mult),
             [rr, self.lam], [rr])
        P.op("dve", lambda e: e.tensor_scalar(out=t0[:], in0=b0[:, c0:c0 + 128], scalar1=rr[:, 0:1], scalar2=None,
                                              op0=ALU.mult), [b0, rr], [t0])
        P.op("dve", lambda e: e.scalar_tensor_tensor(out=o[:], in0=b1[:, c1:c1 + 128], scalar=rr[:, 2:3], in1=t0[:],
                                                     op0=ALU.mult, op1=ALU.add), [b1, rr, t0], [o])
        P.op("act", lambda e: e.activation(out=junk[:], in_=o[:], func=AF.Square, accum_out=rr[:, 3:4]), [o], [junk, rr])
        P.op("act", lambda e: e.activation(out=rr[:, 4:5], in_=rr[:, 3:4], func=AF.Sqrt, scale=1.0 / 128,
                                           bias=self.epsb[:, 0:1]), [rr, self.epsb], [rr])
        P.op("dve", lambda e: e.reciprocal(out=rr[:, 5:6], in_=rr[:, 4:5]), [rr], [rr])
        P.op("dve", lambda e: e.scalar_tensor_tensor(out=mix[:, tt, h * 128:(h + 1) * 128], in0=o[:], scalar=rr[:, 5:6],
                                                     in1=self.gains[:, 384:512], op0=ALU.mult, op1=ALU.mult),
             [o, rr, self.gains], [mix.tks[tt]])

    def cross_attn(self, mKT, mV, mqT, mix, Sb, Ob, PT, F):
        P = self.P
        cnt = 0
        for hd in range(4):
            pp, hp = hd // 2, hd % 2
            r0, r1 = hp * 64, hp * 64 + 64
            for T in range(4):
                for blk in range(2):
                    sb, pt = Sb[cnt % 4], PT[cnt % 4]
                    cnt += 1
                    P.op("pe", lambda e: e.matmul(sb[:, :], mKT[r0:r1, pp, blk * 128:(blk + 1) * 128],
                                                  mqT[r0:r1, pp, T * 512:(T + 1) * 512], start=True, stop=True),
                         [mKT, mqT], [sb])
                    P.op("act", lambda e: e.activation(out=pt[:], in_=sb[:, :], func=AF.Exp), [sb], [pt])
                    for i in range(4):
                        ob = Ob[i // 3]
                        c = (i % 3) * 129
                        P.op("pe", lambda e: e.matmul(ob[:, c:c + 65], pt[:, i * 128:(i + 1) * 128], mV[:, blk, hd, 0:65],
                                                      start=(blk == 0 and i % 3 == 0), stop=(blk == 1),
                                                      skip_group_check=True), [pt, mV], [ob])
                for i in range(4):
                    ob = Ob[i // 3]
                    c = (i % 3) * 129
                    tt = T * 4 + i
                    rr = F["rr"][i % 2]
                    P.op("dve", lambda e: e.reciprocal(out=rr[:, 0:1], in_=ob[:, c + 64:c + 65]), [ob], [rr])
                    P.op("dve", lambda e: e.tensor_scalar(out=mix[:, tt, 768 + hd * 64:832 + hd * 64], in0=ob[:, c:c + 64],
                                                          scalar1=rr[:, 0:1], scalar2=None, op0=ALU.mult),
                         [ob, rr], [mix.tks[tt]])

    def out_proj(self, L, mix, x_src, xa_dst, h2T, Wn, s):
        P = self.P
        Wout = P.sb(s, "Wout", [128, 8, 1024], BF16, n=8)
        self.load_w(Wout, self.I(f"w_out{L}"), 8, 1024)
        mixT = [P.sb(s, f"mixT{i}", [128, 8, 128], BF16) for i in range(2)]
        xin = [P.sb(s, f"oxin{i}", [128, D], F32) for i in range(2)]
        x1 = [P.sb(s, f"ox1{i}", [128, D], F32) for i in range(2)]
        ptm = P.ps(s, "optm", [128, 1024], BF16)
        pj = [P.ps(s, f"opj{i}", [128, 512], F32) for i in range(2)]
        for tt in range(NT):
            mT, xb, xo = mixT[tt % 2], xin[tt % 2], x1[tt % 2]
            P.dma("sp", xb[:], x_src[tt * 128:(tt + 1) * 128, :], [x_src.tks[tt]], [xb])
            for kc in range(8):
                P.op("pe", lambda e: e.transpose(ptm[:, kc * 128:(kc + 1) * 128], mix[:, tt, kc * 128:(kc + 1) * 128],
                                                 self.idb[:]), [mix.tks[tt], self.idb], [ptm])
            P.op("act", lambda e: e.activation(out=mT[:, :, :], in_=ptm[:, :].rearrange("p (k t) -> p k t", k=8),
                                               func=AF.Copy), [ptm], [mT])
            for half in range(2):
                for kc in range(8):
                    P.op("pe", lambda e: e.matmul(pj[half][:, :], mT[:, kc, :], Wout[:, kc, half * 512:(half + 1) * 512],
                                                  start=(kc == 0), stop=(kc == 7)), [mT, Wout.tks[kc]], [pj[half]])
                P.op("dve", lambda e: e.tensor_tensor(out=xo[:, half * 512:(half + 1) * 512], in0=pj[half][:, :],
                                                      in1=xb[:, half * 512:(half + 1) * 512], op=ALU.add),
                     [pj[half], xb], [xo])
            P.dma("sp", xa_dst[tt * 128:(tt + 1) * 128, :], xo[:], [xo], [xa_dst.tks[tt]])
            self.norm_T(xo, self.gcol(1, L), h2T[:, :, 2 + tt * 128:2 + (tt + 1) * 128], h2T.tk, Wn)
        hs = P.sb(s, "hs", [128, 16], BF16)
        hb = P.sb(s, "hb", [128, 4, 16], BF16)
        hacc = P.sb(s, "hacc", [128, 16], F32)
        P.op("dve", lambda e: e.tensor_copy(out=hs[:, :].rearrange("p (k t) -> p k t", k=8), in_=h2T[:, :, 2048:2050]),
             [h2T], [hs])
        P.dma("sp", self.hsrc[L][:, :], hs[:], [hs], [self.hsrc[L]])
        P.allgather(self.hsrc[L], self.hall[L], GROUPS)
        P.dma("sp", hb[:, :, :], self.hall[L].t.ap().rearrange("(r p) c -> p r c", p=128), [self.hall[L]], [hb])
        P.op("dve", lambda e: e.tensor_scalar(out=hacc[:], in0=hb[:, 0, :], scalar1=self.pc[:, 4:5], scalar2=None,
                                              op0=ALU.mult), [hb, self.pc], [hacc])
        for r in range(1, 4):
            P.op("dve", lambda e: e.scalar_tensor_tensor(out=hacc[:], in0=hb[:, r, :], scalar=self.pc[:, 4 + r:5 + r],
                                                         in1=hacc[:], op0=ALU.mult, op1=ALU.add), [hb, self.pc, hacc], [hacc])
        P.op("dve", lambda e: e.tensor_copy(out=h2T[:, :, 0:2], in_=hacc[:, :].rearrange("p (k t) -> p k t", k=8)),
             [hacc], [h2T])

    def phase_b0(self, mix):
        P = self.P
        if True:
            with contextlib.ExitStack() as s2:
                KT = [P.sb(s2, f"KT{i}", [128, S_FULL], BF16) for i in range(2)]
                Vh = [P.sb(s2, f"Vh{i}", [128, 64, 136], BF16) for i in range(2)]
                PT = [P.sb(s2, f"PT{i}", [128, 512], BF16) for i in range(4)]
                Sb = [P.ps(s2, f"Sb{i}", [128, 512], F32) for i in range(4)]
                Ob = [P.ps(s2, f"Ob{i}", [128, 512], F32) for i in range(3)]
                F = dict(rr=[P.sb(s2, f"frr{i}", [128, 8], F32) for i in range(2)],
                         t0=[P.sb(s2, f"ft0{i}", [128, 128], F32) for i in range(2)],
                         o=[P.sb(s2, f"fo{i}", [128, 128], F32) for i in range(2)],
                         junk=[P.sb(s2, f"fj{i}", [128, 128], F32) for i in range(2)])
                for v in Vh:
                    P.op("pool", lambda e: e.memset(v[:, :, 128:136], 1.0), [], [v])
                qT, pc = self.qT, self.pc
                for h in range(6):
                    kt, vh = KT[h % 2], Vh[h % 2]
                    for r in range(4):
                        P.dma("sp", kt[:, r * TOK:(r + 1) * TOK],
                              self.kall[h // 2][r * 256 + (h % 2) * 128:r * 256 + (h % 2 + 1) * 128, :],
                              [self.kall[h // 2]], [kt])
                    for r in range(4):
                        P.dma("sp", vh[:, r * 16:(r + 1) * 16, 0:128],
                              self.vall[h // 2].t.ap()[r * TOK:(r + 1) * TOK, (h % 2) * 128:(h % 2 + 1) * 128].rearrange(
                                  "(b p) c -> p b c", p=128),
                              [self.vall[h // 2]], [vh])
                    for T in range(4):
                        jmax = min(63, 51 + 4 * T)

                        def emit_S(j):
                            for m in range(2):
                                sb, pt = Sb[(2 * j + m) % 4], PT[(2 * j + m) % 4]
                                P.op("pe", lambda e: e.matmul(sb[:, :], kt[m * 64:(m + 1) * 64, j * 128:(j + 1) * 128],
                                                              qT[m * 64:(m + 1) * 64, h, T * 512:(T + 1) * 512],
                                                              start=True, stop=True), [kt, qT], [sb])
                                for qcp in range(4):
                                    off = j - 4 * T - 16 * qcp
                                    if off < -1 or off > 3:
                                        continue
                                    sc = pc[:, qcp:qcp + 1]
                                    if off >= 1:
                                        P.op("dve", lambda e: e.scalar_tensor_tensor(
                                            out=sb[:, 0:off * 128], in0=self.maskw[:, 0:off * 128], scalar=sc,
                                            in1=sb[:, 0:off * 128], op0=ALU.mult, op1=ALU.add),
                                            [sb, self.maskw, pc], [sb])
                                    if 0 <= off <= 3:
                                        P.op("dve", lambda e: e.scalar_tensor_tensor(
                                            out=sb[:, off * 128:(off + 1) * 128], in0=self.Dg[:, h, :], scalar=sc,
                                            in1=sb[:, off * 128:(off + 1) * 128], op0=ALU.mult, op1=ALU.add),
                                            [sb, self.Dg, pc], [sb])
                                    if 0 <= off + 1 <= 3:
                                        i = off + 1
                                        P.op("dve", lambda e: e.scalar_tensor_tensor(
                                            out=sb[:, i * 128:(i + 1) * 128], in0=self.Cn[:, h, :], scalar=sc,
                                            in1=sb[:, i * 128:(i + 1) * 128], op0=ALU.mult, op1=ALU.add),
                                            [sb, self.Cn, pc], [sb])
                                P.op("act", lambda e: e.activation(out=pt[:], in_=sb[:, :], func=AF.Exp,
                                                                   bias=self.btab[:, h, T * 64 + j:T * 64 + j + 1]),
                                     [sb, self.btab], [pt])

                        def emit_PV(j):
                            started = set()
                            for m in range(2):
                                pt = PT[(2 * j + m) % 4]
                                for i in range(4):
                                    a = 2 * i + m
                                    ob = Ob[a // 3]
                                    c = (a % 3) * 129
                                    st_ = (j == 0) and (a // 3 not in started)
                                    started.add(a // 3)
                                    P.op("pe", lambda e: e.matmul(ob[:, c:c + 129], pt[:, i * 128:(i + 1) * 128],
                                                                  vh[:, j, 0:129], start=st_, stop=(j == jmax),
                                                                  skip_group_check=True),
                                         [pt, vh], [ob])

                        emit_S(0)
                        for j in range(jmax + 1):
                            if j + 1 <= jmax:
                                emit_S(j + 1)
                            emit_PV(j)
                        for i in range(4):
                            self.attn_finalize_diff(Ob, i, h, T * 4 + i, mix, F, i % 2)
                    if self.stop == "b0h0":
                        break
                self.cross_attn(self.mKT0, self.mV0, self.mqT, mix, Sb, Ob, PT, F)
                P.barrier()

    def phase_out(self, L, mix, x_src, h2T):
        P = self.P
        if f"mix{L}" in self.dbg:
            for tt in range(NT):
                P.dma("sp", self.dbg[f"mix{L}"][tt * 128:(tt + 1) * 128, :], mix[:, tt, :], [mix.tks[tt]],
                      [self.dbg[f"mix{L}"]])
        with contextlib.ExitStack() as s3:
            Wn = dict(junk=P.sb(s3, "njunk", [128, D], F32), ss=P.sb(s3, "nss", [128, 1], F32),
                      rt=P.sb(s3, "nrt", [128, 2], F32), xn=P.sb(s3, "nxn", [128, D], BF16),
                      pt=P.ps(s3, "npt0", [128, 1024], BF16))
            self.out_proj(L, mix, x_src, self.xa[L], h2T, Wn, s3)
            P.barrier()

    def ffn(self, L, h2T, xa_src, x_dst):
        P = self.P
        with contextlib.ExitStack() as s:
            actT = P.sb(s, "actT", [128, 22, TOK], BF16, n=22)
            cw = P.sb(s, "cw", [128, 44, 3], F32)
            cb = P.sb(s, "cb", [128, 44], F32)
            P.dma("sp", cw[:, :, :], self.I("convw")[:, L * 132:(L + 1) * 132].rearrange("p (c w) -> p c w", w=3),
                  [self.I("convw")], [cw])
            P.dma("sp", cb[:], self.I("convb")[:, L * 44:(L + 1) * 44], [self.I("convb")], [cb])
            with contextlib.ExitStack() as s2:
                wup = [P.sb(s2, f"wup{i}", [128, 8, 256], BF16, n=8) for i in range(2)]
                us = [P.sb(s2, f"us{i}", [128, 2, 1026], F32) for i in range(2)]
                cv = [P.sb(s2, f"cv{i}", [128, 2, 1024], F32) for i in range(2)]
                pu = [P.ps(s2, f"pu{i}", [128, 512], F32) for i in range(4)]
                ph = P.ps(s2, "ph", [128, 512], F32, n=4)
                wsrc = self.I(f"w_up{L}")
                it = 0
                for cc in range(22):
                    wb = wup[cc % 2]
                    P.dma("pool", wb[:, :, 0:128],
                          wsrc.t.ap()[:, cc * 128:(cc + 1) * 128].rearrange("(k p) c -> p k c", p=128), [wsrc], wb.tks)
                    P.dma("pool", wb[:, :, 128:256],
                          wsrc.t.ap()[:, DFF + cc * 128:DFF + (cc + 1) * 128].rearrange("(k p) c -> p k c", p=128),
                          [wsrc], wb.tks)
                    for half in range(2):
                        ub, cvb = us[it % 2], cv[it % 2]
                        t0 = half * 1024
                        for part in range(2):
                            hs = ph.tks[(2 * it + part) % 4]
                            hc = ((2 * it + part) % 4) * 8
                            for kc in range(8):
                                P.op("pe", lambda e: e.matmul(ph[:, hc:hc + 2], wb[:, kc, part * 128:(part + 1) * 128],
                                                              h2T[:, kc, t0:t0 + 2], start=(kc == 0), stop=(kc == 7)),
                                     [wb.tks[kc], h2T], [hs])
                            P.op("act", lambda e: e.activation(out=ub[:, part, 0:2], in_=ph[:, hc:hc + 2], func=AF.Copy),
                                 [hs], [ub])
                            for pc_ in range(2):
                                pb = pu[(4 * it + 2 * part + pc_) % 4]
                                c0 = t0 + 2 + pc_ * 512
                                for kc in range(8):
                                    P.op("pe", lambda e: e.matmul(pb[:, :], wb[:, kc, part * 128:(part + 1) * 128],
                                                                  h2T[:, kc, c0:c0 + 512], start=(kc == 0), stop=(kc == 7)),
                                         [wb.tks[kc], h2T], [pb])
                                P.op("act", lambda e: e.activation(out=ub[:, part, 2 + pc_ * 512:2 + (pc_ + 1) * 512],
                                                                   in_=pb[:, :], func=AF.Copy), [pb], [ub])
                        for part in range(2):
                            ch = part * 22 + cc
                            P.op("act", lambda e: e.activation(out=cvb[:, part, :], in_=ub[:, part, 2:1026], func=AF.Identity,
                                                               scale=cw[:, ch, 2:3], bias=cb[:, ch:ch + 1]),
                                 [ub, cw, cb], [cvb])
                            P.op("dve", lambda e: e.scalar_tensor_tensor(out=cvb[:, part, :], in0=ub[:, part, 1:1025],
                                                                         scalar=cw[:, ch, 1:2], in1=cvb[:, part, :],
                                                                         op0=ALU.mult, op1=ALU.add), [ub, cw, cvb], [cvb])
                            P.op("dve", lambda e: e.scalar_tensor_tensor(out=cvb[:, part, :], in0=ub[:, part, 0:1024],
                                                                         scalar=cw[:, ch, 0:1], in1=cvb[:, part, :],
                                                                         op0=ALU.mult, op1=ALU.add), [ub, cw, cvb], [cvb])
                        P.op("act", lambda e: e.activation(out=cvb[:, 1, :], in_=cvb[:, 1, :], func=AF.Silu), [cvb], [cvb])
                        P.op("dve", lambda e: e.tensor_tensor(out=actT[:, cc, t0:t0 + 1024], in0=cvb[:, 0, :], in1=cvb[:, 1, :],
                                                              op=ALU.mult), [cvb], [actT.tks[cc]])
                        it += 1
                P.barrier()
            with contextlib.ExitStack() as s3:
                Wd = P.sb(s3, "Wd", [128, 22, 1024], BF16, n=22)
                self.load_w(Wd, self.I(f"w_down{L}"), 22, 1024)
                xin = [P.sb(s3, f"dxin{i}", [128, D], F32) for i in range(2)]
                xo = [P.sb(s3, f"dxo{i}", [128, D], F32) for i in range(2)]
                pd = [P.ps(s3, f"pd{i}", [128, 512], F32) for i in range(4)]
                for tt in range(NT):
                    xb, xob = xin[tt % 2], xo[tt % 2]
                    P.dma("sp", xb[:], xa_src[tt * 128:(tt + 1) * 128, :], [xa_src.tks[tt]], [xb])
                    for half in range(2):
                        pb = pd[(2 * tt + half) % 4]
                        for cc in range(22):
                            P.op("pe", lambda e: e.matmul(pb[:, :], actT[:, cc, tt * 128:(tt + 1) * 128],
                                                          Wd[:, cc, half * 512:(half + 1) * 512],
                                                          start=(cc == 0), stop=(cc == 21)), [actT.tks[cc], Wd.tks[cc]], [pb])
                        P.op("dve", lambda e: e.tensor_tensor(out=xob[:, half * 512:(half + 1) * 512], in0=pb[:, :],
                                                              in1=xb[:, half * 512:(half + 1) * 512], op=ALU.add),
                             [pb, xb], [xob])
                    P.dma("sp", x_dst[tt * 128:(tt + 1) * 128, :], xob[:], [xob], [x_dst.tks[tt]])
                P.barrier()

    def phase_c1(self, st, QAT):
        P = self.P
        bc = self.bc
        with contextlib.ExitStack() as s:
            Win = P.sb(s, "Win1", [128, 8, 2576], BF16, n=8)
            self.load_w(Win, self.I("w_in_gla"), 8, 2576)
            gw = P.sb(s, "gw", [16, 384], F32)
            P.dma("sp", gw[:], self.I("gla_gate_w")[:, :], [self.I("gla_gate_w")], [gw])
            glam = P.sb(s, "glam", [128, 256], F32)
            P.dma("sp", glam[:], self.I("glam")[:, :], [self.I("glam")], [glam])
            ones = P.sb(s, "ones", [128, 1], F32)
            P.op("dve", lambda e: e.memset(ones[:], 1.0), [], [ones])
            xin = [P.sb(s, f"cxin{i}", [128, D], F32) for i in range(2)]
            hT = [P.sb(s, f"chT{i}", [128, 8, 128], BF16) for i in range(2)]
            prt = [P.sb(s, f"prt{i}", [128, 2192], F32) for i in range(2)]
            W = dict(junk=P.sb(s, "cjunk", [128, D], F32), ss=P.sb(s, "css", [128, 1], F32),
                     rt=P.sb(s, "crt", [128, 2], F32), xn=P.sb(s, "cxn", [128, D], BF16),
                     pt=P.ps(s, "cpt0", [128, 1024], BF16),
                     sq=P.sb(s, "csq", [128, 768], F32), ssg=P.sb(s, "cssg", [128, 96], F32),
                     t1=P.sb(s, "ct1", [128, 768], F32))
            pt = W["pt"]
            Bk = [P.ps(s, f"cB{i}", [128, 512], F32) for i in range(7)]
            pj = [Bk[0], Bk[1]]
            pq, pk, pz, pgT, pkv = Bk[2], Bk[3], Bk[4], Bk[5], Bk[6]
            mn = P.sb(s, "cmn", [128, 256], BF16)
            vb = P.sb(s, "cvb", [128, 768], BF16)
            glT = P.sb(s, "glT", [16, 128], F32)
            la = P.sb(s, "la", [128, 384], F32)
            Ep = P.sb(s, "Ep", [96, 512], F32)
            En = P.sb(s, "En", [96, 512], F32)
            QpT = P.sb(s, "QpT", [96, 4, 128], BF16)
            QnT = P.sb(s, "QnT", [96, 4, 128], BF16)
            KpT = P.sb(s, "KpT", [96, 4, 128], BF16)
            KnT = P.sb(s, "KnT", [96, 4, 128], BF16)
            Qlo = P.sb(s, "Qlo", [96, 4, 128], BF16)
            Qhi = P.sb(s, "Qhi", [96, 4, 128], BF16)
            Ed = P.sb(s, "Ed", [128, 384], F32)
            Kend = P.sb(s, "Kend", [128, 2, 384], BF16)
            sc1 = P.sb(s, "sc1", [128, 128], F32)
            sc2 = P.sb(s, "sc2", [128, 128], F32)
            scT = P.sb(s, "scT", [128, 4, 128], BF16)
            S = P.sb(s, "Sloc", [96, 768], F32)
            Sa = P.sb(s, "Sa", [96, 768], BF16)
            Sb_ = P.sb(s, "Sbb", [96, 768], BF16)
            A = P.sb(s, "Acum", [96, 4], F32)
            gsum = P.sb(s, "gsum", [96, 772], F32)
            srb = [P.sb(s, f"srb{i}", [128, 768], BF16) for i in range(2)]
            olb = [P.sb(s, f"olb{i}", [128, 768], F32) for i in range(2)]
            P.op("dve", lambda e: e.memset(S[:], 0.0), [], [S])
            P.op("dve", lambda e: e.memset(A[:], 1.0), [], [A])
            P.op("pool", lambda e: e.memset(Qlo[:], 0.0), [], [Qlo])
            P.op("pool", lambda e: e.memset(Qhi[:], 0.0), [], [Qhi])
            mle, mgt = glam[:, 0:128], glam[:, 128:256]
            ogb = BC_OFF["gla_gb"]
            chunks = [(384, 512), (896, 512), (1408, 512), (1920, 512), (2432, 144)]
            import os
            LV = int(os.environ.get("K_C1LVL", "99"))
            for tt in range(int(os.environ.get("K_C1NT", NT))):
                xb, hb, pr = xin[tt % 2], hT[tt % 2], prt[tt % 2]
                P.dma("sp", xb[:], self.xf0[tt * 128:(tt + 1) * 128, :], [self.xf0.tks[tt]], [xb])
                self.norm_T(xb, self.gcol(0, 1), hb[:, :, :], hb.tk, W)
                for n, (c0, w) in enumerate(chunks):
                    pb = pj[n % 2]
                    for kc in range(8):
                        P.op("pe", lambda e: e.matmul(pb[:, 0:w], hb[:, kc, :], Win[:, kc, c0:c0 + w],
                                                      start=(kc == 0), stop=(kc == 7)), [hb, Win.tks[kc]], [pb])
                    if n % 2 == 0:
                        P.op("act", lambda e: e.activation(out=pr[:, c0 - 384:c0 - 384 + w], in_=pb[:, 0:w], func=AF.Copy),
                             [pb], [pr])
                    else:
                        P.op("dve", lambda e: e.tensor_copy(out=pr[:, c0 - 384:c0 - 384 + w], in_=pb[:, 0:w]), [pb], [pr])
                if LV < 1:
                    continue
                for h in range(4):
                    for kc in range(8):
                        P.op("pe", lambda e: e.matmul(pq[0:96, h * 128:(h + 1) * 128], Win[:, kc, h * 96:(h + 1) * 96],
                                                      hb[:, kc, :], start=(kc == 0), stop=(kc == 7)),
                             [hb, Win.tks[kc]], [pq])
                for h in range(4):
                    for kc in range(8):
                        P.op("pe", lambda e: e.matmul(pk[0:96, h * 128:(h + 1) * 128],
                                                      Win[:, kc, 384 + h * 96:384 + (h + 1) * 96],
                                                      hb[:, kc, :], start=(kc == 0), stop=(kc == 7)),
                             [hb, Win.tks[kc]], [pk])
                for kc in range(8):
                    P.op("pe", lambda e: e.matmul(pz[0:16, 384:512], Win[:, kc, 2304:2320], hb[:, kc, :],
                                                  start=(kc == 0), stop=(kc == 7)), [hb, Win.tks[kc]], [pz])
                P.op("act", lambda e: e.activation(out=glT[:], in_=pz[0:16, 384:512], func=AF.Copy), [pz], [glT])
                if LV < 2:
                    continue
                self.group_norm(pr, 1936, 4, self.gains[:, 256:320], mn[:, :].rearrange("p (g d) -> p g d", d=64),
                                mn.tk, W, [pr])
                for h in range(2):
                    P.op("pe", lambda e: e.transpose(pt[:, h * 128:(h + 1) * 128], mn[:, h * 128:(h + 1) * 128],
                                                     self.idb[:]), [mn, self.idb], [pt])
                P.op("act", lambda e: e.activation(out=self.mqT[:, :, tt * 128:(tt + 1) * 128],
                                                   in_=pt[:, 0:256].rearrange("p (h t) -> p h t", h=2), func=AF.Copy),
                     [pt], [self.mqT])
                if LV < 3:
                    continue
                srt = srb[tt % 2]
                P.op("act", lambda e: e.activation(out=srt[:], in_=pr[:, 1152:1920], func=AF.Silu), [pr], [srt])
                P.dma("sp", self.srd[tt * 128:(tt + 1) * 128, :], srt[:], [srt], [self.srd.tks[tt]])
                P.op("pool", lambda e: e.tensor_copy(out=vb[:], in_=pr[:, 384:1152]), [pr], [vb])
                if LV < 4:
                    continue
                P.op("pe", lambda e: e.matmul(pz[:, 0:384], glT[:, :], gw[:, :], start=True, stop=True), [glT, gw], [pz])
                P.op("dve", lambda e: e.tensor_tensor(out=la[:], in0=pz[:, 0:384], in1=bc[:, ogb:ogb + 384], op=ALU.add),
                     [pz, bc], [la])
                P.op("act", lambda e: e.activation(out=la[:], in_=la[:], func=AF.Exp, scale=-1.0), [la], [la])
                P.op("act", lambda e: e.activation(out=la[:], in_=la[:], func=AF.Ln, bias=ones[:, 0:1]), [la, ones], [la])
                P.op("dve", lambda e: e.tensor_scalar(out=la[:], in0=la[:], scalar1=-1.0 / 16, scalar2=None, op0=ALU.mult),
                     [la], [la])
                if LV < 5:
                    continue
                for h in range(4):
                    P.op("pe", lambda e: e.matmul(pgT[0:96, h * 128:(h + 1) * 128], la[:, h * 96:(h + 1) * 96], mle,
                                                  start=True, stop=True), [la, glam], [pgT])
                P.op("pe", lambda e: e.matmul(pz[:, 0:384], mgt, la[:, :], start=True, stop=True), [la, glam], [pz])
                P.op("act", lambda e: e.activation(out=Ep[:], in_=pgT[0:96, :], func=AF.Exp), [pgT], [Ep])
                P.op("act", lambda e: e.activation(out=En[:], in_=pgT[0:96, :], func=AF.Exp, scale=-1.0), [pgT], [En])
                P.op("act", lambda e: e.activation(out=Ed[:], in_=pz[:, 0:384], func=AF.Exp), [pz], [Ed])
                if LV < 6:
                    continue
                qs = 96 ** -0.5
                P.op("dve", lambda e: e.scalar_tensor_tensor(out=QpT[:, :, :].rearrange("p h t -> p (h t)"), in0=pq[0:96, :],
                                                             scalar=qs, in1=Ep[:], op0=ALU.mult, op1=ALU.mult),
                     [pq, Ep], [QpT])
                P.op("dve", lambda e: e.scalar_tensor_tensor(out=QnT[:, :, :].rearrange("p h t -> p (h t)"), in0=pq[0:96, :],
                                                             scalar=qs, in1=En[:], op0=ALU.mult, op1=ALU.mult),
                     [pq, En], [QnT])
                P.op("dve", lambda e: e.tensor_tensor(out=KpT[:, :, :].rearrange("p h t -> p (h t)"), in0=pk[0:96, :],
                                                      in1=Ep[:], op=ALU.mult), [pk, Ep], [KpT])
                P.op("dve", lambda e: e.tensor_tensor(out=KnT[:, :, :].rearrange("p h t -> p (h t)"), in0=pk[0:96, :],
                                                      in1=En[:], op=ALU.mult), [pk, En], [KnT])
                for ck in range(2):
                    msel = glam[:, 63 + 64 * ck:64 + 64 * ck]
                    P.op("dve", lambda e: e.scalar_tensor_tensor(out=Kend[:, ck, :], in0=pr[:, 0:384], scalar=msel, in1=Ed[:],
                                                                 op0=ALU.mult, op1=ALU.mult), [pr, Ed, glam], [Kend])
                if LV < 7:
                    continue
                P.op("pool", lambda e: e.tensor_copy(out=Qlo[:, :, 0:64], in_=QpT[:, :, 0:64]), [QpT], [Qlo])
                P.op("pool", lambda e: e.tensor_copy(out=Qhi[:, :, 64:128], in_=QpT[:, :, 64:128]), [QpT], [Qhi])
                if LV < 8:
                    continue
                for h in range(4):
                    P.op("dve", lambda e: e.tensor_scalar(out=QAT[:, tt, h, 0:64], in0=QpT[:, h, 0:64], scalar1=A[:, h:h + 1],
                                                          scalar2=None, op0=ALU.mult), [QpT, A], [QAT.tks[tt]])
                for h in range(4):
                    P.op("dve", lambda e: e.tensor_tensor(out=A[:, h:h + 1], in0=A[:, h:h + 1], in1=Ep[:, h * 128 + 63:h * 128 + 64],
                                                          op=ALU.mult), [A, Ep], [A])
                for h in range(4):
                    P.op("dve", lambda e: e.tensor_scalar(out=QAT[:, tt, h, 64:128], in0=QpT[:, h, 64:128], scalar1=A[:, h:h + 1],
                                                          scalar2=None, op0=ALU.mult), [QpT, A], [QAT.tks[tt]])
                for h in range(4):
                    P.op("dve", lambda e: e.tensor_tensor(out=A[:, h:h + 1], in0=A[:, h:h + 1], in1=Ep[:, h * 128 + 127:h * 128 + 128],
                                                          op=ALU.mult), [A, Ep], [A])
                if LV < 9:
                    continue
                psc = [pq, pk]
                for h in range(4):
                    pb = psc[h // 2]
                    c = (h % 2) * 256
                    P.op("pe", lambda e: e.matmul(pb[:, c:c + 128], KnT[:, h, :], QpT[:, h, :], start=True, stop=True),
                         [KnT, QpT], [pb])
                    P.op("pe", lambda e: e.matmul(pb[:, c + 128:c + 256], KpT[:, h, :], QnT[:, h, :], start=True, stop=True),
                         [KpT, QnT], [pb])
                    P.op("dve", lambda e: e.tensor_tensor(out=sc1[:], in0=pb[:, c:c + 128], in1=mle, op=ALU.mult),
                         [pb, glam], [sc1])
                    P.op("dve", lambda e: e.tensor_tensor(out=sc2[:], in0=pb[:, c + 128:c + 256], in1=mgt, op=ALU.mult),
                         [pb, glam], [sc2])
                    P.op("pool", lambda e: e.tensor_tensor(out=scT[:, h, :], in0=sc1[:], in1=sc2[:], op=ALU.add),
                         [sc1, sc2], [scT])
                if LV < 10:
                    continue
                P.op("act", lambda e: e.activation(out=Sa[:], in_=S[:], func=AF.Copy), [S], [Sa])
                for h in range(4):
                    hs = slice(h * 192, (h + 1) * 192)
                    for ck in range(int(os.environ.get("K_NCK", 2))):
                        P.op("pe", lambda e: e.matmul(pkv[0:96, ck * 192:(ck + 1) * 192],
                                                      Kend[:, ck, h * 96:(h + 1) * 96],
                                                      vb[:, hs], start=True, stop=True),
                             [Kend, vb], [pkv])
                    if os.environ.get("K_SUB") == "a":
                        continue
                    P.op("dve", lambda e: e.scalar_tensor_tensor(out=S[:, hs], in0=S[:, hs], scalar=Ep[:, h * 128 + 63:h * 128 + 64],
                                                                 in1=pkv[0:96, 0:192], op0=ALU.mult, op1=ALU.add),
                         [S, Ep, pkv], [S])
                    P.op("act", lambda e: e.activation(out=Sb_[:, hs], in_=S[:, hs], func=AF.Copy), [S], [Sb_])
                    P.op("dve", lambda e: e.scalar_tensor_tensor(out=S[:, hs], in0=S[:, hs],
                                                                 scalar=Ep[:, h * 128 + 127:h * 128 + 128],
                                                                 in1=pkv[0:96, 192:384], op0=ALU.mult, op1=ALU.add),
                         [S, Ep, pkv], [S])
                    if os.environ.get("K_SUB") == "b":
                        continue
                    ob = pj[h // 2]
                    oc = (h % 2) * 192
                    P.op("pe", lambda e: e.matmul(ob[:, oc:oc + 192], scT[:, h, :], vb[:, hs], start=True, stop=False),
                         [scT, vb], [ob])
                    P.op("pe", lambda e: e.matmul(ob[:, oc:oc + 192], Qlo[:, h, :], Sa[:, hs], start=False, stop=False),
                         [Qlo, Sa], [ob])
                    P.op("pe", lambda e: e.matmul(ob[:, oc:oc + 192], Qhi[:, h, :], Sb_[:, hs], start=False, stop=True),
                         [Qhi, Sb_], [ob])
                    P.op("dve", lambda e: e.tensor_copy(out=olb[tt % 2][:, hs], in_=ob[:, oc:oc + 192]), [ob], [olb[tt % 2]])
                P.dma("sp", self.olocd[tt * 128:(tt + 1) * 128, :], olb[tt % 2][:], [olb[tt % 2]], [self.olocd.tks[tt]])
            P.op("dve", lambda e: e.tensor_copy(out=gsum[:, 0:4], in_=A[:]), [A], [gsum])
            P.op("dve", lambda e: e.tensor_copy(out=gsum[:, 4:772], in_=S[:]), [S], [gsum])
            P.dma("sp", self.gsrc[:, :], gsum[:], [gsum], [self.gsrc])
            if not os.environ.get("K_NOCC"):
                P.allgather(self.gsrc, self.gall, GROUPS)
            P.barrier()

    def phase_d1(self, mix, QAT, mKT, mV):
        P = self.P
        pc = self.pc
        with contextlib.ExitStack() as s:
            ga = P.sb(s, "ga", [96, 4, 772], F32)
            P.dma("sp", ga[:, :, :], self.gall.t.ap().rearrange("(r p) c -> p r c", p=96), [self.gall], [ga])
            Si = P.sb(s, "Si", [96, 768], F32)
            Sib = P.sb(s, "Sib", [96, 768], BF16)
            coef = P.sb(s, "coef", [96, 4], F32)
            tmpL = P.sb(s, "tmpL", [96, 768], F32)
            P.op("dve", lambda e: e.memset(Si[:], 0.0), [], [Si])
            for r in range(4):
                lt, nlt = pc[0:96, 8 + r:9 + r], pc[0:96, 12 + r:13 + r]
                P.op("dve", lambda e: e.tensor_scalar(out=coef[:], in0=ga[:, r, 0:4], scalar1=lt, scalar2=nlt,
                                                      op0=ALU.mult, op1=ALU.add), [ga, pc], [coef])
                P.op("dve", lambda e: e.tensor_scalar(out=tmpL[:], in0=ga[:, r, 4:772], scalar1=lt, scalar2=None,
                                                      op0=ALU.mult), [ga, pc], [tmpL])
                for h in range(4):
                    hs = slice(h * 192, (h + 1) * 192)
                    P.op("dve", lambda e: e.scalar_tensor_tensor(out=Si[:, hs], in0=Si[:, hs], scalar=coef[:, h:h + 1],
                                                                 in1=tmpL[:, hs], op0=ALU.mult, op1=ALU.add),
                         [Si, coef, tmpL], [Si])
            P.op("act", lambda e: e.activation(out=Sib[:], in_=Si[:], func=AF.Copy), [Si], [Sib])
            pcb = [P.ps(s, f"dB{i}", [128, 512], F32) for i in range(2)]
            o = [P.sb(s, f"do{i}", [128, 768], F32) for i in range(2)]
            sq = P.sb(s, "dsq", [128, 768], F32)
            st4 = P.sb(s, "dst4", [128, 12], F32)
            t1 = P.sb(s, "dt1", [128, 768], F32)
            t2 = P.sb(s, "dt2", [128, 768], F32)
            olb = [P.sb(s, f"dolb{i}", [128, 768], F32) for i in range(2)]
            srb = [P.sb(s, f"dsrb{i}", [128, 768], BF16) for i in range(2)]
            for tt in range(NT):
                ot = o[tt % 2]
                ol, srt = olb[tt % 2], srb[tt % 2]
                P.dma("sp", ol[:], self.olocd[tt * 128:(tt + 1) * 128, :], [self.olocd.tks[tt]], [ol])
                P.dma("sp", srt[:], self.srd[tt * 128:(tt + 1) * 128, :], [self.srd.tks[tt]], [srt])
                for h in range(4):
                    hs = slice(h * 192, (h + 1) * 192)
                    ob = pcb[h // 2]
                    oc = (h % 2) * 192
                    P.op("pe", lambda e: e.matmul(ob[:, oc:oc + 192], QAT[:, tt, h, :], Sib[:, hs], start=True, stop=True),
                         [QAT.tks[tt], Sib], [ob])
                    P.op("dve", lambda e: e.tensor_tensor(out=ot[:, hs], in0=ob[:, oc:oc + 192], in1=ol[:, hs], op=ALU.add),
                         [ob, ol], [ot])
                P.op("pool", lambda e: e.tensor_tensor(out=sq[:], in0=ot[:], in1=ot[:], op=ALU.mult), [ot], [sq])
                P.op("dve", lambda e: e.tensor_reduce(out=st4[:, 0:4], in_=sq[:, :].rearrange("p (g d) -> p g d", d=192),
                                                      axis=AX.X, op=ALU.add), [sq], [st4])
                P.op("act", lambda e: e.activation(out=st4[:, 4:8], in_=st4[:, 0:4], func=AF.Sqrt, scale=1.0 / 192,
                                                   bias=self.epsb[:, 0:1]), [st4, self.epsb], [st4])
                P.op("dve", lambda e: e.reciprocal(out=st4[:, 8:12], in_=st4[:, 4:8]), [st4], [st4])
                P.op("dve", lambda e: e.tensor_tensor(out=t1[:, :].rearrange("p (g d) -> p g d", d=192),
                                                      in0=ot[:, :].rearrange("p (g d) -> p g d", d=192),
                                                      in1=st4[:, 8:12].unsqueeze(2).to_broadcast([128, 4, 192]), op=ALU.mult),
                     [ot, st4], [t1])
                P.op("pool", lambda e: e.tensor_tensor(out=t2[:, :].rearrange("p (g d) -> p g d", d=192),
                                                       in0=t1[:, :].rearrange("p (g d) -> p g d", d=192),
                                                       in1=self.gains[:, 512:704].unsqueeze(1).to_broadcast([128, 4, 192]),
                                                       op=ALU.mult), [t1, self.gains], [t2])
                P.op("dve", lambda e: e.tensor_tensor(out=mix[:, tt, 0:768], in0=t2[:], in1=srt[:], op=ALU.mult),
                     [t2, srt], [mix.tks[tt]])
            with contextlib.ExitStack() as s2:
                PT = [P.sb(s2, f"xPT{i}", [128, 512], BF16) for i in range(4)]
                Sb = [P.ps(s2, f"xSb{i}", [128, 512], F32) for i in range(4)]
                Ob = [P.ps(s2, f"xOb{i}", [128, 512], F32) for i in range(2)]
                F = dict(rr=[P.sb(s2, f"xfrr{i}", [128, 8], F32) for i in range(2)])
                self.cross_attn(mKT, mV, self.mqT, mix, Sb, Ob, PT, F)
            P.barrier()

    def build(self):
        P = self.P
        with contextlib.ExitStack() as st:
            self.setup(st)
            self.setup_gains(st)
            P.barrier()
            with contextlib.ExitStack() as sl0:
                h2T = P.sb(sl0, "h2T", [128, 8, TOK + 2], BF16)
                with contextlib.ExitStack() as satt:
                  if self.stop not in ("l1only", "c1only"):
                      self.qT = P.sb(satt, "qT", [128, 6, TOK], BF16)
                      self.mqT = P.sb(satt, "mqT", [128, 2, TOK], BF16)
                      if self.stop == "s0":
                          P.barrier()
                          return self.nc
                      self.phase_a0(satt)
                      if self.stop == "s3":
                          return self.nc
                      self.setup_attn_consts(satt)
                      self.mKT0, self.mV0 = self.setup_mem(satt, 0, "0")
                      mix = P.sb(satt, "mix", [128, NT, 1024], BF16, n=NT)
                      self.phase_b0(mix)
                      if self.stop == "s4":
                          return self.nc
                      self.phase_out(0, mix, self.I("x"), h2T)
                      if self.stop == "s5":
                          return self.nc
                dst = self.out if self.stop in ("l0", "b0h0") else self.xf0
                if self.stop == "a0":
                    dst = None
                if dst is not None and self.stop not in ("l1only", "c1only"):
                    self.ffn(0, h2T, self.xa[0], dst)
                if self.stop in ("l0", "b0h0", "a0"):
                    P.barrier()
                    return self.nc
                with contextlib.ExitStack() as s1:
                    self.mqT = P.sb(s1, "mqT1", [128, 2, TOK], BF16)
                    self.mKT1, self.mV1 = self.setup_mem(s1, 1, "1")
                    QAT = P.sb(s1, "QAT", [96, NT, 4, 128], BF16, n=NT)
                    self.phase_c1(s1, QAT)
                    if self.stop in ("c1", "c1only"):
                        return self.nc
                    mix = P.sb(s1, "mix1", [128, NT, 1024], BF16, n=NT)
                    self.phase_d1(mix, QAT, self.mKT1, self.mV1)
                    self.phase_out(1, mix, self.xf0, h2T)
                self.ffn(1, h2T, self.xa[1], self.out)
            P.barrier(cc=True)
        return self.nc


def _prep_inputs(inputs):
    f = lambda a: np.ascontiguousarray(np.asarray(a, np.float32))
    x, mem = f(inputs["x"]), f(inputs["mem"])
    bcv = np.concatenate([
        f(inputs["rel_bias"]).reshape(-1), f(inputs["diff_qk_norm"]).reshape(-1), f(inputs["diff_lambda"]).reshape(-1),
        f(inputs["diff_out_norm"]).reshape(-1), f(inputs["gla_gate_b"]).reshape(-1), f(inputs["gla_out_norm"]).reshape(-1),
        f(inputs["mem_qk_norm"]).reshape(-1)])
    assert bcv.shape[0] == BC_N
    bc = np.ascontiguousarray(np.broadcast_to(bcv[None, :], (128, BC_N)))
    colg = np.stack([f(inputs["attn_norm"]), f(inputs["ffn_norm"]), f(inputs["mem_norm"])], 0)
    colg = np.ascontiguousarray(colg.reshape(3, 2, 8, 128).transpose(3, 0, 1, 2).reshape(128, 48))
    convw = np.ascontiguousarray(f(inputs["conv_w"]).reshape(2, 3, 44, 128).transpose(3, 0, 2, 1).reshape(128, 2 * 44 * 3))
    convb = np.ascontiguousarray(f(inputs["conv_b"]).reshape(2, 44, 128).transpose(2, 0, 1).reshape(128, 88))
    mle, mgt = _gla_consts()
    common = dict(
        bc=bc, colg=colg, convw=convw, convb=convb,
        idb=np.eye(128, dtype=np.float32).astype(ml_dtypes.bfloat16),
        t5m=np.ascontiguousarray(T5_MASKS.reshape(128, NSLOT * 128)),
        glam=np.ascontiguousarray(np.concatenate([mle, mgt], 1)),
        w_in_diff=f(inputs["w_in_diff"])[0], w_in_gla=f(inputs["w_in_gla"])[0], gla_gate_w=f(inputs["gla_gate_w"])[0],
    )
    for i in range(2):
        common[f"w_kv{i}"] = f(inputs["w_mem_kv"])[i]
        common[f"w_out{i}"] = f(inputs["w_out"])[i]
        common[f"w_up{i}"] = f(inputs["w_up"])[i]
        common[f"w_down{i}"] = f(inputs["w_down"])[i]
    in_maps = []
    for c in range(8):
        b, qc = c // 4, c % 4
        pc, mt = _core_consts(c)
        m = dict(common)
        m["x"] = np.ascontiguousarray(x[b, qc * TOK:(qc + 1) * TOK])
        m["mem"] = np.ascontiguousarray(mem[b])
        m["pc"] = pc
        m["mt"] = mt
        in_maps.append(m)
    return in_maps


def run(inputs, stop=None, dbg=(), trace=False):
    B = Builder(stop)
    for name, shape, dt in dbg:
        B.dbg_out(name, shape, dt)
    nc = B.build()
    in_maps = _prep_inputs(inputs)
    used = set(B.inputs.keys())
    in_maps = [{k: v for k, v in m.items() if k in used} for m in in_maps]
    res = run_bass_kernel_spmd(nc, in_maps, core_ids=list(range(8)))
    return res, B


def kernel(**inputs):
    res, _ = run(inputs)
    out = np.zeros((2, S_FULL, D), np.float32)
    for c in range(8):
        b, qc = c // 4, c % 4
        out[b, qc * TOK:(qc + 1) * TOK] = res.results[c]["out"]
    return out
```
